# Optimizing a Trainium2 kernel written in Bass

```python
import jax
import jax.numpy as jnp
from jax import lax
import numpy as np

D_MODEL = 2048
BATCH = 4
SEQ = 8192
DEPTH = 2
DEC_BATCH = 8
DEC_SEQ = 64
PAST_LEN = 4096

CHUNK = 64
Q_BLOCK = 128
SB_HEAD_DIM = 128
SB_HEADS = D_MODEL // (2 * SB_HEAD_DIM)
SB_WIDTH = SB_HEADS * SB_HEAD_DIM
CONV_CH = D_MODEL - SB_WIDTH
CONV_WIDTH = 31
D_FF = 256 * ((8 * D_MODEL // 3 + 255) // 256)
IN_COLS = 3 * SB_WIDTH + 2 * CONV_CH
EPS = 1e-6

kernel_name = 'stickbreak_conformer_macaron_stream'


def rms_norm(x, g):
    xf = x.astype(jnp.float32)
    y = xf * lax.rsqrt(jnp.mean(xf * xf, axis=-1, keepdims=True) + EPS)
    return (y * g.astype(jnp.float32)).astype(x.dtype)


def layer_norm(x, g, b):
    xf = x.astype(jnp.float32)
    mu = jnp.mean(xf, axis=-1, keepdims=True)
    xc = xf - mu
    var = jnp.mean(xc * xc, axis=-1, keepdims=True)
    y = xc * lax.rsqrt(var + EPS) * g.astype(jnp.float32) + b.astype(jnp.float32)
    return y.astype(x.dtype)


def swiglu(h, w_gate, w_up, w_down):
    return (jax.nn.silu(h @ w_gate) * (h @ w_up)) @ w_down


def sb_block(q, k, v, q_pos, k_pos):
    z = jnp.einsum('bqhd,bkhd->bhqk', q, k, preferred_element_type=jnp.float32)
    z = z * (SB_HEAD_DIM ** -0.5)
    causal = k_pos[None, :] < q_pos[:, None]
    sp = jnp.where(causal, jax.nn.softplus(z), 0.0)
    between = lax.cumsum(sp, axis=3, reverse=True) - sp
    log_w = jax.nn.log_sigmoid(z) - between
    w = jnp.where(causal, jnp.exp(log_w), 0.0)
    return jnp.einsum('bhqk,bkhd->bqhd', w.astype(v.dtype), v)


def sb_prompt(q, k, v):
    B, S, H, d = q.shape
    nb = S // Q_BLOCK
    qb = q.reshape(B, nb, Q_BLOCK, H, d).transpose(1, 0, 2, 3, 4)
    pos = jnp.arange(S, dtype=jnp.int32)
    qpos = pos.reshape(nb, Q_BLOCK)
    out = lax.map(lambda a: sb_block(a[0], k, v, a[1], pos), (qb, qpos))
    return out.transpose(1, 0, 2, 3, 4).reshape(B, S, H, d)


def conv_module(u_c, hist, dw_w, dw_b, ln_g, ln_b):
    a = u_c[..., :CONV_CH] * jax.nn.sigmoid(u_c[..., CONV_CH:])
    full = jnp.concatenate([hist.astype(a.dtype), a], axis=1)
    y = lax.conv_general_dilated(
        full, dw_w[:, None, :].astype(full.dtype), window_strides=(1,), padding='VALID',
        dimension_numbers=('NWC', 'WIO', 'NWC'), feature_group_count=CONV_CH)
    y = y + dw_b
    y = jax.nn.silu(layer_norm(y, ln_g, ln_b))
    new_hist = full[:, -(CONV_WIDTH - 1):]
    return y, new_hist


def hybrid_layer(x, past_k, past_v, conv_hist,
                 g_ffn1, w1_gate, w1_up, w1_down, g_mix, w_in, dw_w, dw_b, ln_g, ln_b,
                 g_attn_out, g_conv_out, w_out, g_ffn2, w2_gate, w2_up, w2_down):
    B, T, _ = x.shape
    x = x + 0.5 * swiglu(rms_norm(x, g_ffn1), w1_gate, w1_up, w1_down)
    h = rms_norm(x, g_mix)
    u = h @ w_in
    q = u[..., :SB_WIDTH].reshape(B, T, SB_HEADS, SB_HEAD_DIM)
    k = u[..., SB_WIDTH:2 * SB_WIDTH].reshape(B, T, SB_HEADS, SB_HEAD_DIM)
    v = u[..., 2 * SB_WIDTH:3 * SB_WIDTH].reshape(B, T, SB_HEADS, SB_HEAD_DIM)
    u_c = u[..., 3 * SB_WIDTH:]
    if past_k is None:
        attn = sb_prompt(q, k, v)
        conv_hist = jnp.zeros((B, CONV_WIDTH - 1, CONV_CH), x.dtype)
    else:
        P = past_k.shape[1]
        k_all = jnp.concatenate([past_k.astype(k.dtype), k], axis=1)
        v_all = jnp.concatenate([past_v.astype(v.dtype), v], axis=1)
        k_pos = jnp.arange(P + T, dtype=jnp.int32)
        q_pos = P + jnp.arange(T, dtype=jnp.int32)
        attn = sb_block(q, k_all, v_all, q_pos, k_pos)
    conv_out, new_hist = conv_module(u_c, conv_hist, dw_w, dw_b, ln_g, ln_b)
    mixed = jnp.concatenate([rms_norm(attn.reshape(B, T, SB_WIDTH), g_attn_out),
                             rms_norm(conv_out, g_conv_out)], axis=-1)
    x = x + mixed @ w_out
    x = x + 0.5 * swiglu(rms_norm(x, g_ffn2), w2_gate, w2_up, w2_down)
    return x, k, v, new_hist


def setup_inputs(seed: int = 0) -> dict:
    key = jax.random.key(seed)
    ks = jax.random.split(key, 24)
    f32 = jnp.float32

    def nrm(k, shape, scale):
        return jax.random.normal(k, shape, f32) * scale

    def gain(k, shape):
        return 1.0 + 0.02 * jax.random.normal(k, shape, f32)

    kv_shape = (DEPTH, DEC_BATCH, PAST_LEN, SB_HEADS, SB_HEAD_DIM)
    return {
        'x_prompt': nrm(ks[0], (BATCH, SEQ, D_MODEL), 1.0),
        'x_sample': nrm(ks[1], (DEC_BATCH, DEC_SEQ, D_MODEL), 1.0),
        'cache_k': nrm(ks[2], kv_shape, 1.0),
        'cache_v': nrm(ks[3], kv_shape, 1.0),
        'state_conv': nrm(ks[4], (DEPTH, DEC_BATCH, CONV_WIDTH - 1, CONV_CH), 0.5),
        'norm_ffn1': gain(ks[5], (DEPTH, D_MODEL)),
        'ffn1_gate': nrm(ks[6], (DEPTH, D_MODEL, D_FF), D_MODEL ** -0.5),
        'ffn1_up': nrm(ks[7], (DEPTH, D_MODEL, D_FF), D_MODEL ** -0.5),
        'ffn1_down': nrm(ks[8], (DEPTH, D_FF, D_MODEL), D_FF ** -0.5),
        'norm_mix': gain(ks[9], (DEPTH, D_MODEL)),
        'w_in': nrm(ks[10], (DEPTH, D_MODEL, IN_COLS), D_MODEL ** -0.5),
        'dw_weight': nrm(ks[11], (DEPTH, CONV_WIDTH, CONV_CH), CONV_WIDTH ** -0.5),
        'dw_bias': nrm(ks[12], (DEPTH, CONV_CH), 0.02),
        'conv_ln_gain': gain(ks[13], (DEPTH, CONV_CH)),
        'conv_ln_bias': nrm(ks[14], (DEPTH, CONV_CH), 0.02),
        'norm_attn_out': gain(ks[15], (DEPTH, SB_WIDTH)),
        'norm_conv_out': gain(ks[16], (DEPTH, CONV_CH)),
        'w_out': nrm(ks[17], (DEPTH, D_MODEL, D_MODEL), D_MODEL ** -0.5),
        'norm_ffn2': gain(ks[18], (DEPTH, D_MODEL)),
        'ffn2_gate': nrm(ks[19], (DEPTH, D_MODEL, D_FF), D_MODEL ** -0.5),
        'ffn2_up': nrm(ks[20], (DEPTH, D_MODEL, D_FF), D_MODEL ** -0.5),
        'ffn2_down': nrm(ks[21], (DEPTH, D_FF, D_MODEL), D_FF ** -0.5),
        'norm_final': gain(ks[22], (D_MODEL,)),
    }


def reference(x_prompt, x_sample, cache_k, cache_v, state_conv,
              norm_ffn1, ffn1_gate, ffn1_up, ffn1_down, norm_mix, w_in,
              dw_weight, dw_bias, conv_ln_gain, conv_ln_bias,
              norm_attn_out, norm_conv_out, w_out,
              norm_ffn2, ffn2_gate, ffn2_up, ffn2_down, norm_final):
    xp, xs = x_prompt, x_sample
    kp_l, vp_l, cp_l, ks_l, vs_l, cs_l = [], [], [], [], [], []
    for l in range(DEPTH):
        lw = (norm_ffn1[l], ffn1_gate[l], ffn1_up[l], ffn1_down[l], norm_mix[l], w_in[l],
              dw_weight[l], dw_bias[l], conv_ln_gain[l], conv_ln_bias[l],
              norm_attn_out[l], norm_conv_out[l], w_out[l],
              norm_ffn2[l], ffn2_gate[l], ffn2_up[l], ffn2_down[l])
        xp, kp, vp, cp = hybrid_layer(xp, None, None, None, *lw)
        xs, kk, vv, cs = hybrid_layer(xs, cache_k[l], cache_v[l], state_conv[l], *lw)
        kp_l.append(kp)
        vp_l.append(vp)
        cp_l.append(cp)
        ks_l.append(kk)
        vs_l.append(vv)
        cs_l.append(cs)
    y_prompt = rms_norm(xp, norm_final)
    y_sample = rms_norm(xs, norm_final)
    return (y_prompt, y_sample,
            jnp.stack(kp_l), jnp.stack(vp_l), jnp.stack(cp_l),
            jnp.stack(ks_l), jnp.stack(vs_l), jnp.stack(cs_l))
```

```python
import math
from contextlib import ExitStack

import numpy as np
import concourse.bass as bass
import concourse.mybir as mybir
from concourse.bass_utils import run_bass_kernel_spmd

F32 = mybir.dt.float32
BF16 = mybir.dt.bfloat16
AF = mybir.ActivationFunctionType
ALU = mybir.AluOpType

D = 2048
DC = 16
FF = 5632
FC = 44
NH = 8
CC = 8
INC = 5120
EPS = 1e-6
QS = 1.0 / math.sqrt(128.0)
NEG = -30000.0
PL = 336
NPAR = 2 * PL + 16
NWB = 4

SAME_ENGINE_SYNC = True


class Op:
    __slots__ = ("eng", "fn", "deps", "sig", "dma_key", "dma_cnt", "needs_sig")


class Prog:
    ENG = ["pe", "act", "dve", "pool", "sp"]

    def __init__(self):
        self.ops = {e: [] for e in self.ENG}
        self.lw = {}
        self.rd = {}
        self.dma_cnt = {}
        self.dma_last = {}

    def add(self, eng, fn, reads=(), writes=(), dma_key=None, extra_deps=()):
        op = Op()
        op.eng = eng
        op.fn = fn
        op.dma_key = dma_key
        op.needs_sig = False
        op.sig = 0
        op.dma_cnt = 0
        deps = set(extra_deps)
        lw = self.lw
        rd = self.rd
        for r in reads:
            w = lw.get(r)
            if w is not None:
                deps.add(w)
        for w_ in writes:
            w = lw.get(w_)
            if w is not None:
                deps.add(w)
            for x in rd.get(w_, ()):
                deps.add(x)
        if dma_key is not None:
            c = self.dma_cnt.get(dma_key, 0) + 1
            self.dma_cnt[dma_key] = c
            op.dma_cnt = c
            self.dma_last[dma_key] = op
        for r in reads:
            rd.setdefault(r, []).append(op)
        for w_ in writes:
            lw[w_] = op
            rd[w_] = []
        op.deps = deps
        self.ops[eng].append(op)
        return op

    def barrier(self, engines=("pe", "act", "dve", "sp")):
        lasts = []
        for e in engines:
            for op in reversed(self.ops[e]):
                if op.fn is not None and op.dma_key is None:
                    lasts.append(op)
                    break
        deps = set(lasts)
        for k, v in self.dma_last.items():
            if not (isinstance(k, tuple) and k[0] == "wb"):
                deps.add(v)
        for e in engines:
            self.add(e, None, extra_deps=list(deps))

    def emit(self, nc, stack):
        ops = self.ops
        for e in self.ENG:
            for op in ops[e]:
                for d in op.deps:
                    if d.dma_key is not None:
                        continue
                    if d.eng == op.eng and (op.eng == "pe" or not SAME_ENGINE_SYNC):
                        continue
                    d.needs_sig = True
        for e in self.ENG:
            c = 0
            for op in ops[e]:
                if op.dma_key is None and op.needs_sig:
                    c += 1
                    op.sig = c
        sems = {}
        for e in self.ENG:
            sems[("eng", e)] = stack.enter_context(nc.semaphore("s_" + e))
        for i, k in enumerate(self.dma_cnt):
            sems[("dma", k)] = stack.enter_context(nc.semaphore("d%d" % i))
        block = stack.enter_context(nc.Block())

        def run(ename, eng):
            known = {}
            for op in ops[ename]:
                waits = {}
                for d in op.deps:
                    if d.dma_key is not None:
                        key = ("dma", d.dma_key)
                        val = 16 * d.dma_cnt
                    else:
                        if d.eng == ename and (ename == "pe" or not SAME_ENGINE_SYNC):
                            continue
                        key = ("eng", d.eng)
                        val = d.sig
                    if waits.get(key, 0) < val:
                        waits[key] = val
                for key, val in waits.items():
                    if known.get(key, 0) >= val:
                        continue
                    eng.wait_ge(sems[key], val)
                    known[key] = val
                if op.fn is None:
                    continue
                inst = op.fn(eng)
                if op.dma_key is not None:
                    inst.then_inc(sems[("dma", op.dma_key)], 16)
                elif op.needs_sig:
                    inst.then_inc(sems[("eng", ename)], 1)

        @block.tensor
        def _(eng):
            run("pe", eng)

        @block.scalar
        def _(eng):
            run("act", eng)

        @block.vector
        def _(eng):
            run("dve", eng)

        @block.gpsimd
        def _(eng):
            run("pool", eng)

        @block.sync
        def _(eng):
            run("sp", eng)


class DryProg:
    def add(self, *a, **k):
        return None

    def barrier(self, *a, **k):
        return None


def MMS(lst):
    def f(e):
        r = None
        for (o, l, rh, s, t) in lst:
            r = e.matmul(o, lhsT=l, rhs=rh, start=s, stop=t)
        return r
    return f


def TRS(lst):
    def f(e):
        r = None
        for (o, i, idn) in lst:
            r = e.transpose(out=o, in_=i, identity=idn)
        return r
    return f


def ACTV(out, in_, func, **kw):
    return lambda e: e.activation(out=out, in_=in_, func=func, **kw)


def TT(out, in0, in1, op):
    return lambda e: e.tensor_tensor(out=out, in0=in0, in1=in1, op=op)


def TS(out, in0, s1, s2, op0, op1=None):
    if op1 is None:
        return lambda e: e.tensor_scalar(out=out, in0=in0, scalar1=s1, scalar2=None, op0=op0)
    return lambda e: e.tensor_scalar(out=out, in0=in0, scalar1=s1, scalar2=s2, op0=op0, op1=op1)


def STT(out, in0, scalar, in1, op0, op1):
    return lambda e: e.scalar_tensor_tensor(out=out, in0=in0, scalar=scalar, in1=in1, op0=op0, op1=op1)


def CP(out, in_):
    return lambda e: e.tensor_copy(out=out, in_=in_)


def RCP(out, in_):
    return lambda e: e.reciprocal(out=out, in_=in_)


def MSET(ap, v):
    return lambda e: e.memset(ap, v)


def DMA(out, in_):
    return lambda e: e.dma_start(out=out, in_=in_)


def subs(T):
    return [(ts, min(128, T - ts * 128)) for ts in range((T + 127) // 128)]


def build(S, PAST, L):
    NG = S // 512
    NKBS = PAST // 128
    SK = PAST + 64
    nc = bass.Bass("TRN2", target_bir_lowering=False)

    def din(name, shape, dt=F32):
        return nc.dram_tensor(name, list(shape), dt, kind="ExternalInput").ap()

    def dout(name, shape, dt=F32):
        return nc.dram_tensor(name, list(shape), dt, kind="ExternalOutput").ap()

    def dscr(name, shape, dt=F32):
        return nc.dram_tensor(name, list(shape), dt, kind="Internal").ap()

    xp = din("xp", [S, D])
    xs = din("xs", [64, D])
    ck = din("ck", [L, PAST, 1024])
    cv = din("cv", [L, PAST, 1024])
    sc = din("sc", [L, 30, 1024])
    wg = [din("wg1", [L, D, FF]), din("wg2", [L, D, FF])]
    wu = [din("wu1", [L, D, FF]), din("wu2", [L, D, FF])]
    wd = [din("wd1", [L, FF, D]), din("wd2", [L, FF, D])]
    win = din("win", [L, D, INC])
    wout = din("wout", [L, D, D])
    prm_d = din("prm", [128, NPAR])
    gfin_d = din("gfin", [128, D])

    yp = dout("yp", [S, D])
    ys = dout("ys", [64, D])
    nkp = dout("nkp", [L, S, 1024])
    nvp = dout("nvp", [L, S, 1024])
    ncp = dout("ncp", [L, 30, 1024])
    nks = dout("nks", [L, 64, 1024])
    nvs = dout("nvs", [L, 64, 1024])
    ncs = dout("ncs", [L, 30, 1024])

    x1_d = dscr("x1_d", [S + 64, D])
    xc_d = dscr("xc_d", [S + 64, D])
    qT_d = dscr("qT_d", [NH, 128, S + 64], BF16)
    kT_d = dscr("kT_d", [NH, 128, S], BF16)
    kTs_d = dscr("kTs_d", [L, NH, 128, SK], BF16)
    vS_d = dscr("vS_d", [S, 1024], BF16)
    vSs_d = dscr("vSs_d", [L, SK, 1024], BF16)
    aT_d = dscr("aT_d", [CC, 128, 30 + S])
    aTs_d = dscr("aTs_d", [L, CC, 128, 94])
    atT_d = dscr("atT_d", [NH, 128, S + 64])

    groups = [("p", g * 512, 512, g * 512) for g in range(NG)] + [("s", 0, 64, S)]

    with ExitStack() as st:
        def sb(name, shape, dt):
            return st.enter_context(nc.sbuf_tensor("sb_" + name, list(shape), dt))

        x_tm = sb("x_tm", [128, 4, D], F32)
        arena = sb("arena", [128, 22528], BF16)
        xh = sb("xh", [128, 4, D], BF16)
        hT = sb("hT", [128, DC, 512], BF16)
        WB = sb("WB", [128, NWB, 16, 512], BF16)
        tmpf = sb("tmpf", [128, 2, 512], F32)
        stat = sb("stat", [128, 16], F32)
        rep = sb("rep", [128, 6, 512], F32)
        prm = sb("prm", [128, NPAR], F32)
        gfin = sb("gfin", [128, D], F32)
        identb = sb("identb", [128, 128], BF16)
        identf = sb("identf", [128, 128], F32)
        onesf = sb("onesf", [128, 128], F32)
        ntri = sb("ntri", [128, 128], BF16)
        nones = sb("nones", [128, 128], BF16)
        masks = sb("masks", [128, 4, 512], BF16)
        zf = sb("zf", [128, 512], F32)
        ps = [st.enter_context(nc.psum_tensor("ps%d" % i, [128, 512], F32)) for i in range(8)]

        def av(off_bytes, shape, dt):
            n = 1
            for s_ in shape[1:]:
                n *= s_
            nb = n * (4 if dt == F32 else 2)
            v = arena[:, off_bytes // 2:(off_bytes + nb) // 2]
            if dt == F32:
                v = v.bitcast(F32)
            if len(shape) == 3:
                v = v.rearrange("p (a b) -> p a b", a=shape[1])
            return v

        def xv(off_bytes, shape, dt):
            n = 1
            for s_ in shape[1:]:
                n *= s_
            flat = x_tm[:].rearrange("p a b -> p (a b)")
            nb = n * (4 if dt == F32 else 2)
            v = flat[:, off_bytes // 4:(off_bytes + nb) // 4]
            if dt == BF16:
                v = v.bitcast(BF16)
            if len(shape) == 3:
                v = v.rearrange("p (a b) -> p a b", a=shape[1])
            return v

        KB = 1024
        actT = av(0, [128, FC, 512], BF16)
        qTb = av(0, [128, NH, 512], BF16)
        kTb = av(8 * KB, [128, NH, 512], BF16)
        v_bf = av(16 * KB, [128, 4, 1024], BF16)
        aTb = av(24 * KB, [128, CC, 512], F32)
        kv_k = xv(0, [128, 4, 1024], F32)
        kv_v = xv(16 * KB, [128, 4, 1024], F32)
        bufA = av(0, [128, 8, 512], F32)
        abuf = [av(16 * KB, [128, 544], F32), av(16 * KB + 2176, [128, 544], F32)]
        nct = av(40 * KB, [128, 1024], F32)
        sct = av(28 * KB, [128, 1024], F32)
        scT = av(32 * KB, [128, CC, 32], F32)
        KT = [av(0, [128, 8192], BF16), xv(0, [128, 8192], BF16)]
        Vb = [av(16 * KB, [128, 64, 128], BF16), xv(16 * KB, [128, 64, 128], BF16)]
        o2 = 33 * KB
        qt = [av(o2, [128, 512], BF16), av(o2 + 1 * KB, [128, 512], BF16)]
        ebuf = [av(o2 + 2 * KB, [128, 512], F32), av(o2 + 4 * KB, [128, 512], F32)]
        spb = [av(o2 + 6 * KB, [128, 512], BF16), av(o2 + 7 * KB, [128, 512], BF16)]
        pb = [av(o2 + 8 * KB, [128, 512], BF16), av(o2 + 9 * KB, [128, 512], BF16)]
        rbb = [hT[:, 0, :], hT[:, 1, :]]
        Rf = xh[:, 0, 0:1024].bitcast(F32)
        oTs = [xh[:, 1, 0:1024].bitcast(F32), xh[:, 2, 0:1024].bitcast(F32)]
        ckb = [hT[:, 4:6, :].rearrange("p a b -> p (a b)"), hT[:, 6:8, :].rearrange("p a b -> p (a b)")]
        kTc = [hT[:, 8:10, :].rearrange("p a b -> p (a b)").rearrange("p (h s) -> p h s", h=NH),
               hT[:, 10:12, :].rearrange("p a b -> p (a b)").rearrange("p (h s) -> p h s", h=NH)]

        import os
        KSTOP = float(os.environ.get("KSTOP", "99"))

        class _Stop(Exception):
            pass

        def stages(P, W):
            try:
                _stages(P, W)
            except _Stop:
                pass
            P.barrier(engines=("pe", "act", "dve", "pool", "sp"))

        def _stages(P, W):
            P.add("sp", DMA(prm[:], prm_d), writes=["prm"], dma_key="prm")
            P.add("sp", DMA(gfin[:], gfin_d), writes=["gfin"], dma_key="gfin")
            P.add("pool", MSET(zf[:], 0.0), writes=["zf"])
            P.add("pool", MSET(identf[:], 0.0), writes=["identf"])
            P.add("pool", lambda e: e.affine_select(out=identf[:], in_=identf[:], pattern=[[-1, 128]],
                                                   compare_op=ALU.not_equal, fill=1.0, base=0,
                                                   channel_multiplier=1), reads=["identf"], writes=["identf"])
            P.add("dve", CP(identb[:], identf[:]), reads=["identf"], writes=["identb"])
            P.add("pool", MSET(onesf[:], 1.0), writes=["onesf"])
            P.add("pool", MSET(nones[:], -1.0), writes=["nones"])
            P.add("pool", lambda e: e.affine_select(out=ntri[:], in_=nones[:], pattern=[[-1, 128]],
                                                   compare_op=ALU.is_ge, fill=0.0, base=0,
                                                   channel_multiplier=1), reads=["nones"], writes=["ntri"])
            for r in range(4):
                P.add("pool", (lambda r_: lambda e: e.affine_select(
                    out=masks[:, r_, :], in_=zf[:], pattern=[[1, 512]], compare_op=ALU.is_ge, fill=NEG,
                    base=-128 * r_ - 1, channel_multiplier=-1))(r), reads=["zf"], writes=["masks"])
            P.add("sp", DMA(aT_d.rearrange("c p t -> p c t")[:, :, 0:30],
                            zf[:, 0:240].rearrange("p (c t) -> p c t", c=CC)), reads=["zf"], dma_key="zh")
            P.barrier(engines=("pe", "act", "dve", "pool", "sp"))
            if KSTOP <= 1:
                raise _Stop()
            if not W.dry:
                W.precast()

            evac_rr = [0]

            def evac_copy(out, in_, reads, writes, scale=None):
                evac_rr[0] ^= 1
                if evac_rr[0]:
                    if scale is None:
                        P.add("act", ACTV(out, in_, AF.Copy), reads=reads, writes=writes)
                    else:
                        P.add("act", ACTV(out, in_, AF.Copy, scale=scale), reads=reads, writes=writes)
                else:
                    if scale is None:
                        P.add("dve", CP(out, in_), reads=reads, writes=writes)
                    else:
                        P.add("dve", TS(out, in_, scale, None, ALU.mult), reads=reads, writes=writes)

            for l in range(L):
                for blk in range(NKBS):
                    b2 = blk % 2
                    P.add("pool", DMA(ckb[b2], ck[l, blk * 128:(blk + 1) * 128, :]), writes=[("ckb", b2)],
                          dma_key=("ckb", b2))
                    for half in range(2):
                        bank = ps[(2 * blk + half) % 4]
                        pv = bank[:].bitcast(BF16)
                        P.add("pe", TRS([(pv[:, i * 128:(i + 1) * 128],
                                          ckb[b2][:, (half * 4 + i) * 128:(half * 4 + i + 1) * 128], identb[:])
                                         for i in range(4)]),
                              reads=[("ckb", b2), "identb"], writes=[("ps", (2 * blk + half) % 4)])
                        evac_copy(kTc[b2][:, half * 4:half * 4 + 4, :],
                                  pv[:, 0:512].rearrange("p (h s) -> p h s", h=4),
                                  [("ps", (2 * blk + half) % 4)], [("kTc", b2)])
                    P.add("sp", DMA(kTs_d[l].rearrange("h p s -> p h s")[:, :, blk * 128:(blk + 1) * 128], kTc[b2]),
                          reads=[("kTc", b2)], dma_key=("kTc", b2))
                    P.add("pool", DMA(xh[:, b2, 0:1024], cv[l, blk * 128:(blk + 1) * 128, :]), writes=[("cvb", b2)],
                          dma_key=("cvb", b2))
                    P.add("sp", DMA(vSs_d[l, blk * 128:(blk + 1) * 128, :], xh[:, b2, 0:1024]), reads=[("cvb", b2)],
                          dma_key=("cvs", b2))
                P.add("sp", DMA(sct[0:30, :], sc[l]), writes=["sct"], dma_key="sct")
                for c in range(CC):
                    P.add("pe", TRS([(ps[4 + c % 2][:, 0:30], sct[0:30, c * 128:(c + 1) * 128], identf[0:30, 0:30])]),
                          reads=["sct", "identf"], writes=[("ps", 4 + c % 2)])
                    evac_copy(scT[:, c, 0:30], ps[4 + c % 2][:, 0:30], [("ps", 4 + c % 2)], ["scT"])
                P.add("sp", DMA(aTs_d[l].rearrange("c p t -> p c t")[:, :, 0:30], scT[:, :, 0:30]), reads=["scT"],
                      dma_key="scT")
            P.barrier()
            if KSTOP <= 2:
                raise _Stop()

            def norm_hT(T, gcol):
                sl = subs(T)
                P.add("dve", MSET(stat[:, 0:4], 0.0), writes=["stat"])
                for (ts, pr) in sl:
                    P.add("act", ACTV(xh[:pr, ts, :], x_tm[:pr, ts, :], AF.Square, accum_out=stat[:pr, ts:ts + 1]),
                          reads=[("x", ts), "stat"], writes=[("xh", ts), "stat"])
                P.add("dve", TS(stat[:, 4:8], stat[:, 0:4], 1.0 / D, EPS, ALU.mult, ALU.add), reads=["stat"], writes=["stat"])
                P.add("act", ACTV(stat[:, 8:12], stat[:, 4:8], AF.Sqrt), reads=["stat"], writes=["stat"])
                P.add("dve", RCP(stat[:, 12:16], stat[:, 8:12]), reads=["stat"], writes=["stat"])
                for (ts, pr) in sl:
                    P.add("act", ACTV(xh[:pr, ts, :], x_tm[:pr, ts, :], AF.Copy, scale=stat[:pr, 12 + ts:13 + ts]),
                          reads=[("x", ts), "stat"], writes=[("xh", ts)])
                for c in range(DC):
                    bk = 4 + c % 2
                    pv = ps[bk][:].bitcast(BF16)
                    P.add("pe", TRS([(pv[:, ts * 128:ts * 128 + pr], xh[:pr, ts, c * 128:(c + 1) * 128], identb[:pr, :pr])
                                     for (ts, pr) in sl]),
                          reads=[("xh", ts) for (ts, pr) in sl] + ["identb"], writes=[("ps", bk)])
                    evac_copy(hT[:, c, 0:T], pv[:, 0:T], [("ps", bk), "prm"], [("hT", c)],
                              scale=prm[:, gcol + c:gcol + c + 1])

            def ffn(T, l, which):
                sl = subs(T)
                Wg, Wu, Wd = wg[which][l], wu[which][l], wd[which][l]
                hreads = [("hT", c) for c in range(DC)]
                for fg in range(11):
                    bg = W.next(Wg[:, fg * 512:(fg + 1) * 512], 16)
                    bu = W.next(Wu[:, fg * 512:(fg + 1) * 512], 16)
                    for fi in range(4):
                        f = fg * 4 + fi
                        gb = f % 2
                        ub = 2 + f % 2
                        P.add("pe", MMS([(ps[gb][:, 0:T], WB[:, bg, c, fi * 128:(fi + 1) * 128], hT[:, c, 0:T], c == 0, c == DC - 1)
                                         for c in range(DC)]),
                              reads=[("wb", bg)] + hreads, writes=[("ps", gb)])
                        P.add("pe", MMS([(ps[ub][:, 0:T], WB[:, bu, c, fi * 128:(fi + 1) * 128], hT[:, c, 0:T], c == 0, c == DC - 1)
                                         for c in range(DC)]),
                              reads=[("wb", bu)] + hreads, writes=[("ps", ub)])
                        P.add("act", ACTV(tmpf[:, gb, 0:T], ps[gb][:, 0:T], AF.Silu), reads=[("ps", gb)], writes=[("tmpf", gb)])
                        P.add("dve", TT(actT[:, f, 0:T], ps[ub][:, 0:T], tmpf[:, gb, 0:T], ALU.mult),
                              reads=[("ps", ub), ("tmpf", gb)], writes=[("actT", f)])
                for cg in range(4):
                    base = 4 * (cg % 2)
                    for blk in range(3):
                        kc = 16 if blk < 2 else 12
                        bw = W.next(Wd[blk * 2048:blk * 2048 + kc * 128, cg * 512:(cg + 1) * 512], kc)
                        lst = []
                        for fc in range(kc):
                            f = blk * 16 + fc
                            for (ts, pr) in sl:
                                lst.append((ps[base + ts][:pr, :], actT[:, f, ts * 128:ts * 128 + pr], WB[:, bw, fc, :],
                                            f == 0, f == FC - 1))
                        P.add("pe", MMS(lst), reads=[("wb", bw)] + [("actT", blk * 16 + fc) for fc in range(kc)],
                              writes=[("ps", base + ts) for (ts, pr) in sl])
                    for (ts, pr) in sl:
                        xs_ = x_tm[:pr, ts, cg * 512:(cg + 1) * 512]
                        P.add("dve", STT(xs_, ps[base + ts][:pr, :], 0.5, xs_, ALU.mult, ALU.add),
                              reads=[("ps", base + ts), ("x", ts)], writes=[("x", ts)])

            def load_x(src_rows, T):
                for (ts, pr) in subs(T):
                    P.add("sp", DMA(x_tm[:pr, ts, :], src_rows[ts * 128:ts * 128 + pr, :]), writes=[("x", ts)],
                          dma_key=("xl", ts))

            def store_x(dst_rows, T, key):
                for (ts, pr) in subs(T):
                    P.add("sp", DMA(dst_rows[ts * 128:ts * 128 + pr, :], x_tm[:pr, ts, :]), reads=[("x", ts)],
                          dma_key=(key, ts))

            hreads = [("hT", c) for c in range(DC)]

            def win_stage(l, kind, t0, T, r0):
                sl = subs(T)
                Wi = win[l]
                bankc = [0]

                def nb():
                    bankc[0] = (bankc[0] + 1) % 8
                    return bankc[0]

                def fm_block(bw, dst, scale, h0):
                    for hi in range(4):
                        bk = nb()
                        P.add("pe", MMS([(ps[bk][:, 0:T], WB[:, bw, c, hi * 128:(hi + 1) * 128], hT[:, c, 0:T], c == 0, c == DC - 1)
                                         for c in range(DC)]), reads=[("wb", bw)] + hreads, writes=[("ps", bk)])
                        evac_copy(dst[:, h0 + hi, 0:T], ps[bk][:, 0:T], [("ps", bk)], [("stg", id(dst))], scale=scale)

                def tm_block(bw, j, dst32, dstb):
                    for (ts, pr) in sl:
                        bk = nb()
                        P.add("pe", MMS([(ps[bk][:pr, :], hT[:, c, ts * 128:ts * 128 + pr], WB[:, bw, c, :], c == 0, c == DC - 1)
                                         for c in range(DC)]), reads=[("wb", bw)] + hreads, writes=[("ps", bk)])
                        P.add("act", ACTV(dst32[:pr, ts, j * 512:(j + 1) * 512], ps[bk][:pr, :], AF.Copy),
                              reads=[("ps", bk)], writes=[("stg", id(dst32))])
                        if dstb is not None:
                            P.add("dve", CP(dstb[:pr, ts, j * 512:(j + 1) * 512], dst32[:pr, ts, j * 512:(j + 1) * 512]),
                                  reads=[("stg", id(dst32))], writes=[("stg", id(dstb))])

                for j in range(2):
                    bw = W.next(Wi[:, j * 512:(j + 1) * 512], 16)
                    fm_block(bw, qTb, QS, j * 4)
                P.add("sp", DMA(qT_d.rearrange("h p t -> p h t")[:, :, r0:r0 + T], qTb[:, :, 0:T]),
                      reads=[("stg", id(qTb))], dma_key="qTb")
                for j in range(2):
                    bw = W.next(Wi[:, 1024 + j * 512:1024 + (j + 1) * 512], 16)
                    fm_block(bw, kTb, None, j * 4)
                    tm_block(bw, j, kv_k, None)
                if kind == "p":
                    P.add("sp", DMA(kT_d.rearrange("h p t -> p h t")[:, :, t0:t0 + T], kTb[:, :, 0:T]),
                          reads=[("stg", id(kTb))], dma_key="kTb")
                    P.add("sp", DMA(nkp[l, t0:t0 + T, :].rearrange("(a p) n -> p a n", p=128), kv_k[:, :, :]),
                          reads=[("stg", id(kv_k))], dma_key="kvk")
                else:
                    P.add("sp", DMA(kTs_d[l].rearrange("h p t -> p h t")[:, :, PAST:PAST + 64], kTb[:, :, 0:64]),
                          reads=[("stg", id(kTb))], dma_key="kTb")
                    P.add("sp", DMA(nks[l], kv_k[0:64, 0, :]), reads=[("stg", id(kv_k))], dma_key="kvk")
                if KSTOP <= 3.4:
                    raise _Stop()
                for j in range(2):
                    bw = W.next(Wi[:, 2048 + j * 512:2048 + (j + 1) * 512], 16)
                    tm_block(bw, j, kv_v, v_bf)
                if kind == "p":
                    P.add("sp", DMA(nvp[l, t0:t0 + T, :].rearrange("(a p) n -> p a n", p=128), kv_v[:, :, :]),
                          reads=[("stg", id(kv_v))], dma_key="kvv")
                    P.add("sp", DMA(vS_d[t0:t0 + T, :].rearrange("(a p) n -> p a n", p=128), v_bf[:, :, :]),
                          reads=[("stg", id(v_bf))], dma_key="vbf")
                else:
                    P.add("sp", DMA(nvs[l], kv_v[0:64, 0, :]), reads=[("stg", id(kv_v))], dma_key="kvv")
                    P.add("sp", DMA(vSs_d[l, PAST:PAST + 64, :], v_bf[0:64, 0, :]), reads=[("stg", id(v_bf))], dma_key="vbf")
                if KSTOP <= 3.5:
                    raise _Stop()
                for j in range(2):
                    bv = W.next(Wi[:, 3072 + j * 512:3072 + (j + 1) * 512], 16)
                    bg = W.next(Wi[:, 4096 + j * 512:4096 + (j + 1) * 512], 16)
                    for ci in range(4):
                        ch = j * 4 + ci
                        bkv = nb()
                        bkg = nb()
                        P.add("pe", MMS([(ps[bkv][:, 0:T], WB[:, bv, c, ci * 128:(ci + 1) * 128], hT[:, c, 0:T], c == 0, c == DC - 1)
                                         for c in range(DC)]), reads=[("wb", bv)] + hreads, writes=[("ps", bkv)])
                        P.add("pe", MMS([(ps[bkg][:, 0:T], WB[:, bg, c, ci * 128:(ci + 1) * 128], hT[:, c, 0:T], c == 0, c == DC - 1)
                                         for c in range(DC)]), reads=[("wb", bg)] + hreads, writes=[("ps", bkg)])
                        tb = ch % 2
                        P.add("act", ACTV(tmpf[:, tb, 0:T], ps[bkg][:, 0:T], AF.Sigmoid), reads=[("ps", bkg)], writes=[("tmpf", tb)])
                        P.add("dve", TT(aTb[:, ch, 0:T], ps[bkv][:, 0:T], tmpf[:, tb, 0:T], ALU.mult),
                              reads=[("ps", bkv), ("tmpf", tb)], writes=[("stg", id(aTb))])
                if kind == "p":
                    P.add("sp", DMA(aT_d.rearrange("c p t -> p c t")[:, :, 30 + t0:30 + t0 + T], aTb[:, :, 0:T]),
                          reads=[("stg", id(aTb))], dma_key="aTb")
                else:
                    P.add("sp", DMA(aTs_d[l].rearrange("c p t -> p c t")[:, :, 30:94], aTb[:, :, 0:64]),
                          reads=[("stg", id(aTb))], dma_key="aTb")
                if kind == "s" or t0 + T == S:
                    for c in range(CC):
                        bk = nb()
                        P.add("pe", TRS([(ps[bk][0:32, 0:128], aTb[:, c, T - 32:T], identf[:])]),
                              reads=[("stg", id(aTb)), "identf"], writes=[("ps", bk)])
                        evac_copy(nct[0:32, c * 128:(c + 1) * 128], ps[bk][0:32, 0:128], [("ps", bk)], ["nct"])
                    dst = ncs[l] if kind == "s" else ncp[l]
                    P.add("sp", DMA(dst, nct[2:32, :]), reads=["nct"], dma_key="nct")

            def attn_head(KTv, Vv, qsrc_list, blocks_of, Tq_of, out_dst_of, kres, vres):
                for qi, qsrc in enumerate(qsrc_list):
                    Tq = Tq_of(qi)
                    blocks = blocks_of(qi)
                    qb = qi % 2
                    P.add("sp", DMA(qt[qb][:, 0:Tq], qsrc), writes=[("qt", qb)], dma_key=("qt", qb))
                    ob = 6 + qi % 2
                    nblk = len(blocks)
                    need_r_zero = any(b[1] < 128 for b in blocks)
                    if need_r_zero:
                        P.add("dve", MSET(Rf[:, 0:Tq], 0.0), writes=["Rf"])

                    def zlist(k, bank, last):
                        col0, rows, vblk, mr = blocks[k]
                        lst = [(ps[bank][:rows, 0:Tq], KTv[:, col0:col0 + rows], qt[qb][:, 0:Tq], True, last and mr is None)]
                        rd = [kres, ("qt", qb)]
                        if mr is not None:
                            lst.append((ps[bank][:rows, 0:Tq], identb[:rows, :rows], masks[:rows, mr, 0:Tq], False, last))
                            rd += ["identb", "masks"]
                        return lst, rd

                    def zmm(k):
                        lst, rd = zlist(k, k % 2, True)
                        P.add("pe", MMS(lst), reads=rd, writes=[("ps", k % 2)])

                    zmm(0)
                    for k in range(nblk + 1):
                        if k + 1 < nblk:
                            zmm(k + 1)
                        if k < nblk:
                            col0, rows, vblk, mr = blocks[k]
                            zb = k % 2
                            kb2 = k % 2
                            P.add("act", ACTV(ebuf[kb2][:rows, 0:Tq], ps[zb][:rows, 0:Tq], AF.Exp),
                                  reads=[("ps", zb)], writes=[("e", kb2)])
                        if k >= 1:
                            col0p, rowsp, vblkp, mrp = blocks[k - 1]
                            z2p = 2 + (k - 1) % 2
                            kbp = (k - 1) % 2
                            P.add("act", ACTV(pb[kbp][:rowsp, 0:Tq], ps[z2p][:rowsp, 0:Tq], AF.Exp),
                                  reads=[("ps", z2p)], writes=[("pb", kbp)])
                            P.add("pe", MMS([(ps[ob][:, 0:Tq], Vv[:rowsp, vblkp, :], pb[kbp][:rowsp, 0:Tq], k == 1, k == nblk)]),
                                  reads=[("pb", kbp), vres], writes=[("ps", ob)])
                        if k < nblk:
                            z2 = 2 + k % 2
                            P.add("act", ACTV(spb[kb2][:rows, 0:Tq], ebuf[kb2][:rows, 0:Tq], AF.Ln, bias=1.0),
                                  reads=[("e", kb2)], writes=[("sp", kb2)])
                            lst, rd = zlist(k, z2, False)
                            lst.append((ps[z2][:rows, 0:Tq], ntri[:rows, :rows], spb[kb2][:rows, 0:Tq], False, k == 0))
                            rd += [("sp", kb2), "ntri"]
                            if k > 0:
                                prow = blocks[k - 1][1] if k == 1 else 128
                                lst.append((ps[z2][:rows, 0:Tq], nones[:prow, :rows], rbb[kb2][:prow, 0:Tq], False, True))
                                rd += [("rb", kb2), "nones"]
                            P.add("pe", MMS(lst), reads=rd, writes=[("ps", z2)])
                            if k + 1 < nblk:
                                if k == 0 and not need_r_zero:
                                    P.add("dve", CP(Rf[:rows, 0:Tq], spb[kb2][:rows, 0:Tq]), reads=[("sp", kb2)], writes=["Rf"])
                                else:
                                    P.add("dve", TT(Rf[:rows, 0:Tq], Rf[:rows, 0:Tq], spb[kb2][:rows, 0:Tq], ALU.add),
                                          reads=[("sp", kb2), "Rf"], writes=["Rf"])
                                nr = rows if k == 0 else 128
                                P.add("dve", CP(rbb[(k + 1) % 2][:nr, 0:Tq], Rf[:nr, 0:Tq]), reads=["Rf"],
                                      writes=[("rb", (k + 1) % 2)])
                    osb = qi % 2
                    evac_copy(oTs[osb][:, 0:Tq], ps[ob][:, 0:Tq], [("ps", ob)], [("oT", osb)])
                    P.add("sp", DMA(out_dst_of(qi), oTs[osb][:, 0:Tq]), reads=[("oT", osb)], dma_key=("oT", osb))

            def attention(l):
                hc = 0
                for h in range(NH):
                    hb = hc % 2
                    hc += 1
                    P.add("sp", DMA(KT[hb][:, 0:S], kT_d[h]), writes=[("KT", hb)], dma_key=("KT", hb))
                    P.add("sp", DMA(Vb[hb][:, 0:S // 128, :], vS_d[:, h * 128:(h + 1) * 128].rearrange("(b p) d -> p b d", p=128)),
                          writes=[("V", hb)], dma_key=("V", hb))
                    attn_head(KT[hb], Vb[hb],
                              [qT_d[h, :, qi * 512:(qi + 1) * 512] for qi in range(S // 512)],
                              lambda qi: [(kb * 128, 128, kb, (kb - 4 * qi) if kb >= 4 * qi else None)
                                          for kb in range(4 * qi + 3, -1, -1)],
                              lambda qi: 512,
                              (lambda h_: lambda qi: atT_d[h_, :, qi * 512:(qi + 1) * 512])(h),
                              ("KT", hb), ("V", hb))
                for h in range(NH):
                    hb = hc % 2
                    hc += 1
                    P.add("sp", DMA(KT[hb][:, 0:SK], kTs_d[l, h]), writes=[("KT", hb)], dma_key=("KT", hb))
                    P.add("sp", DMA(Vb[hb][:, 0:NKBS, :],
                                    vSs_d[l, 0:PAST, h * 128:(h + 1) * 128].rearrange("(b p) d -> p b d", p=128)),
                          writes=[("V", hb)], dma_key=("V", hb))
                    P.add("sp", DMA(Vb[hb][0:64, NKBS, :], vSs_d[l, PAST:PAST + 64, h * 128:(h + 1) * 128]),
                          writes=[("V", hb)], dma_key=("Vt", hb))
                    attn_head(KT[hb], Vb[hb],
                              [qT_d[h, :, S:S + 64]],
                              lambda qi: [(PAST, 64, NKBS, 0)] + [(kb * 128, 128, kb, None) for kb in range(NKBS - 1, -1, -1)],
                              lambda qi: 64,
                              (lambda h_: lambda qi: atT_d[h_, :, S:S + 64])(h),
                              ("KT", hb), ("V", hb))

            def rep_rstd(bank, T, scale, dst_i, tmp_i):
                P.add("dve", TS(rep[:, tmp_i, 0:T], ps[bank][:, 0:T], scale, EPS, ALU.mult, ALU.add), reads=[("ps", bank)], writes=[("rep", tmp_i)])
                P.add("act", ACTV(rep[:, tmp_i, 0:T], rep[:, tmp_i, 0:T], AF.Sqrt), reads=[("rep", tmp_i)], writes=[("rep", tmp_i)])
                P.add("dve", RCP(rep[:, dst_i, 0:T], rep[:, tmp_i, 0:T]), reads=[("rep", tmp_i)], writes=[("rep", dst_i)])

            def sumsq_ps(bank, src, n, T):
                for i in range(n):
                    tb = i % 2
                    P.add("act", ACTV(tmpf[:, tb, 0:T], src[:, i, 0:T], AF.Square), reads=[("y", i)], writes=[("tmpf", tb)])
                    P.add("pe", MMS([(ps[bank][:, 0:T], onesf[:], tmpf[:, tb, 0:T], i == 0, i == n - 1)]),
                          reads=[("tmpf", tb), "onesf"], writes=[("ps", bank)])

            def mixer_tail(l, kind, t0, T, r0):
                sl = subs(T)
                pc = l * PL
                load_x(x1_d[r0:r0 + T, :], T)
                P.add("sp", DMA(bufA[:, :, 0:T], atT_d.rearrange("h p t -> p h t")[:, :, r0:r0 + T]), writes=[("y", c) for c in range(8)], dma_key="bufA")
                sumsq_ps(0, bufA, NH, T)
                rep_rstd(0, T, 1.0 / 1024, 0, 1)
                for h in range(NH):
                    P.add("dve", STT(hT[:, h, 0:T], bufA[:, h, 0:T], prm[:, pc + 320 + h:pc + 321 + h], rep[:, 0, 0:T], ALU.mult, ALU.mult),
                          reads=[("y", h), ("rep", 0), "prm"], writes=[("hT", h)])
                for c0 in range(0, CC, 2):
                    for c in (c0, c0 + 1):
                        ab = c % 2
                        if kind == "p":
                            src = aT_d[c, :, t0:t0 + 30 + T]
                        else:
                            src = aTs_d[l, c, :, 0:94]
                        P.add("sp", DMA(abuf[ab][:, 0:30 + T], src), writes=[("abuf", ab)], dma_key=("abuf", ab))
                    for w in range(31):
                        for c in (c0, c0 + 1):
                            ab = c % 2
                            yv = bufA[:, c, 0:T]
                            wc = pc + 48 + c * 31
                            if w == 0:
                                P.add("dve", TS(yv, abuf[ab][:, 0:T], prm[:, wc:wc + 1], prm[:, pc + 296 + c:pc + 297 + c], ALU.mult, ALU.add),
                                      reads=[("abuf", ab), "prm"], writes=[("y", c)])
                            else:
                                P.add("dve", STT(yv, abuf[ab][:, w:w + T], prm[:, wc + w:wc + w + 1], yv, ALU.mult, ALU.add),
                                      reads=[("abuf", ab), ("y", c)], writes=[("y", c)])
                for c in range(CC):
                    P.add("pe", MMS([(ps[1][:, 0:T], onesf[:], bufA[:, c, 0:T], c == 0, c == CC - 1)]),
                          reads=[("y", c), "onesf"], writes=[("ps", 1)])
                for c in range(CC):
                    tb = c % 2
                    P.add("act", ACTV(tmpf[:, tb, 0:T], bufA[:, c, 0:T], AF.Square), reads=[("y", c)], writes=[("tmpf", tb)])
                    P.add("pe", MMS([(ps[2][:, 0:T], onesf[:], tmpf[:, tb, 0:T], c == 0, c == CC - 1)]),
                          reads=[("tmpf", tb), "onesf"], writes=[("ps", 2)])
                P.add("dve", TS(rep[:, 2, 0:T], ps[1][:, 0:T], 1.0 / 1024, None, ALU.mult), reads=[("ps", 1)], writes=[("rep", 2)])
                P.add("dve", TT(rep[:, 3, 0:T], rep[:, 2, 0:T], rep[:, 2, 0:T], ALU.mult), reads=[("rep", 2)], writes=[("rep", 3)])
                P.add("dve", STT(rep[:, 3, 0:T], ps[2][:, 0:T], 1.0 / 1024, rep[:, 3, 0:T], ALU.mult, ALU.subtract),
                      reads=[("ps", 2), ("rep", 3)], writes=[("rep", 3)])
                P.add("dve", TS(rep[:, 3, 0:T], rep[:, 3, 0:T], EPS, None, ALU.add), reads=[("rep", 3)], writes=[("rep", 3)])
                P.add("act", ACTV(rep[:, 3, 0:T], rep[:, 3, 0:T], AF.Sqrt), reads=[("rep", 3)], writes=[("rep", 3)])
                P.add("dve", RCP(rep[:, 4, 0:T], rep[:, 3, 0:T]), reads=[("rep", 3)], writes=[("rep", 4)])
                for c in range(CC):
                    yv = bufA[:, c, 0:T]
                    P.add("dve", TT(yv, yv, rep[:, 2, 0:T], ALU.subtract), reads=[("y", c), ("rep", 2)], writes=[("y", c)])
                    P.add("dve", TT(yv, yv, rep[:, 4, 0:T], ALU.mult), reads=[("y", c), ("rep", 4)], writes=[("y", c)])
                    P.add("act", ACTV(yv, yv, AF.Silu, scale=prm[:, pc + 304 + c:pc + 305 + c], bias=prm[:, pc + 312 + c:pc + 313 + c]),
                          reads=[("y", c), "prm"], writes=[("y", c)])
                for c in range(CC):
                    tb = c % 2
                    P.add("act", ACTV(tmpf[:, tb, 0:T], bufA[:, c, 0:T], AF.Square), reads=[("y", c)], writes=[("tmpf", tb)])
                    P.add("pe", MMS([(ps[3][:, 0:T], onesf[:], tmpf[:, tb, 0:T], c == 0, c == CC - 1)]),
                          reads=[("tmpf", tb), "onesf"], writes=[("ps", 3)])
                P.add("dve", TS(rep[:, 5, 0:T], ps[3][:, 0:T], 1.0 / 1024, EPS, ALU.mult, ALU.add), reads=[("ps", 3)], writes=[("rep", 5)])
                P.add("act", ACTV(rep[:, 5, 0:T], rep[:, 5, 0:T], AF.Sqrt), reads=[("rep", 5)], writes=[("rep", 5)])
                P.add("dve", RCP(rep[:, 5, 0:T], rep[:, 5, 0:T]), reads=[("rep", 5)], writes=[("rep", 5)])
                for c in range(CC):
                    P.add("dve", STT(hT[:, 8 + c, 0:T], bufA[:, c, 0:T], prm[:, pc + 328 + c:pc + 329 + c], rep[:, 5, 0:T], ALU.mult, ALU.mult),
                          reads=[("y", c), ("rep", 5), "prm"], writes=[("hT", 8 + c)])
                for cg in range(4):
                    base = 4 * (cg % 2)
                    bw = W.next(wout[l][:, cg * 512:(cg + 1) * 512], 16)
                    for (ts, pr) in sl:
                        P.add("pe", MMS([(ps[base + ts][:pr, :], hT[:, c, ts * 128:ts * 128 + pr], WB[:, bw, c, :], c == 0, c == DC - 1)
                                         for c in range(DC)]), reads=[("wb", bw)] + hreads, writes=[("ps", base + ts)])
                        xs_ = x_tm[:pr, ts, cg * 512:(cg + 1) * 512]
                        P.add("dve", TT(xs_, ps[base + ts][:pr, :], xs_, ALU.add), reads=[("ps", base + ts), ("x", ts)], writes=[("x", ts)])

            def final_norm(kind, t0, T):
                sl = subs(T)
                P.add("dve", MSET(stat[:, 0:4], 0.0), writes=["stat"])
                for (ts, pr) in sl:
                    P.add("act", ACTV(xh[:pr, ts, :], x_tm[:pr, ts, :], AF.Square, accum_out=stat[:pr, ts:ts + 1]),
                          reads=[("x", ts), "stat"], writes=[("xh", ts), "stat"])
                P.add("dve", TS(stat[:, 4:8], stat[:, 0:4], 1.0 / D, EPS, ALU.mult, ALU.add), reads=["stat"], writes=["stat"])
                P.add("act", ACTV(stat[:, 8:12], stat[:, 4:8], AF.Sqrt), reads=["stat"], writes=["stat"])
                P.add("dve", RCP(stat[:, 12:16], stat[:, 8:12]), reads=["stat"], writes=["stat"])
                for (ts, pr) in sl:
                    P.add("dve", STT(x_tm[:pr, ts, :], x_tm[:pr, ts, :], stat[:pr, 12 + ts:13 + ts], gfin[:pr, :], ALU.mult, ALU.mult),
                          reads=[("x", ts), "stat", "gfin"], writes=[("x", ts)])
                dst = yp[t0:t0 + T, :] if kind == "p" else ys
                store_x(dst, T, "xst")

            for l in range(L):
                pc = l * PL
                for (kind, t0, T, r0) in groups:
                    if l == 0:
                        load_x(xp[t0:t0 + T, :] if kind == "p" else xs, T)
                    else:
                        load_x(xc_d[r0:r0 + T, :], T)
                    norm_hT(T, pc + 0)
                    P.barrier()
                    ffn(T, l, 0)
                    store_x(x1_d[r0:r0 + T, :], T, "xst")
                    if KSTOP <= 3:
                        raise _Stop()
                    norm_hT(T, pc + 16)
                    P.barrier()
                    if KSTOP <= 3.2:
                        raise _Stop()
                    win_stage(l, kind, t0, T, r0)
                    P.barrier()
                    if KSTOP <= 3.6 or (KSTOP <= 3.8 and kind == "p" and t0 + T == S):
                        raise _Stop()
                if KSTOP <= 4:
                    raise _Stop()
                attention(l)
                P.barrier()
                if KSTOP <= 5:
                    raise _Stop()
                for (kind, t0, T, r0) in groups:
                    mixer_tail(l, kind, t0, T, r0)
                    P.barrier()
                    norm_hT(T, pc + 32)
                    P.barrier()
                    ffn(T, l, 1)
                    if l < L - 1:
                        store_x(xc_d[r0:r0 + T, :], T, "xst")
                    else:
                        final_norm(kind, t0, T)
                    P.barrier()

        class WStream:
            def __init__(self, P, seq, uniq=None, wsc=None):
                self.P = P
                self.dry = seq is None
                self.seq = [] if seq is None else seq
                self.uniq = uniq
                self.wsc = wsc
                self.n = 0
                self.issued = 0

            def next(self, blk, kc):
                if self.dry:
                    self.seq.append((blk, kc))
                    return 0
                n = self.n
                self.n += 1
                lim = min(len(self.seq), n + NWB - 1)
                while self.issued < lim:
                    m = self.issued
                    b_, kc_ = self.seq[m]
                    bi = m % NWB
                    idx = self.uniq[(b_.tensor.name, b_.offset)]
                    self.P.add("pool", DMA(WB[:, bi, 0:kc_, :], self.wsc[idx][:, 0:kc_, :]),
                               reads=[("wsc", idx)], writes=[("wb", bi)], dma_key=("wb", bi))
                    self.issued += 1
                return n % NWB

            def precast(self):
                done = set()
                i = 0
                for (b_, kc_) in self.seq:
                    key = (b_.tensor.name, b_.offset)
                    idx = self.uniq[key]
                    if idx in done:
                        continue
                    done.add(idx)
                    bi = i % NWB
                    i += 1
                    self.P.add("pool", DMA(WB[:, bi, 0:kc_, :], b_.rearrange("(k p) n -> p k n", p=128)),
                               writes=[("wb", bi)], dma_key=("wb", bi))
                    self.P.add("sp", DMA(self.wsc[idx][:, 0:kc_, :], WB[:, bi, 0:kc_, :]),
                               reads=[("wb", bi)], writes=[("wsc", idx)], dma_key=("wsst", bi))

        dry = WStream(DryProg(), None)
        stages(dry.P, dry)
        uniq = {}
        for (b_, kc_) in dry.seq:
            uniq.setdefault((b_.tensor.name, b_.offset), len(uniq))
        nu = max(len(uniq), 1)
        wsc_parts = [dscr("wsc_d%d" % i, [min(96, nu - i * 96), 128, 16, 512], BF16) for i in range((nu + 95) // 96)]

        class _Wsc:
            def __getitem__(self, idx):
                return wsc_parts[idx // 96][idx % 96]
        wsc_d = _Wsc()
        P = Prog()
        W = WStream(P, dry.seq, uniq, wsc_d)
        stages(P, W)
        P.emit(nc, st)
    return nc


def _pack_params(inp, L):
    prm = np.zeros((128, NPAR), np.float32)

    def fm(v, nchunk):
        return np.ascontiguousarray(v.reshape(nchunk, 128).T)

    for l in range(L):
        pc = l * PL
        prm[:, pc + 0:pc + 16] = fm(inp["norm_ffn1"][l], 16)
        prm[:, pc + 16:pc + 32] = fm(inp["norm_mix"][l], 16)
        prm[:, pc + 32:pc + 48] = fm(inp["norm_ffn2"][l], 16)
        dw = inp["dw_weight"][l]
        prm[:, pc + 48:pc + 296] = dw.reshape(31, 8, 128).transpose(2, 1, 0).reshape(128, 248)
        prm[:, pc + 296:pc + 304] = fm(inp["dw_bias"][l], 8)
        prm[:, pc + 304:pc + 312] = fm(inp["conv_ln_gain"][l], 8)
        prm[:, pc + 312:pc + 320] = fm(inp["conv_ln_bias"][l], 8)
        prm[:, pc + 320:pc + 328] = fm(inp["norm_attn_out"][l], 8)
        prm[:, pc + 328:pc + 336] = fm(inp["norm_conv_out"][l], 8)
    return prm


_NC_CACHE = {}


def kernel(**inp):
    inp = {k: np.asarray(v) for k, v in inp.items()}
    B, S, _ = inp["x_prompt"].shape
    BS = inp["x_sample"].shape[0]
    L = inp["w_in"].shape[0]
    PAST = inp["cache_k"].shape[2]
    import os
    ncores = int(os.environ.get('KCORES', '8'))
    key = (S, PAST, L)
    if key not in _NC_CACHE:
        _NC_CACHE[key] = build(S, PAST, L)
    nc = _NC_CACHE[key]
    prm = _pack_params(inp, L)
    gfin = np.ascontiguousarray(np.broadcast_to(inp["norm_final"][None, :], (128, D))).astype(np.float32)
    in_maps = []
    for c in range(ncores):
        pb = c % B
        sbi = c % BS
        in_maps.append({
            "xp": np.ascontiguousarray(inp["x_prompt"][pb]),
            "xs": np.ascontiguousarray(inp["x_sample"][sbi]),
            "ck": np.ascontiguousarray(inp["cache_k"][:, sbi].reshape(L, PAST, 1024)),
            "cv": np.ascontiguousarray(inp["cache_v"][:, sbi].reshape(L, PAST, 1024)),
            "sc": np.ascontiguousarray(inp["state_conv"][:, sbi]),
            "wg1": inp["ffn1_gate"], "wu1": inp["ffn1_up"], "wd1": inp["ffn1_down"],
            "wg2": inp["ffn2_gate"], "wu2": inp["ffn2_up"], "wd2": inp["ffn2_down"],
            "win": inp["w_in"], "wout": inp["w_out"],
            "prm": prm, "gfin": gfin,
        })
    res = run_bass_kernel_spmd(nc, in_maps, core_ids=list(range(ncores)))
    R = list(res.results)
    while len(R) < 8:
        R.append(R[0])
    y_p = np.stack([R[b]["yp"] for b in range(B)])
    y_s = np.stack([R[b]["ys"] for b in range(BS)])
    nkp = np.stack([R[b]["nkp"] for b in range(B)], axis=1).reshape(L, B, S, NH, 128)
    nvp = np.stack([R[b]["nvp"] for b in range(B)], axis=1).reshape(L, B, S, NH, 128)
    ncp = np.stack([R[b]["ncp"] for b in range(B)], axis=1)
    nks = np.stack([R[b]["nks"] for b in range(BS)], axis=1).reshape(L, BS, 64, NH, 128)
    nvs = np.stack([R[b]["nvs"] for b in range(BS)], axis=1).reshape(L, BS, 64, NH, 128)
    ncs = np.stack([R[b]["ncs"] for b in range(BS)], axis=1)
    return (y_p, y_s, nkp, nvp, ncp, nks, nvs, ncs)
```

```python
import math
from contextlib import ExitStack

import numpy as np
import concourse.bass as bass
import concourse.mybir as mybir
from concourse.bass_utils import run_bass_kernel_spmd

F32 = mybir.dt.float32
BF16 = mybir.dt.bfloat16
AF = mybir.ActivationFunctionType
ALU = mybir.AluOpType

D = 2048
DC = 16
FF = 5632
FC = 44
NH = 8
CC = 8
INC = 5120
EPS = 1e-6
QS = 1.0 / math.sqrt(128.0)
NEG = -30000.0
PL = 336
NPAR = 2 * PL + 16
NWB = 4

SAME_ENGINE_SYNC = True


class Op:
    __slots__ = ("eng", "fn", "deps", "sig", "dma_key", "dma_cnt", "needs_sig")


class Prog:
    ENG = ["pe", "act", "dve", "pool", "sp"]

    def __init__(self):
        self.ops = {e: [] for e in self.ENG}
        self.lw = {}
        self.rd = {}
        self.dma_cnt = {}
        self.dma_last = {}

    def add(self, eng, fn, reads=(), writes=(), dma_key=None, extra_deps=()):
        op = Op()
        op.eng = eng
        op.fn = fn
        op.dma_key = dma_key
        op.needs_sig = False
        op.sig = 0
        op.dma_cnt = 0
        deps = set(extra_deps)
        lw = self.lw
        rd = self.rd
        for r in reads:
            w = lw.get(r)
            if w is not None:
                deps.add(w)
        for w_ in writes:
            w = lw.get(w_)
            if w is not None:
                deps.add(w)
            for x in rd.get(w_, ()):
                deps.add(x)
        if dma_key is not None:
            c = self.dma_cnt.get(dma_key, 0) + 1
            self.dma_cnt[dma_key] = c
            op.dma_cnt = c
            self.dma_last[dma_key] = op
        for r in reads:
            rd.setdefault(r, []).append(op)
        for w_ in writes:
            lw[w_] = op
            rd[w_] = []
        op.deps = deps
        self.ops[eng].append(op)
        return op

    def barrier(self, engines=("pe", "act", "dve", "sp")):
        lasts = []
        for e in engines:
            for op in reversed(self.ops[e]):
                if op.fn is not None and op.dma_key is None:
                    lasts.append(op)
                    break
        deps = set(lasts)
        for k, v in self.dma_last.items():
            if not (isinstance(k, tuple) and k[0] == "wb"):
                deps.add(v)
        for e in engines:
            self.add(e, None, extra_deps=list(deps))

    def emit(self, nc, stack):
        ops = self.ops
        for e in self.ENG:
            for op in ops[e]:
                for d in op.deps:
                    if d.dma_key is not None:
                        continue
                    if d.eng == op.eng and (op.eng == "pe" or (not SAME_ENGINE_SYNC and op.eng != "pool")):
                        continue
                    d.needs_sig = True
        for e in self.ENG:
            c = 0
            for op in ops[e]:
                if op.dma_key is None and op.needs_sig:
                    c += 1
                    op.sig = c
        sems = {}
        for e in self.ENG:
            sems[("eng", e)] = stack.enter_context(nc.semaphore("s_" + e))
        for i, k in enumerate(self.dma_cnt):
            sems[("dma", k)] = stack.enter_context(nc.semaphore("d%d" % i))
        block = stack.enter_context(nc.Block())

        def run(ename, eng):
            known = {}
            for op in ops[ename]:
                waits = {}
                for d in op.deps:
                    if d.dma_key is not None:
                        key = ("dma", d.dma_key)
                        val = 16 * d.dma_cnt
                    else:
                        if d.eng == ename and (ename == "pe" or (not SAME_ENGINE_SYNC and ename != "pool")):
                            continue
                        key = ("eng", d.eng)
                        val = d.sig
                    if waits.get(key, 0) < val:
                        waits[key] = val
                for key, val in waits.items():
                    if known.get(key, 0) >= val:
                        continue
                    eng.wait_ge(sems[key], val)
                    known[key] = val
                if op.fn is None:
                    continue
                inst = op.fn(eng)
                if op.dma_key is not None:
                    inst.then_inc(sems[("dma", op.dma_key)], 16)
                elif op.needs_sig:
                    inst.then_inc(sems[("eng", ename)], 1)

        @block.tensor
        def _(eng):
            run("pe", eng)

        @block.scalar
        def _(eng):
            run("act", eng)

        @block.vector
        def _(eng):
            run("dve", eng)

        @block.gpsimd
        def _(eng):
            run("pool", eng)

        @block.sync
        def _(eng):
            run("sp", eng)


class DryProg:
    def add(self, *a, **k):
        return None

    def barrier(self, *a, **k):
        return None


def MMS(lst):
    def f(e):
        r = None
        for (o, l, rh, s, t) in lst:
            r = e.matmul(o, lhsT=l, rhs=rh, start=s, stop=t)
        return r
    return f


def TRS(lst):
    def f(e):
        r = None
        for (o, i, idn) in lst:
            r = e.transpose(out=o, in_=i, identity=idn)
        return r
    return f


def ACTV(out, in_, func, **kw):
    return lambda e: e.activation(out=out, in_=in_, func=func, **kw)


def TT(out, in0, in1, op):
    return lambda e: e.tensor_tensor(out=out, in0=in0, in1=in1, op=op)


def TS(out, in0, s1, s2, op0, op1=None):
    if op1 is None:
        return lambda e: e.tensor_scalar(out=out, in0=in0, scalar1=s1, scalar2=None, op0=op0)
    return lambda e: e.tensor_scalar(out=out, in0=in0, scalar1=s1, scalar2=s2, op0=op0, op1=op1)


def STT(out, in0, scalar, in1, op0, op1):
    return lambda e: e.scalar_tensor_tensor(out=out, in0=in0, scalar=scalar, in1=in1, op0=op0, op1=op1)


def CP(out, in_):
    return lambda e: e.tensor_copy(out=out, in_=in_)


def RCP(out, in_):
    return lambda e: e.reciprocal(out=out, in_=in_)


def MSET(ap, v):
    return lambda e: e.memset(ap, v)


def DMA(out, in_):
    return lambda e: e.dma_start(out=out, in_=in_)


def subs(T):
    return [(ts, min(128, T - ts * 128)) for ts in range((T + 127) // 128)]


def build(S, PAST, L):
    NG = S // 512
    NKBS = PAST // 128
    SK = PAST + 64
    nc = bass.Bass("TRN2", target_bir_lowering=False)

    def din(name, shape, dt=F32):
        return nc.dram_tensor(name, list(shape), dt, kind="ExternalInput").ap()

    def dout(name, shape, dt=F32):
        return nc.dram_tensor(name, list(shape), dt, kind="ExternalOutput").ap()

    def dscr(name, shape, dt=F32):
        return nc.dram_tensor(name, list(shape), dt, kind="Internal").ap()

    xp = din("xp", [S, D])
    xs = din("xs", [64, D])
    ck = din("ck", [L, PAST, 1024])
    cv = din("cv", [L, PAST, 1024])
    sc = din("sc", [L, 30, 1024])
    wg = [din("wg1", [L, D, FF]), din("wg2", [L, D, FF])]
    wu = [din("wu1", [L, D, FF]), din("wu2", [L, D, FF])]
    wd = [din("wd1", [L, FF, D]), din("wd2", [L, FF, D])]
    win = din("win", [L, D, INC])
    wout = din("wout", [L, D, D])
    prm_d = din("prm", [128, NPAR])
    gfin_d = din("gfin", [128, D])

    yp = dout("yp", [S, D])
    ys = dout("ys", [64, D])
    nkp = dout("nkp", [L, S, 1024])
    nvp = dout("nvp", [L, S, 1024])
    ncp = dout("ncp", [L, 30, 1024])
    nks = dout("nks", [L, 64, 1024])
    nvs = dout("nvs", [L, 64, 1024])
    ncs = dout("ncs", [L, 30, 1024])

    x1_d = dscr("x1_d", [S + 64, D])
    xc_d = dscr("xc_d", [S + 64, D])
    qT_d = dscr("qT_d", [NH, 128, S + 64], BF16)
    kT_d = dscr("kT_d", [NH, 128, S], BF16)
    kTs_d = dscr("kTs_d", [L, NH, 128, SK], BF16)
    vS_d = dscr("vS_d", [S, 1024], BF16)
    vSs_d = dscr("vSs_d", [L, SK, 1024], BF16)
    aT_d = dscr("aT_d", [CC, 128, 30 + S])
    aTs_d = dscr("aTs_d", [L, CC, 128, 94])
    atT_d = dscr("atT_d", [NH, 128, S + 64])

    groups = [("p", g * 512, 512, g * 512) for g in range(NG)] + [("s", 0, 64, S)]

    with ExitStack() as st:
        def sb(name, shape, dt):
            return st.enter_context(nc.sbuf_tensor("sb_" + name, list(shape), dt))

        x_tm = sb("x_tm", [128, 4, D], F32)
        arena = sb("arena", [128, 22528], BF16)
        xh = sb("xh", [128, 4, D], BF16)
        hT = sb("hT", [128, DC, 512], BF16)
        WB = sb("WB", [128, NWB, 16, 512], BF16)
        tmpf = sb("tmpf", [128, 2, 512], F32)
        stat = sb("stat", [128, 16], F32)
        rep = sb("rep", [128, 6, 512], F32)
        prm = sb("prm", [128, NPAR], F32)
        gfin = sb("gfin", [128, D], F32)
        identb = sb("identb", [128, 128], BF16)
        identf = sb("identf", [128, 128], F32)
        onesf = sb("onesf", [128, 128], F32)
        ntri = sb("ntri", [128, 128], BF16)
        nones = sb("nones", [128, 128], BF16)
        masks = sb("masks", [128, 4, 512], BF16)
        zf = sb("zf", [128, 512], F32)
        ps = [st.enter_context(nc.psum_tensor("ps%d" % i, [128, 512], F32)) for i in range(8)]

        def av(off_bytes, shape, dt):
            n = 1
            for s_ in shape[1:]:
                n *= s_
            nb = n * (4 if dt == F32 else 2)
            v = arena[:, off_bytes // 2:(off_bytes + nb) // 2]
            if dt == F32:
                v = v.bitcast(F32)
            if len(shape) == 3:
                v = v.rearrange("p (a b) -> p a b", a=shape[1])
            return v

        def xv(off_bytes, shape, dt):
            n = 1
            for s_ in shape[1:]:
                n *= s_
            flat = x_tm[:].rearrange("p a b -> p (a b)")
            nb = n * (4 if dt == F32 else 2)
            v = flat[:, off_bytes // 4:(off_bytes + nb) // 4]
            if dt == BF16:
                v = v.bitcast(BF16)
            if len(shape) == 3:
                v = v.rearrange("p (a b) -> p a b", a=shape[1])
            return v

        KB = 1024
        actT = av(0, [128, FC, 512], BF16)
        qTb = av(0, [128, NH, 512], BF16)
        kTb = av(8 * KB, [128, NH, 512], BF16)
        v_bf = av(16 * KB, [128, 4, 1024], BF16)
        aTb = av(24 * KB, [128, CC, 512], F32)
        kv_k = xv(0, [128, 4, 1024], F32)
        kv_v = xv(16 * KB, [128, 4, 1024], F32)
        bufA = av(0, [128, 8, 512], F32)
        abuf = [av(16 * KB, [128, 544], F32), av(16 * KB + 2176, [128, 544], F32)]
        nct = av(40 * KB, [128, 1024], F32)
        sct = av(28 * KB, [128, 1024], F32)
        scT = av(32 * KB, [128, CC, 32], F32)
        KT = [av(0, [128, 8192], BF16), xv(0, [128, 8192], BF16)]
        Vb = [av(16 * KB, [128, 64, 128], BF16), xv(16 * KB, [128, 64, 128], BF16)]
        o2 = 33 * KB
        qt = [av(o2, [128, 512], BF16), av(o2 + 1 * KB, [128, 512], BF16)]
        ebuf = [av(o2 + 2 * KB, [128, 512], F32), av(o2 + 4 * KB, [128, 512], F32)]
        spb = [av(o2 + 6 * KB, [128, 512], BF16), av(o2 + 7 * KB, [128, 512], BF16)]
        pb = [av(o2 + 8 * KB, [128, 512], BF16), av(o2 + 9 * KB, [128, 512], BF16)]
        rbb = [hT[:, 0, :], hT[:, 1, :]]
        Rf = xh[:, 0, 0:1024].bitcast(F32)
        oTs = [xh[:, 1, 0:1024].bitcast(F32), xh[:, 2, 0:1024].bitcast(F32)]
        ckb = [hT[:, 4:6, :].rearrange("p a b -> p (a b)"), hT[:, 6:8, :].rearrange("p a b -> p (a b)")]
        kTc = [hT[:, 8:10, :].rearrange("p a b -> p (a b)").rearrange("p (h s) -> p h s", h=NH),
               hT[:, 10:12, :].rearrange("p a b -> p (a b)").rearrange("p (h s) -> p h s", h=NH)]

        import os
        KSTOP = float(os.environ.get("KSTOP", "99"))

        class _Stop(Exception):
            pass

        def stages(P, W):
            try:
                _stages(P, W)
            except _Stop:
                pass
            P.barrier(engines=("pe", "act", "dve", "pool", "sp"))

        def _stages(P, W):
            P.add("sp", DMA(prm[:], prm_d), writes=["prm"], dma_key="prm")
            P.add("sp", DMA(gfin[:], gfin_d), writes=["gfin"], dma_key="gfin")
            P.add("pool", MSET(zf[:], 0.0), writes=["zf"])
            P.add("pool", MSET(identf[:], 0.0), writes=["identf"])
            P.add("pool", lambda e: e.affine_select(out=identf[:], in_=identf[:], pattern=[[-1, 128]],
                                                   compare_op=ALU.not_equal, fill=1.0, base=0,
                                                   channel_multiplier=1), reads=["identf"], writes=["identf"])
            P.add("dve", CP(identb[:], identf[:]), reads=["identf"], writes=["identb"])
            P.add("pool", MSET(onesf[:], 1.0), writes=["onesf"])
            P.add("pool", MSET(nones[:], -1.0), writes=["nones"])
            P.add("pool", lambda e: e.affine_select(out=ntri[:], in_=nones[:], pattern=[[-1, 128]],
                                                   compare_op=ALU.is_ge, fill=0.0, base=0,
                                                   channel_multiplier=1), reads=["nones"], writes=["ntri"])
            for r in range(4):
                P.add("pool", (lambda r_: lambda e: e.affine_select(
                    out=masks[:, r_, :], in_=zf[:], pattern=[[1, 512]], compare_op=ALU.is_ge, fill=NEG,
                    base=-128 * r_ - 1, channel_multiplier=-1))(r), reads=["zf"], writes=["masks"])
            P.add("sp", DMA(aT_d.rearrange("c p t -> p c t")[:, :, 0:30],
                            zf[:, 0:240].rearrange("p (c t) -> p c t", c=CC)), reads=["zf"], dma_key="zh")
            P.barrier(engines=("pe", "act", "dve", "pool", "sp"))
            if KSTOP <= 1:
                raise _Stop()

            evac_rr = [0]

            def evac_copy(out, in_, reads, writes, scale=None):
                evac_rr[0] ^= 1
                if evac_rr[0]:
                    if scale is None:
                        P.add("act", ACTV(out, in_, AF.Copy), reads=reads, writes=writes)
                    else:
                        P.add("act", ACTV(out, in_, AF.Copy, scale=scale), reads=reads, writes=writes)
                else:
                    if scale is None:
                        P.add("dve", CP(out, in_), reads=reads, writes=writes)
                    else:
                        P.add("dve", TS(out, in_, scale, None, ALU.mult), reads=reads, writes=writes)

            for l in range(L):
                for blk in range(NKBS):
                    b2 = blk % 2
                    P.add("pool", DMA(ckb[b2], ck[l, blk * 128:(blk + 1) * 128, :]), writes=[("ckb", b2)],
                          dma_key=("ckb", b2))
                    for half in range(2):
                        bank = ps[(2 * blk + half) % 4]
                        pv = bank[:].bitcast(BF16)
                        P.add("pe", TRS([(pv[:, i * 128:(i + 1) * 128],
                                          ckb[b2][:, (half * 4 + i) * 128:(half * 4 + i + 1) * 128], identb[:])
                                         for i in range(4)]),
                              reads=[("ckb", b2), "identb"], writes=[("ps", (2 * blk + half) % 4)])
                        evac_copy(kTc[b2][:, half * 4:half * 4 + 4, :],
                                  pv[:, 0:512].rearrange("p (h s) -> p h s", h=4),
                                  [("ps", (2 * blk + half) % 4)], [("kTc", b2)])
                    P.add("sp", DMA(kTs_d[l].rearrange("h p s -> p h s")[:, :, blk * 128:(blk + 1) * 128], kTc[b2]),
                          reads=[("kTc", b2)], dma_key=("kTc", b2))
                    P.add("pool", DMA(xh[:, b2, 0:1024], cv[l, blk * 128:(blk + 1) * 128, :]), writes=[("cvb", b2)],
                          dma_key=("cvb", b2))
                    P.add("sp", DMA(vSs_d[l, blk * 128:(blk + 1) * 128, :], xh[:, b2, 0:1024]), reads=[("cvb", b2)],
                          dma_key=("cvs", b2))
                P.add("sp", DMA(sct[0:30, :], sc[l]), writes=["sct"], dma_key="sct")
                for c in range(CC):
                    P.add("pe", TRS([(ps[4 + c % 2][:, 0:30], sct[0:30, c * 128:(c + 1) * 128], identf[0:30, 0:30])]),
                          reads=["sct", "identf"], writes=[("ps", 4 + c % 2)])
                    evac_copy(scT[:, c, 0:30], ps[4 + c % 2][:, 0:30], [("ps", 4 + c % 2)], ["scT"])
                P.add("sp", DMA(aTs_d[l].rearrange("c p t -> p c t")[:, :, 0:30], scT[:, :, 0:30]), reads=["scT"],
                      dma_key="scT")
            P.barrier()
            if KSTOP <= 2:
                raise _Stop()

            def norm_hT(T, gcol):
                sl = subs(T)
                P.add("dve", MSET(stat[:, 0:4], 0.0), writes=["stat"])
                for (ts, pr) in sl:
                    P.add("act", ACTV(xh[:pr, ts, :], x_tm[:pr, ts, :], AF.Square, accum_out=stat[:pr, ts:ts + 1]),
                          reads=[("x", ts), "stat"], writes=[("xh", ts), "stat"])
                P.add("dve", TS(stat[:, 4:8], stat[:, 0:4], 1.0 / D, EPS, ALU.mult, ALU.add), reads=["stat"], writes=["stat"])
                P.add("act", ACTV(stat[:, 8:12], stat[:, 4:8], AF.Sqrt), reads=["stat"], writes=["stat"])
                P.add("dve", RCP(stat[:, 12:16], stat[:, 8:12]), reads=["stat"], writes=["stat"])
                for (ts, pr) in sl:
                    P.add("act", ACTV(xh[:pr, ts, :], x_tm[:pr, ts, :], AF.Copy, scale=stat[:pr, 12 + ts:13 + ts]),
                          reads=[("x", ts), "stat"], writes=[("xh", ts)])
                for c in range(DC):
                    bk = 4 + c % 2
                    pv = ps[bk][:].bitcast(BF16)
                    P.add("pe", TRS([(pv[:, ts * 128:ts * 128 + pr], xh[:pr, ts, c * 128:(c + 1) * 128], identb[:pr, :pr])
                                     for (ts, pr) in sl]),
                          reads=[("xh", ts) for (ts, pr) in sl] + ["identb"], writes=[("ps", bk)])
                    evac_copy(hT[:, c, 0:T], pv[:, 0:T], [("ps", bk), "prm"], [("hT", c)],
                              scale=prm[:, gcol + c:gcol + c + 1])

            def ffn(T, l, which):
                sl = subs(T)
                Wg, Wu, Wd = wg[which][l], wu[which][l], wd[which][l]
                hreads = [("hT", c) for c in range(DC)]
                for fg in range(11):
                    bg = W.next(Wg[:, fg * 512:(fg + 1) * 512], 16)
                    bu = W.next(Wu[:, fg * 512:(fg + 1) * 512], 16)
                    for fi in range(4):
                        f = fg * 4 + fi
                        gb = f % 2
                        ub = 2 + f % 2
                        P.add("pe", MMS([(ps[gb][:, 0:T], WB[:, bg, c, fi * 128:(fi + 1) * 128], hT[:, c, 0:T], c == 0, c == DC - 1)
                                         for c in range(DC)]),
                              reads=[("wb", bg)] + hreads, writes=[("ps", gb)])
                        P.add("pe", MMS([(ps[ub][:, 0:T], WB[:, bu, c, fi * 128:(fi + 1) * 128], hT[:, c, 0:T], c == 0, c == DC - 1)
                                         for c in range(DC)]),
                              reads=[("wb", bu)] + hreads, writes=[("ps", ub)])
                        P.add("act", ACTV(tmpf[:, gb, 0:T], ps[gb][:, 0:T], AF.Silu), reads=[("ps", gb)], writes=[("tmpf", gb)])
                        P.add("dve", TT(actT[:, f, 0:T], ps[ub][:, 0:T], tmpf[:, gb, 0:T], ALU.mult),
                              reads=[("ps", ub), ("tmpf", gb)], writes=[("actT", f)])
                for cg in range(4):
                    base = 4 * (cg % 2)
                    for blk in range(3):
                        kc = 16 if blk < 2 else 12
                        bw = W.next(Wd[blk * 2048:blk * 2048 + kc * 128, cg * 512:(cg + 1) * 512], kc)
                        lst = []
                        for fc in range(kc):
                            f = blk * 16 + fc
                            for (ts, pr) in sl:
                                lst.append((ps[base + ts][:pr, :], actT[:, f, ts * 128:ts * 128 + pr], WB[:, bw, fc, :],
                                            f == 0, f == FC - 1))
                        P.add("pe", MMS(lst), reads=[("wb", bw)] + [("actT", blk * 16 + fc) for fc in range(kc)],
                              writes=[("ps", base + ts) for (ts, pr) in sl])
                    for (ts, pr) in sl:
                        xs_ = x_tm[:pr, ts, cg * 512:(cg + 1) * 512]
                        P.add("dve", STT(xs_, ps[base + ts][:pr, :], 0.5, xs_, ALU.mult, ALU.add),
                              reads=[("ps", base + ts), ("x", ts)], writes=[("x", ts)])

            def load_x(src_rows, T):
                for (ts, pr) in subs(T):
                    P.add("sp", DMA(x_tm[:pr, ts, :], src_rows[ts * 128:ts * 128 + pr, :]), writes=[("x", ts)],
                          dma_key=("xl", ts))

            def store_x(dst_rows, T, key):
                for (ts, pr) in subs(T):
                    P.add("sp", DMA(dst_rows[ts * 128:ts * 128 + pr, :], x_tm[:pr, ts, :]), reads=[("x", ts)],
                          dma_key=(key, ts))

            hreads = [("hT", c) for c in range(DC)]

            def win_stage(l, kind, t0, T, r0):
                sl = subs(T)
                Wi = win[l]
                bankc = [0]

                def nb():
                    bankc[0] = (bankc[0] + 1) % 8
                    return bankc[0]

                def fm_block(bw, dst, scale, h0):
                    for hi in range(4):
                        bk = nb()
                        P.add("pe", MMS([(ps[bk][:, 0:T], WB[:, bw, c, hi * 128:(hi + 1) * 128], hT[:, c, 0:T], c == 0, c == DC - 1)
                                         for c in range(DC)]), reads=[("wb", bw)] + hreads, writes=[("ps", bk)])
                        evac_copy(dst[:, h0 + hi, 0:T], ps[bk][:, 0:T], [("ps", bk)], [("stg", id(dst))], scale=scale)

                def tm_block(bw, j, dst32, dstb):
                    for (ts, pr) in sl:
                        bk = nb()
                        P.add("pe", MMS([(ps[bk][:pr, :], hT[:, c, ts * 128:ts * 128 + pr], WB[:, bw, c, :], c == 0, c == DC - 1)
                                         for c in range(DC)]), reads=[("wb", bw)] + hreads, writes=[("ps", bk)])
                        P.add("act", ACTV(dst32[:pr, ts, j * 512:(j + 1) * 512], ps[bk][:pr, :], AF.Copy),
                              reads=[("ps", bk)], writes=[("stg", id(dst32))])
                        if dstb is not None:
                            P.add("dve", CP(dstb[:pr, ts, j * 512:(j + 1) * 512], dst32[:pr, ts, j * 512:(j + 1) * 512]),
                                  reads=[("stg", id(dst32))], writes=[("stg", id(dstb))])

                for j in range(2):
                    bw = W.next(Wi[:, j * 512:(j + 1) * 512], 16)
                    fm_block(bw, qTb, QS, j * 4)
                P.add("sp", DMA(qT_d.rearrange("h p t -> p h t")[:, :, r0:r0 + T], qTb[:, :, 0:T]),
                      reads=[("stg", id(qTb))], dma_key="qTb")
                for j in range(2):
                    bw = W.next(Wi[:, 1024 + j * 512:1024 + (j + 1) * 512], 16)
                    fm_block(bw, kTb, None, j * 4)
                    tm_block(bw, j, kv_k, None)
                if kind == "p":
                    P.add("sp", DMA(kT_d.rearrange("h p t -> p h t")[:, :, t0:t0 + T], kTb[:, :, 0:T]),
                          reads=[("stg", id(kTb))], dma_key="kTb")
                    P.add("sp", DMA(nkp[l, t0:t0 + T, :].rearrange("(a p) n -> p a n", p=128), kv_k[:, :, :]),
                          reads=[("stg", id(kv_k))], dma_key="kvk")
                else:
                    P.add("sp", DMA(kTs_d[l].rearrange("h p t -> p h t")[:, :, PAST:PAST + 64], kTb[:, :, 0:64]),
                          reads=[("stg", id(kTb))], dma_key="kTb")
                    P.add("sp", DMA(nks[l], kv_k[0:64, 0, :]), reads=[("stg", id(kv_k))], dma_key="kvk")
                if KSTOP <= 3.4:
                    raise _Stop()
                for j in range(2):
                    bw = W.next(Wi[:, 2048 + j * 512:2048 + (j + 1) * 512], 16)
                    tm_block(bw, j, kv_v, v_bf)
                if kind == "p":
                    P.add("sp", DMA(nvp[l, t0:t0 + T, :].rearrange("(a p) n -> p a n", p=128), kv_v[:, :, :]),
                          reads=[("stg", id(kv_v))], dma_key="kvv")
                    P.add("sp", DMA(vS_d[t0:t0 + T, :].rearrange("(a p) n -> p a n", p=128), v_bf[:, :, :]),
                          reads=[("stg", id(v_bf))], dma_key="vbf")
                else:
                    P.add("sp", DMA(nvs[l], kv_v[0:64, 0, :]), reads=[("stg", id(kv_v))], dma_key="kvv")
                    P.add("sp", DMA(vSs_d[l, PAST:PAST + 64, :], v_bf[0:64, 0, :]), reads=[("stg", id(v_bf))], dma_key="vbf")
                if KSTOP <= 3.5:
                    raise _Stop()
                for j in range(2):
                    bv = W.next(Wi[:, 3072 + j * 512:3072 + (j + 1) * 512], 16)
                    bg = W.next(Wi[:, 4096 + j * 512:4096 + (j + 1) * 512], 16)
                    for ci in range(4):
                        ch = j * 4 + ci
                        bkv = nb()
                        bkg = nb()
                        P.add("pe", MMS([(ps[bkv][:, 0:T], WB[:, bv, c, ci * 128:(ci + 1) * 128], hT[:, c, 0:T], c == 0, c == DC - 1)
                                         for c in range(DC)]), reads=[("wb", bv)] + hreads, writes=[("ps", bkv)])
                        P.add("pe", MMS([(ps[bkg][:, 0:T], WB[:, bg, c, ci * 128:(ci + 1) * 128], hT[:, c, 0:T], c == 0, c == DC - 1)
                                         for c in range(DC)]), reads=[("wb", bg)] + hreads, writes=[("ps", bkg)])
                        tb = ch % 2
                        P.add("act", ACTV(tmpf[:, tb, 0:T], ps[bkg][:, 0:T], AF.Sigmoid), reads=[("ps", bkg)], writes=[("tmpf", tb)])
                        P.add("dve", TT(aTb[:, ch, 0:T], ps[bkv][:, 0:T], tmpf[:, tb, 0:T], ALU.mult),
                              reads=[("ps", bkv), ("tmpf", tb)], writes=[("stg", id(aTb))])
                if kind == "p":
                    P.add("sp", DMA(aT_d.rearrange("c p t -> p c t")[:, :, 30 + t0:30 + t0 + T], aTb[:, :, 0:T]),
                          reads=[("stg", id(aTb))], dma_key="aTb")
                else:
                    P.add("sp", DMA(aTs_d[l].rearrange("c p t -> p c t")[:, :, 30:94], aTb[:, :, 0:64]),
                          reads=[("stg", id(aTb))], dma_key="aTb")
                if kind == "s" or t0 + T == S:
                    for c in range(CC):
                        bk = nb()
                        P.add("pe", TRS([(ps[bk][0:32, 0:128], aTb[:, c, T - 32:T], identf[:])]),
                              reads=[("stg", id(aTb)), "identf"], writes=[("ps", bk)])
                        evac_copy(nct[0:32, c * 128:(c + 1) * 128], ps[bk][0:32, 0:128], [("ps", bk)], ["nct"])
                    dst = ncs[l] if kind == "s" else ncp[l]
                    P.add("sp", DMA(dst, nct[2:32, :]), reads=["nct"], dma_key="nct")

            def attn_head(KTv, Vv, qsrc_list, blocks_of, Tq_of, out_dst_of, kres, vres):
                for qi, qsrc in enumerate(qsrc_list):
                    Tq = Tq_of(qi)
                    blocks = blocks_of(qi)
                    qb = qi % 2
                    P.add("sp", DMA(qt[qb][:, 0:Tq], qsrc), writes=[("qt", qb)], dma_key=("qt", qb))
                    ob = 6 + qi % 2
                    nblk = len(blocks)
                    need_r_zero = any(b[1] < 128 for b in blocks)
                    if need_r_zero:
                        P.add("dve", MSET(Rf[:, 0:Tq], 0.0), writes=["Rf"])

                    def zlist(k, bank, last):
                        col0, rows, vblk, mr = blocks[k]
                        lst = [(ps[bank][:rows, 0:Tq], KTv[:, col0:col0 + rows], qt[qb][:, 0:Tq], True, last and mr is None)]
                        rd = [kres, ("qt", qb)]
                        if mr is not None:
                            lst.append((ps[bank][:rows, 0:Tq], identb[:rows, :rows], masks[:rows, mr, 0:Tq], False, last))
                            rd += ["identb", "masks"]
                        return lst, rd

                    def zmm(k):
                        lst, rd = zlist(k, k % 2, True)
                        P.add("pe", MMS(lst), reads=rd, writes=[("ps", k % 2)])

                    def emm(k):
                        rows_ = blocks[k][1]
                        P.add("act", ACTV(ebuf[k % 2][:rows_, 0:Tq], ps[k % 2][:rows_, 0:Tq], AF.Exp),
                              reads=[("ps", k % 2)], writes=[("e", k % 2)])

                    zmm(0)
                    if nblk > 1:
                        zmm(1)
                    emm(0)
                    for k in range(nblk + 1):
                        if k + 1 < nblk:
                            emm(k + 1)
                        if k + 2 < nblk:
                            zmm(k + 2)
                        if k < nblk:
                            col0, rows, vblk, mr = blocks[k]
                            z2 = 2 + k % 2
                            kb2 = k % 2
                            P.add("act", ACTV(spb[kb2][:rows, 0:Tq], ebuf[kb2][:rows, 0:Tq], AF.Ln, bias=1.0),
                                  reads=[("e", kb2)], writes=[("sp", kb2)])
                            lst, rd = zlist(k, z2, False)
                            lst.append((ps[z2][:rows, 0:Tq], ntri[:rows, :rows], spb[kb2][:rows, 0:Tq], False, k == 0))
                            rd += [("sp", kb2), "ntri"]
                            if k > 0:
                                prow = blocks[k - 1][1] if k == 1 else 128
                                lst.append((ps[z2][:rows, 0:Tq], nones[:prow, :rows], rbb[kb2][:prow, 0:Tq], False, True))
                                rd += [("rb", kb2), "nones"]
                            P.add("pe", MMS(lst), reads=rd, writes=[("ps", z2)])
                            if k + 1 < nblk:
                                if k == 0 and not need_r_zero:
                                    P.add("dve", CP(Rf[:rows, 0:Tq], spb[kb2][:rows, 0:Tq]), reads=[("sp", kb2)], writes=["Rf"])
                                else:
                                    P.add("dve", TT(Rf[:rows, 0:Tq], Rf[:rows, 0:Tq], spb[kb2][:rows, 0:Tq], ALU.add),
                                          reads=[("sp", kb2), "Rf"], writes=["Rf"])
                                nr = rows if k == 0 else 128
                                P.add("dve", CP(rbb[(k + 1) % 2][:nr, 0:Tq], Rf[:nr, 0:Tq]), reads=["Rf"],
                                      writes=[("rb", (k + 1) % 2)])
                        if k >= 1:
                            col0, rows, vblk, mr = blocks[k - 1]
                            z2 = 2 + (k - 1) % 2
                            kb2 = (k - 1) % 2
                            P.add("act", ACTV(pb[kb2][:rows, 0:Tq], ps[z2][:rows, 0:Tq], AF.Exp),
                                  reads=[("ps", z2)], writes=[("pb", kb2)])
                            P.add("pe", MMS([(ps[ob][:, 0:Tq], Vv[:rows, vblk, :], pb[kb2][:rows, 0:Tq], k == 1, k == nblk)]),
                                  reads=[("pb", kb2), vres], writes=[("ps", ob)])
                    osb = qi % 2
                    evac_copy(oTs[osb][:, 0:Tq], ps[ob][:, 0:Tq], [("ps", ob)], [("oT", osb)])
                    P.add("sp", DMA(out_dst_of(qi), oTs[osb][:, 0:Tq]), reads=[("oT", osb)], dma_key=("oT", osb))

            def attention(l):
                hc = 0
                for h in range(NH):
                    hb = hc % 2
                    hc += 1
                    P.add("sp", DMA(KT[hb][:, 0:S], kT_d[h]), writes=[("KT", hb)], dma_key=("KT", hb))
                    P.add("sp", DMA(Vb[hb][:, 0:S // 128, :], vS_d[:, h * 128:(h + 1) * 128].rearrange("(b p) d -> p b d", p=128)),
                          writes=[("V", hb)], dma_key=("V", hb))
                    attn_head(KT[hb], Vb[hb],
                              [qT_d[h, :, qi * 512:(qi + 1) * 512] for qi in range(S // 512)],
                              lambda qi: [(kb * 128, 128, kb, (kb - 4 * qi) if kb >= 4 * qi else None)
                                          for kb in range(4 * qi + 3, -1, -1)],
                              lambda qi: 512,
                              (lambda h_: lambda qi: atT_d[h_, :, qi * 512:(qi + 1) * 512])(h),
                              ("KT", hb), ("V", hb))
                for h in range(NH):
                    hb = hc % 2
                    hc += 1
                    P.add("sp", DMA(KT[hb][:, 0:SK], kTs_d[l, h]), writes=[("KT", hb)], dma_key=("KT", hb))
                    P.add("sp", DMA(Vb[hb][:, 0:NKBS, :],
                                    vSs_d[l, 0:PAST, h * 128:(h + 1) * 128].rearrange("(b p) d -> p b d", p=128)),
                          writes=[("V", hb)], dma_key=("V", hb))
                    P.add("sp", DMA(Vb[hb][0:64, NKBS, :], vSs_d[l, PAST:PAST + 64, h * 128:(h + 1) * 128]),
                          writes=[("V", hb)], dma_key=("Vt", hb))
                    attn_head(KT[hb], Vb[hb],
                              [qT_d[h, :, S:S + 64]],
                              lambda qi: [(PAST, 64, NKBS, 0)] + [(kb * 128, 128, kb, None) for kb in range(NKBS - 1, -1, -1)],
                              lambda qi: 64,
                              (lambda h_: lambda qi: atT_d[h_, :, S:S + 64])(h),
                              ("KT", hb), ("V", hb))

            def rep_rstd(bank, T, scale, dst_i, tmp_i):
                P.add("dve", TS(rep[:, tmp_i, 0:T], ps[bank][:, 0:T], scale, EPS, ALU.mult, ALU.add), reads=[("ps", bank)], writes=[("rep", tmp_i)])
                P.add("act", ACTV(rep[:, tmp_i, 0:T], rep[:, tmp_i, 0:T], AF.Sqrt), reads=[("rep", tmp_i)], writes=[("rep", tmp_i)])
                P.add("dve", RCP(rep[:, dst_i, 0:T], rep[:, tmp_i, 0:T]), reads=[("rep", tmp_i)], writes=[("rep", dst_i)])

            def sumsq_ps(bank, src, n, T):
                for i in range(n):
                    tb = i % 2
                    P.add("act", ACTV(tmpf[:, tb, 0:T], src[:, i, 0:T], AF.Square), reads=[("y", i)], writes=[("tmpf", tb)])
                    P.add("pe", MMS([(ps[bank][:, 0:T], onesf[:], tmpf[:, tb, 0:T], i == 0, i == n - 1)]),
                          reads=[("tmpf", tb), "onesf"], writes=[("ps", bank)])

            def mixer_tail(l, kind, t0, T, r0):
                sl = subs(T)
                pc = l * PL
                load_x(x1_d[r0:r0 + T, :], T)
                P.add("sp", DMA(bufA[:, :, 0:T], atT_d.rearrange("h p t -> p h t")[:, :, r0:r0 + T]), writes=[("y", c) for c in range(8)], dma_key="bufA")
                sumsq_ps(0, bufA, NH, T)
                rep_rstd(0, T, 1.0 / 1024, 0, 1)
                for h in range(NH):
                    P.add("dve", STT(hT[:, h, 0:T], bufA[:, h, 0:T], prm[:, pc + 320 + h:pc + 321 + h], rep[:, 0, 0:T], ALU.mult, ALU.mult),
                          reads=[("y", h), ("rep", 0), "prm"], writes=[("hT", h)])
                for c0 in range(0, CC, 2):
                    for c in (c0, c0 + 1):
                        ab = c % 2
                        if kind == "p":
                            src = aT_d[c, :, t0:t0 + 30 + T]
                        else:
                            src = aTs_d[l, c, :, 0:94]
                        P.add("sp", DMA(abuf[ab][:, 0:30 + T], src), writes=[("abuf", ab)], dma_key=("abuf", ab))
                    for w in range(31):
                        for c in (c0, c0 + 1):
                            ab = c % 2
                            yv = bufA[:, c, 0:T]
                            wc = pc + 48 + c * 31
                            if w == 0:
                                P.add("dve", TS(yv, abuf[ab][:, 0:T], prm[:, wc:wc + 1], prm[:, pc + 296 + c:pc + 297 + c], ALU.mult, ALU.add),
                                      reads=[("abuf", ab), "prm"], writes=[("y", c)])
                            else:
                                P.add("dve", STT(yv, abuf[ab][:, w:w + T], prm[:, wc + w:wc + w + 1], yv, ALU.mult, ALU.add),
                                      reads=[("abuf", ab), ("y", c)], writes=[("y", c)])
                for c in range(CC):
                    P.add("pe", MMS([(ps[1][:, 0:T], onesf[:], bufA[:, c, 0:T], c == 0, c == CC - 1)]),
                          reads=[("y", c), "onesf"], writes=[("ps", 1)])
                for c in range(CC):
                    tb = c % 2
                    P.add("act", ACTV(tmpf[:, tb, 0:T], bufA[:, c, 0:T], AF.Square), reads=[("y", c)], writes=[("tmpf", tb)])
                    P.add("pe", MMS([(ps[2][:, 0:T], onesf[:], tmpf[:, tb, 0:T], c == 0, c == CC - 1)]),
                          reads=[("tmpf", tb), "onesf"], writes=[("ps", 2)])
                P.add("dve", TS(rep[:, 2, 0:T], ps[1][:, 0:T], 1.0 / 1024, None, ALU.mult), reads=[("ps", 1)], writes=[("rep", 2)])
                P.add("dve", TT(rep[:, 3, 0:T], rep[:, 2, 0:T], rep[:, 2, 0:T], ALU.mult), reads=[("rep", 2)], writes=[("rep", 3)])
                P.add("dve", STT(rep[:, 3, 0:T], ps[2][:, 0:T], 1.0 / 1024, rep[:, 3, 0:T], ALU.mult, ALU.subtract),
                      reads=[("ps", 2), ("rep", 3)], writes=[("rep", 3)])
                P.add("dve", TS(rep[:, 3, 0:T], rep[:, 3, 0:T], EPS, None, ALU.add), reads=[("rep", 3)], writes=[("rep", 3)])
                P.add("act", ACTV(rep[:, 3, 0:T], rep[:, 3, 0:T], AF.Sqrt), reads=[("rep", 3)], writes=[("rep", 3)])
                P.add("dve", RCP(rep[:, 4, 0:T], rep[:, 3, 0:T]), reads=[("rep", 3)], writes=[("rep", 4)])
                for c in range(CC):
                    yv = bufA[:, c, 0:T]
                    P.add("dve", TT(yv, yv, rep[:, 2, 0:T], ALU.subtract), reads=[("y", c), ("rep", 2)], writes=[("y", c)])
                    P.add("dve", TT(yv, yv, rep[:, 4, 0:T], ALU.mult), reads=[("y", c), ("rep", 4)], writes=[("y", c)])
                    P.add("act", ACTV(yv, yv, AF.Silu, scale=prm[:, pc + 304 + c:pc + 305 + c], bias=prm[:, pc + 312 + c:pc + 313 + c]),
                          reads=[("y", c), "prm"], writes=[("y", c)])
                for c in range(CC):
                    tb = c % 2
                    P.add("act", ACTV(tmpf[:, tb, 0:T], bufA[:, c, 0:T], AF.Square), reads=[("y", c)], writes=[("tmpf", tb)])
                    P.add("pe", MMS([(ps[3][:, 0:T], onesf[:], tmpf[:, tb, 0:T], c == 0, c == CC - 1)]),
                          reads=[("tmpf", tb), "onesf"], writes=[("ps", 3)])
                P.add("dve", TS(rep[:, 5, 0:T], ps[3][:, 0:T], 1.0 / 1024, EPS, ALU.mult, ALU.add), reads=[("ps", 3)], writes=[("rep", 5)])
                P.add("act", ACTV(rep[:, 5, 0:T], rep[:, 5, 0:T], AF.Sqrt), reads=[("rep", 5)], writes=[("rep", 5)])
                P.add("dve", RCP(rep[:, 5, 0:T], rep[:, 5, 0:T]), reads=[("rep", 5)], writes=[("rep", 5)])
                for c in range(CC):
                    P.add("dve", STT(hT[:, 8 + c, 0:T], bufA[:, c, 0:T], prm[:, pc + 328 + c:pc + 329 + c], rep[:, 5, 0:T], ALU.mult, ALU.mult),
                          reads=[("y", c), ("rep", 5), "prm"], writes=[("hT", 8 + c)])
                for cg in range(4):
                    base = 4 * (cg % 2)
                    bw = W.next(wout[l][:, cg * 512:(cg + 1) * 512], 16)
                    for (ts, pr) in sl:
                        P.add("pe", MMS([(ps[base + ts][:pr, :], hT[:, c, ts * 128:ts * 128 + pr], WB[:, bw, c, :], c == 0, c == DC - 1)
                                         for c in range(DC)]), reads=[("wb", bw)] + hreads, writes=[("ps", base + ts)])
                        xs_ = x_tm[:pr, ts, cg * 512:(cg + 1) * 512]
                        P.add("dve", TT(xs_, ps[base + ts][:pr, :], xs_, ALU.add), reads=[("ps", base + ts), ("x", ts)], writes=[("x", ts)])

            def final_norm(kind, t0, T):
                sl = subs(T)
                P.add("dve", MSET(stat[:, 0:4], 0.0), writes=["stat"])
                for (ts, pr) in sl:
                    P.add("act", ACTV(xh[:pr, ts, :], x_tm[:pr, ts, :], AF.Square, accum_out=stat[:pr, ts:ts + 1]),
                          reads=[("x", ts), "stat"], writes=[("xh", ts), "stat"])
                P.add("dve", TS(stat[:, 4:8], stat[:, 0:4], 1.0 / D, EPS, ALU.mult, ALU.add), reads=["stat"], writes=["stat"])
                P.add("act", ACTV(stat[:, 8:12], stat[:, 4:8], AF.Sqrt), reads=["stat"], writes=["stat"])
                P.add("dve", RCP(stat[:, 12:16], stat[:, 8:12]), reads=["stat"], writes=["stat"])
                for (ts, pr) in sl:
                    P.add("dve", STT(x_tm[:pr, ts, :], x_tm[:pr, ts, :], stat[:pr, 12 + ts:13 + ts], gfin[:pr, :], ALU.mult, ALU.mult),
                          reads=[("x", ts), "stat", "gfin"], writes=[("x", ts)])
                dst = yp[t0:t0 + T, :] if kind == "p" else ys
                store_x(dst, T, "xst")

            for l in range(L):
                pc = l * PL
                for (kind, t0, T, r0) in groups:
                    if l == 0:
                        load_x(xp[t0:t0 + T, :] if kind == "p" else xs, T)
                    else:
                        load_x(xc_d[r0:r0 + T, :], T)
                    norm_hT(T, pc + 0)
                    P.barrier()
                    ffn(T, l, 0)
                    store_x(x1_d[r0:r0 + T, :], T, "xst")
                    if KSTOP <= 3:
                        raise _Stop()
                    norm_hT(T, pc + 16)
                    P.barrier()
                    if KSTOP <= 3.2:
                        raise _Stop()
                    win_stage(l, kind, t0, T, r0)
                    P.barrier()
                    if KSTOP <= 3.6 or (KSTOP <= 3.8 and kind == "p" and t0 + T == S):
                        raise _Stop()
                if KSTOP <= 4:
                    raise _Stop()
                attention(l)
                P.barrier()
                if KSTOP <= 5:
                    raise _Stop()
                for (kind, t0, T, r0) in groups:
                    mixer_tail(l, kind, t0, T, r0)
                    P.barrier()
                    norm_hT(T, pc + 32)
                    P.barrier()
                    ffn(T, l, 1)
                    if l < L - 1:
                        store_x(xc_d[r0:r0 + T, :], T, "xst")
                    else:
                        final_norm(kind, t0, T)
                    P.barrier()

        class WStream:
            def __init__(self, P, seq):
                self.P = P
                self.dry = seq is None
                self.seq = [] if seq is None else seq
                self.n = 0
                self.issued = 0

            def next(self, blk, kc):
                if self.dry:
                    self.seq.append((blk, kc))
                    return 0
                n = self.n
                self.n += 1
                lim = min(len(self.seq), n + NWB - 1)
                while self.issued < lim:
                    m = self.issued
                    b_, kc_ = self.seq[m]
                    bi = m % NWB
                    self.P.add("pool", DMA(WB[:, bi, 0:kc_, :], b_.rearrange("(k p) n -> p k n", p=128)),
                               writes=[("wb", bi)], dma_key=("wb", bi))
                    self.issued += 1
                return n % NWB

        dry = WStream(DryProg(), None)
        stages(dry.P, dry)
        P = Prog()
        W = WStream(P, dry.seq)
        stages(P, W)
        P.emit(nc, st)
    return nc


def _pack_params(inp, L):
    prm = np.zeros((128, NPAR), np.float32)

    def fm(v, nchunk):
        return np.ascontiguousarray(v.reshape(nchunk, 128).T)

    for l in range(L):
        pc = l * PL
        prm[:, pc + 0:pc + 16] = fm(inp["norm_ffn1"][l], 16)
        prm[:, pc + 16:pc + 32] = fm(inp["norm_mix"][l], 16)
        prm[:, pc + 32:pc + 48] = fm(inp["norm_ffn2"][l], 16)
        dw = inp["dw_weight"][l]
        prm[:, pc + 48:pc + 296] = dw.reshape(31, 8, 128).transpose(2, 1, 0).reshape(128, 248)
        prm[:, pc + 296:pc + 304] = fm(inp["dw_bias"][l], 8)
        prm[:, pc + 304:pc + 312] = fm(inp["conv_ln_gain"][l], 8)
        prm[:, pc + 312:pc + 320] = fm(inp["conv_ln_bias"][l], 8)
        prm[:, pc + 320:pc + 328] = fm(inp["norm_attn_out"][l], 8)
        prm[:, pc + 328:pc + 336] = fm(inp["norm_conv_out"][l], 8)
    return prm


_NC_CACHE = {}


def kernel(**inp):
    inp = {k: np.asarray(v) for k, v in inp.items()}
    B, S, _ = inp["x_prompt"].shape
    BS = inp["x_sample"].shape[0]
    L = inp["w_in"].shape[0]
    PAST = inp["cache_k"].shape[2]
    import os
    ncores = int(os.environ.get('KCORES', '8'))
    key = (S, PAST, L)
    if key not in _NC_CACHE:
        _NC_CACHE[key] = build(S, PAST, L)
    nc = _NC_CACHE[key]
    prm = _pack_params(inp, L)
    gfin = np.ascontiguousarray(np.broadcast_to(inp["norm_final"][None, :], (128, D))).astype(np.float32)
    in_maps = []
    for c in range(ncores):
        pb = c % B
        sbi = c % BS
        in_maps.append({
            "xp": np.ascontiguousarray(inp["x_prompt"][pb]),
            "xs": np.ascontiguousarray(inp["x_sample"][sbi]),
            "ck": np.ascontiguousarray(inp["cache_k"][:, sbi].reshape(L, PAST, 1024)),
            "cv": np.ascontiguousarray(inp["cache_v"][:, sbi].reshape(L, PAST, 1024)),
            "sc": np.ascontiguousarray(inp["state_conv"][:, sbi]),
            "wg1": inp["ffn1_gate"], "wu1": inp["ffn1_up"], "wd1": inp["ffn1_down"],
            "wg2": inp["ffn2_gate"], "wu2": inp["ffn2_up"], "wd2": inp["ffn2_down"],
            "win": inp["w_in"], "wout": inp["w_out"],
            "prm": prm, "gfin": gfin,
        })
    res = run_bass_kernel_spmd(nc, in_maps, core_ids=list(range(ncores)))
    R = list(res.results)
    while len(R) < 8:
        R.append(R[0])
    y_p = np.stack([R[b]["yp"] for b in range(B)])
    y_s = np.stack([R[b]["ys"] for b in range(BS)])
    nkp = np.stack([R[b]["nkp"] for b in range(B)], axis=1).reshape(L, B, S, NH, 128)
    nvp = np.stack([R[b]["nvp"] for b in range(B)], axis=1).reshape(L, B, S, NH, 128)
    ncp = np.stack([R[b]["ncp"] for b in range(B)], axis=1)
    nks = np.stack([R[b]["nks"] for b in range(BS)], axis=1).reshape(L, BS, 64, NH, 128)
    nvs = np.stack([R[b]["nvs"] for b in range(BS)], axis=1).reshape(L, BS, 64, NH, 128)
    ncs = np.stack([R[b]["ncs"] for b in range(BS)], axis=1)
    return (y_p, y_s, nkp, nvp, ncp, nks, nvs, ncs)
```

```python
import math
from contextlib import ExitStack

import numpy as np
import concourse.bass as bass
import concourse.mybir as mybir
from concourse.bass_utils import run_bass_kernel_spmd

F32 = mybir.dt.float32
BF16 = mybir.dt.bfloat16
AF = mybir.ActivationFunctionType
ALU = mybir.AluOpType

D = 2048
DC = 16
FF = 5632
FC = 44
NH = 8
CC = 8
INC = 5120
EPS = 1e-6
QS = 1.0 / math.sqrt(128.0)
NEG = -30000.0
PL = 336
NPAR = 2 * PL + 16
NWB = 4

SAME_ENGINE_SYNC = True


class Op:
    __slots__ = ("eng", "fn", "deps", "sig", "dma_key", "dma_cnt", "needs_sig")


class Prog:
    ENG = ["pe", "act", "dve", "pool", "sp"]

    def __init__(self):
        self.ops = {e: [] for e in self.ENG}
        self.lw = {}
        self.rd = {}
        self.dma_cnt = {}
        self.dma_last = {}

    def add(self, eng, fn, reads=(), writes=(), dma_key=None, extra_deps=()):
        op = Op()
        op.eng = eng
        op.fn = fn
        op.dma_key = dma_key
        op.needs_sig = False
        op.sig = 0
        op.dma_cnt = 0
        deps = set(extra_deps)
        lw = self.lw
        rd = self.rd
        for r in reads:
            w = lw.get(r)
            if w is not None:
                deps.add(w)
        for w_ in writes:
            w = lw.get(w_)
            if w is not None:
                deps.add(w)
            for x in rd.get(w_, ()):
                deps.add(x)
        if dma_key is not None:
            c = self.dma_cnt.get(dma_key, 0) + 1
            self.dma_cnt[dma_key] = c
            op.dma_cnt = c
            self.dma_last[dma_key] = op
        for r in reads:
            rd.setdefault(r, []).append(op)
        for w_ in writes:
            lw[w_] = op
            rd[w_] = []
        op.deps = deps
        self.ops[eng].append(op)
        return op

    def barrier(self, engines=("pe", "act", "dve", "sp")):
        lasts = []
        for e in engines:
            for op in reversed(self.ops[e]):
                if op.fn is not None and op.dma_key is None:
                    lasts.append(op)
                    break
        deps = set(lasts)
        for k, v in self.dma_last.items():
            if not (isinstance(k, tuple) and k[0] == "wb"):
                deps.add(v)
        for e in engines:
            self.add(e, None, extra_deps=list(deps))

    def emit(self, nc, stack):
        ops = self.ops
        for e in self.ENG:
            for op in ops[e]:
                for d in op.deps:
                    if d.dma_key is not None:
                        continue
                    if d.eng == op.eng and (op.eng == "pe" or (not SAME_ENGINE_SYNC and op.eng != "pool")):
                        continue
                    d.needs_sig = True
        for e in self.ENG:
            c = 0
            for op in ops[e]:
                if op.dma_key is None and op.needs_sig:
                    c += 1
                    op.sig = c
        sems = {}
        for e in self.ENG:
            sems[("eng", e)] = stack.enter_context(nc.semaphore("s_" + e))
        for i, k in enumerate(self.dma_cnt):
            sems[("dma", k)] = stack.enter_context(nc.semaphore("d%d" % i))
        block = stack.enter_context(nc.Block())

        def run(ename, eng):
            known = {}
            for op in ops[ename]:
                waits = {}
                for d in op.deps:
                    if d.dma_key is not None:
                        key = ("dma", d.dma_key)
                        val = 16 * d.dma_cnt
                    else:
                        if d.eng == ename and (ename == "pe" or (not SAME_ENGINE_SYNC and ename != "pool")):
                            continue
                        key = ("eng", d.eng)
                        val = d.sig
                    if waits.get(key, 0) < val:
                        waits[key] = val
                for key, val in waits.items():
                    if known.get(key, 0) >= val:
                        continue
                    eng.wait_ge(sems[key], val)
                    known[key] = val
                if op.fn is None:
                    continue
                inst = op.fn(eng)
                if op.dma_key is not None:
                    inst.then_inc(sems[("dma", op.dma_key)], 16)
                elif op.needs_sig:
                    inst.then_inc(sems[("eng", ename)], 1)

        @block.tensor
        def _(eng):
            run("pe", eng)

        @block.scalar
        def _(eng):
            run("act", eng)

        @block.vector
        def _(eng):
            run("dve", eng)

        @block.gpsimd
        def _(eng):
            run("pool", eng)

        @block.sync
        def _(eng):
            run("sp", eng)


class DryProg:
    def add(self, *a, **k):
        return None

    def barrier(self, *a, **k):
        return None


def MMS(lst):
    def f(e):
        r = None
        for (o, l, rh, s, t) in lst:
            r = e.matmul(o, lhsT=l, rhs=rh, start=s, stop=t)
        return r
    return f


def TRS(lst):
    def f(e):
        r = None
        for (o, i, idn) in lst:
            r = e.transpose(out=o, in_=i, identity=idn)
        return r
    return f


def ACTV(out, in_, func, **kw):
    return lambda e: e.activation(out=out, in_=in_, func=func, **kw)


def TT(out, in0, in1, op):
    return lambda e: e.tensor_tensor(out=out, in0=in0, in1=in1, op=op)


def TS(out, in0, s1, s2, op0, op1=None):
    if op1 is None:
        return lambda e: e.tensor_scalar(out=out, in0=in0, scalar1=s1, scalar2=None, op0=op0)
    return lambda e: e.tensor_scalar(out=out, in0=in0, scalar1=s1, scalar2=s2, op0=op0, op1=op1)


def STT(out, in0, scalar, in1, op0, op1):
    return lambda e: e.scalar_tensor_tensor(out=out, in0=in0, scalar=scalar, in1=in1, op0=op0, op1=op1)


def CP(out, in_):
    return lambda e: e.tensor_copy(out=out, in_=in_)


def RCP(out, in_):
    return lambda e: e.reciprocal(out=out, in_=in_)


def MSET(ap, v):
    return lambda e: e.memset(ap, v)


def DMA(out, in_):
    return lambda e: e.dma_start(out=out, in_=in_)


def subs(T):
    return [(ts, min(128, T - ts * 128)) for ts in range((T + 127) // 128)]


def build(S, PAST, L):
    NG = S // 512
    NKBS = PAST // 128
    SK = PAST + 64
    nc = bass.Bass("TRN2", target_bir_lowering=False)

    def din(name, shape, dt=F32):
        return nc.dram_tensor(name, list(shape), dt, kind="ExternalInput").ap()

    def dout(name, shape, dt=F32):
        return nc.dram_tensor(name, list(shape), dt, kind="ExternalOutput").ap()

    def dscr(name, shape, dt=F32):
        return nc.dram_tensor(name, list(shape), dt, kind="Internal").ap()

    xp = din("xp", [S, D])
    xs = din("xs", [64, D])
    ck = din("ck", [L, PAST, 1024])
    cv = din("cv", [L, PAST, 1024])
    sc = din("sc", [L, 30, 1024])
    wg = [din("wg1", [L, D, FF]), din("wg2", [L, D, FF])]
    wu = [din("wu1", [L, D, FF]), din("wu2", [L, D, FF])]
    wd = [din("wd1", [L, FF, D]), din("wd2", [L, FF, D])]
    win = din("win", [L, D, INC])
    wout = din("wout", [L, D, D])
    prm_d = din("prm", [128, NPAR])
    gfin_d = din("gfin", [128, D])

    yp = dout("yp", [S, D])
    ys = dout("ys", [64, D])
    nkp = dout("nkp", [L, S, 1024])
    nvp = dout("nvp", [L, S, 1024])
    ncp = dout("ncp", [L, 30, 1024])
    nks = dout("nks", [L, 64, 1024])
    nvs = dout("nvs", [L, 64, 1024])
    ncs = dout("ncs", [L, 30, 1024])

    x1_d = dscr("x1_d", [S + 64, D])
    xc_d = dscr("xc_d", [S + 64, D])
    qT_d = dscr("qT_d", [NH, 128, S + 64], BF16)
    kT_d = dscr("kT_d", [NH, 128, S], BF16)
    kTs_d = dscr("kTs_d", [L, NH, 128, SK], BF16)
    vS_d = dscr("vS_d", [S, 1024], BF16)
    vSs_d = dscr("vSs_d", [L, SK, 1024], BF16)
    aT_d = dscr("aT_d", [CC, 128, 30 + S])
    aTs_d = dscr("aTs_d", [L, CC, 128, 94])
    atT_d = dscr("atT_d", [NH, 128, S + 64])

    groups = [("p", g * 512, 512, g * 512) for g in range(NG)] + [("s", 0, 64, S)]

    with ExitStack() as st:
        def sb(name, shape, dt):
            return st.enter_context(nc.sbuf_tensor("sb_" + name, list(shape), dt))

        x_tm = sb("x_tm", [128, 4, D], F32)
        arena = sb("arena", [128, 22528], BF16)
        xh = sb("xh", [128, 4, D], BF16)
        hT = sb("hT", [128, DC, 512], BF16)
        WB = sb("WB", [128, NWB, 16, 512], BF16)
        tmpf = sb("tmpf", [128, 2, 512], F32)
        stat = sb("stat", [128, 16], F32)
        rep = sb("rep", [128, 6, 512], F32)
        prm = sb("prm", [128, NPAR], F32)
        gfin = sb("gfin", [128, D], F32)
        identb = sb("identb", [128, 128], BF16)
        identf = sb("identf", [128, 128], F32)
        onesf = sb("onesf", [128, 128], F32)
        ntri = sb("ntri", [128, 128], BF16)
        nones = sb("nones", [128, 128], BF16)
        masks = sb("masks", [128, 4, 512], BF16)
        zf = sb("zf", [128, 512], F32)
        ps = [st.enter_context(nc.psum_tensor("ps%d" % i, [128, 512], F32)) for i in range(8)]

        def av(off_bytes, shape, dt):
            n = 1
            for s_ in shape[1:]:
                n *= s_
            nb = n * (4 if dt == F32 else 2)
            v = arena[:, off_bytes // 2:(off_bytes + nb) // 2]
            if dt == F32:
                v = v.bitcast(F32)
            if len(shape) == 3:
                v = v.rearrange("p (a b) -> p a b", a=shape[1])
            return v

        def xv(off_bytes, shape, dt):
            n = 1
            for s_ in shape[1:]:
                n *= s_
            flat = x_tm[:].rearrange("p a b -> p (a b)")
            nb = n * (4 if dt == F32 else 2)
            v = flat[:, off_bytes // 4:(off_bytes + nb) // 4]
            if dt == BF16:
                v = v.bitcast(BF16)
            if len(shape) == 3:
                v = v.rearrange("p (a b) -> p a b", a=shape[1])
            return v

        KB = 1024
        actT = av(0, [128, FC, 512], BF16)
        qTb = av(0, [128, NH, 512], BF16)
        kTb = av(8 * KB, [128, NH, 512], BF16)
        v_bf = av(16 * KB, [128, 4, 1024], BF16)
        aTb = av(24 * KB, [128, CC, 512], F32)
        kv_k = xv(0, [128, 4, 1024], F32)
        kv_v = xv(16 * KB, [128, 4, 1024], F32)
        bufA = av(0, [128, 8, 512], F32)
        abuf = [av(16 * KB, [128, 544], F32), av(16 * KB + 2176, [128, 544], F32)]
        nct = av(40 * KB, [128, 1024], F32)
        sct = av(28 * KB, [128, 1024], F32)
        scT = av(32 * KB, [128, CC, 32], F32)
        KT = [av(0, [128, 8192], BF16), xv(0, [128, 8192], BF16)]
        Vb = [av(16 * KB, [128, 64, 128], BF16), xv(16 * KB, [128, 64, 128], BF16)]
        o2 = 33 * KB
        qt = [av(o2, [128, 512], BF16), av(o2 + 1 * KB, [128, 512], BF16)]
        ebuf = [av(o2 + 2 * KB, [128, 512], F32), av(o2 + 4 * KB, [128, 512], F32)]
        spb = [av(o2 + 6 * KB, [128, 512], BF16), av(o2 + 7 * KB, [128, 512], BF16)]
        pb = [av(o2 + 8 * KB, [128, 512], BF16), av(o2 + 9 * KB, [128, 512], BF16)]
        rbb = [hT[:, 0, :], hT[:, 1, :]]
        Rf = xh[:, 0, 0:1024].bitcast(F32)
        oTs = [xh[:, 1, 0:1024].bitcast(F32), xh[:, 2, 0:1024].bitcast(F32)]
        ckb = [hT[:, 4:6, :].rearrange("p a b -> p (a b)"), hT[:, 6:8, :].rearrange("p a b -> p (a b)")]
        kTc = [hT[:, 8:10, :].rearrange("p a b -> p (a b)").rearrange("p (h s) -> p h s", h=NH),
               hT[:, 10:12, :].rearrange("p a b -> p (a b)").rearrange("p (h s) -> p h s", h=NH)]

        import os
        KSTOP = float(os.environ.get("KSTOP", "99"))

        class _Stop(Exception):
            pass

        def stages(P, W):
            try:
                _stages(P, W)
            except _Stop:
                pass
            P.barrier(engines=("pe", "act", "dve", "pool", "sp"))

        def _stages(P, W):
            P.add("sp", DMA(prm[:], prm_d), writes=["prm"], dma_key="prm")
            P.add("sp", DMA(gfin[:], gfin_d), writes=["gfin"], dma_key="gfin")
            P.add("pool", MSET(zf[:], 0.0), writes=["zf"])
            P.add("pool", MSET(identf[:], 0.0), writes=["identf"])
            P.add("pool", lambda e: e.affine_select(out=identf[:], in_=identf[:], pattern=[[-1, 128]],
                                                   compare_op=ALU.not_equal, fill=1.0, base=0,
                                                   channel_multiplier=1), reads=["identf"], writes=["identf"])
            P.add("dve", CP(identb[:], identf[:]), reads=["identf"], writes=["identb"])
            P.add("pool", MSET(onesf[:], 1.0), writes=["onesf"])
            P.add("pool", MSET(nones[:], -1.0), writes=["nones"])
            P.add("pool", lambda e: e.affine_select(out=ntri[:], in_=nones[:], pattern=[[-1, 128]],
                                                   compare_op=ALU.is_ge, fill=0.0, base=0,
                                                   channel_multiplier=1), reads=["nones"], writes=["ntri"])
            for r in range(4):
                P.add("pool", (lambda r_: lambda e: e.affine_select(
                    out=masks[:, r_, :], in_=zf[:], pattern=[[1, 512]], compare_op=ALU.is_ge, fill=NEG,
                    base=-128 * r_ - 1, channel_multiplier=-1))(r), reads=["zf"], writes=["masks"])
            P.add("sp", DMA(aT_d.rearrange("c p t -> p c t")[:, :, 0:30],
                            zf[:, 0:240].rearrange("p (c t) -> p c t", c=CC)), reads=["zf"], dma_key="zh")
            P.barrier(engines=("pe", "act", "dve", "pool", "sp"))
            if KSTOP <= 1:
                raise _Stop()

            evac_rr = [0]

            def evac_copy(out, in_, reads, writes, scale=None):
                evac_rr[0] ^= 1
                if evac_rr[0]:
                    if scale is None:
                        P.add("act", ACTV(out, in_, AF.Copy), reads=reads, writes=writes)
                    else:
                        P.add("act", ACTV(out, in_, AF.Copy, scale=scale), reads=reads, writes=writes)
                else:
                    if scale is None:
                        P.add("dve", CP(out, in_), reads=reads, writes=writes)
                    else:
                        P.add("dve", TS(out, in_, scale, None, ALU.mult), reads=reads, writes=writes)

            for l in range(L):
                for blk in range(NKBS):
                    b2 = blk % 2
                    P.add("pool", DMA(ckb[b2], ck[l, blk * 128:(blk + 1) * 128, :]), writes=[("ckb", b2)],
                          dma_key=("ckb", b2))
                    for half in range(2):
                        bank = ps[(2 * blk + half) % 4]
                        pv = bank[:].bitcast(BF16)
                        P.add("pe", TRS([(pv[:, i * 128:(i + 1) * 128],
                                          ckb[b2][:, (half * 4 + i) * 128:(half * 4 + i + 1) * 128], identb[:])
                                         for i in range(4)]),
                              reads=[("ckb", b2), "identb"], writes=[("ps", (2 * blk + half) % 4)])
                        evac_copy(kTc[b2][:, half * 4:half * 4 + 4, :],
                                  pv[:, 0:512].rearrange("p (h s) -> p h s", h=4),
                                  [("ps", (2 * blk + half) % 4)], [("kTc", b2)])
                    P.add("sp", DMA(kTs_d[l].rearrange("h p s -> p h s")[:, :, blk * 128:(blk + 1) * 128], kTc[b2]),
                          reads=[("kTc", b2)], dma_key=("kTc", b2))
                    P.add("pool", DMA(xh[:, b2, 0:1024], cv[l, blk * 128:(blk + 1) * 128, :]), writes=[("cvb", b2)],
                          dma_key=("cvb", b2))
                    P.add("sp", DMA(vSs_d[l, blk * 128:(blk + 1) * 128, :], xh[:, b2, 0:1024]), reads=[("cvb", b2)],
                          dma_key=("cvs", b2))
                P.add("sp", DMA(sct[0:30, :], sc[l]), writes=["sct"], dma_key="sct")
                for c in range(CC):
                    P.add("pe", TRS([(ps[4 + c % 2][:, 0:30], sct[0:30, c * 128:(c + 1) * 128], identf[0:30, 0:30])]),
                          reads=["sct", "identf"], writes=[("ps", 4 + c % 2)])
                    evac_copy(scT[:, c, 0:30], ps[4 + c % 2][:, 0:30], [("ps", 4 + c % 2)], ["scT"])
                P.add("sp", DMA(aTs_d[l].rearrange("c p t -> p c t")[:, :, 0:30], scT[:, :, 0:30]), reads=["scT"],
                      dma_key="scT")
            P.barrier()
            if KSTOP <= 2:
                raise _Stop()

            def norm_hT(T, gcol):
                sl = subs(T)
                P.add("dve", MSET(stat[:, 0:4], 0.0), writes=["stat"])
                for (ts, pr) in sl:
                    P.add("act", ACTV(xh[:pr, ts, :], x_tm[:pr, ts, :], AF.Square, accum_out=stat[:pr, ts:ts + 1]),
                          reads=[("x", ts), "stat"], writes=[("xh", ts), "stat"])
                P.add("dve", TS(stat[:, 4:8], stat[:, 0:4], 1.0 / D, EPS, ALU.mult, ALU.add), reads=["stat"], writes=["stat"])
                P.add("act", ACTV(stat[:, 8:12], stat[:, 4:8], AF.Sqrt), reads=["stat"], writes=["stat"])
                P.add("dve", RCP(stat[:, 12:16], stat[:, 8:12]), reads=["stat"], writes=["stat"])
                for (ts, pr) in sl:
                    P.add("act", ACTV(xh[:pr, ts, :], x_tm[:pr, ts, :], AF.Copy, scale=stat[:pr, 12 + ts:13 + ts]),
                          reads=[("x", ts), "stat"], writes=[("xh", ts)])
                for c in range(DC):
                    bk = 4 + c % 2
                    pv = ps[bk][:].bitcast(BF16)
                    P.add("pe", TRS([(pv[:, ts * 128:ts * 128 + pr], xh[:pr, ts, c * 128:(c + 1) * 128], identb[:pr, :pr])
                                     for (ts, pr) in sl]),
                          reads=[("xh", ts) for (ts, pr) in sl] + ["identb"], writes=[("ps", bk)])
                    evac_copy(hT[:, c, 0:T], pv[:, 0:T], [("ps", bk), "prm"], [("hT", c)],
                              scale=prm[:, gcol + c:gcol + c + 1])

            def ffn(T, l, which):
                sl = subs(T)
                Wg, Wu, Wd = wg[which][l], wu[which][l], wd[which][l]
                hreads = [("hT", c) for c in range(DC)]
                for fg in range(11):
                    bg = W.next(Wg[:, fg * 512:(fg + 1) * 512], 16)
                    bu = W.next(Wu[:, fg * 512:(fg + 1) * 512], 16)
                    for fi in range(4):
                        f = fg * 4 + fi
                        gb = f % 2
                        ub = 2 + f % 2
                        P.add("pe", MMS([(ps[gb][:, 0:T], WB[:, bg, c, fi * 128:(fi + 1) * 128], hT[:, c, 0:T], c == 0, c == DC - 1)
                                         for c in range(DC)]),
                              reads=[("wb", bg)] + hreads, writes=[("ps", gb)])
                        P.add("pe", MMS([(ps[ub][:, 0:T], WB[:, bu, c, fi * 128:(fi + 1) * 128], hT[:, c, 0:T], c == 0, c == DC - 1)
                                         for c in range(DC)]),
                              reads=[("wb", bu)] + hreads, writes=[("ps", ub)])
                        P.add("act", ACTV(tmpf[:, gb, 0:T], ps[gb][:, 0:T], AF.Silu), reads=[("ps", gb)], writes=[("tmpf", gb)])
                        P.add("dve", TT(actT[:, f, 0:T], ps[ub][:, 0:T], tmpf[:, gb, 0:T], ALU.mult),
                              reads=[("ps", ub), ("tmpf", gb)], writes=[("actT", f)])
                for cg in range(4):
                    base = 4 * (cg % 2)
                    for blk in range(3):
                        kc = 16 if blk < 2 else 12
                        bw = W.next(Wd[blk * 2048:blk * 2048 + kc * 128, cg * 512:(cg + 1) * 512], kc)
                        lst = []
                        for fc in range(kc):
                            f = blk * 16 + fc
                            for (ts, pr) in sl:
                                lst.append((ps[base + ts][:pr, :], actT[:, f, ts * 128:ts * 128 + pr], WB[:, bw, fc, :],
                                            f == 0, f == FC - 1))
                        P.add("pe", MMS(lst), reads=[("wb", bw)] + [("actT", blk * 16 + fc) for fc in range(kc)],
                              writes=[("ps", base + ts) for (ts, pr) in sl])
                    for (ts, pr) in sl:
                        xs_ = x_tm[:pr, ts, cg * 512:(cg + 1) * 512]
                        P.add("dve", STT(xs_, ps[base + ts][:pr, :], 0.5, xs_, ALU.mult, ALU.add),
                              reads=[("ps", base + ts), ("x", ts)], writes=[("x", ts)])

            def load_x(src_rows, T):
                for (ts, pr) in subs(T):
                    P.add("sp", DMA(x_tm[:pr, ts, :], src_rows[ts * 128:ts * 128 + pr, :]), writes=[("x", ts)],
                          dma_key=("xl", ts))

            def store_x(dst_rows, T, key):
                for (ts, pr) in subs(T):
                    P.add("sp", DMA(dst_rows[ts * 128:ts * 128 + pr, :], x_tm[:pr, ts, :]), reads=[("x", ts)],
                          dma_key=(key, ts))

            hreads = [("hT", c) for c in range(DC)]

            def win_stage(l, kind, t0, T, r0):
                sl = subs(T)
                Wi = win[l]
                bankc = [0]

                def nb():
                    bankc[0] = (bankc[0] + 1) % 8
                    return bankc[0]

                def fm_block(bw, dst, scale, h0):
                    for hi in range(4):
                        bk = nb()
                        P.add("pe", MMS([(ps[bk][:, 0:T], WB[:, bw, c, hi * 128:(hi + 1) * 128], hT[:, c, 0:T], c == 0, c == DC - 1)
                                         for c in range(DC)]), reads=[("wb", bw)] + hreads, writes=[("ps", bk)])
                        evac_copy(dst[:, h0 + hi, 0:T], ps[bk][:, 0:T], [("ps", bk)], [("stg", id(dst))], scale=scale)

                def tm_block(bw, j, dst32, dstb):
                    for (ts, pr) in sl:
                        bk = nb()
                        P.add("pe", MMS([(ps[bk][:pr, :], hT[:, c, ts * 128:ts * 128 + pr], WB[:, bw, c, :], c == 0, c == DC - 1)
                                         for c in range(DC)]), reads=[("wb", bw)] + hreads, writes=[("ps", bk)])
                        P.add("act", ACTV(dst32[:pr, ts, j * 512:(j + 1) * 512], ps[bk][:pr, :], AF.Copy),
                              reads=[("ps", bk)], writes=[("stg", id(dst32))])
                        if dstb is not None:
                            P.add("dve", CP(dstb[:pr, ts, j * 512:(j + 1) * 512], dst32[:pr, ts, j * 512:(j + 1) * 512]),
                                  reads=[("stg", id(dst32))], writes=[("stg", id(dstb))])

                for j in range(2):
                    bw = W.next(Wi[:, j * 512:(j + 1) * 512], 16)
                    fm_block(bw, qTb, QS, j * 4)
                P.add("sp", DMA(qT_d.rearrange("h p t -> p h t")[:, :, r0:r0 + T], qTb[:, :, 0:T]),
                      reads=[("stg", id(qTb))], dma_key="qTb")
                for j in range(2):
                    bw = W.next(Wi[:, 1024 + j * 512:1024 + (j + 1) * 512], 16)
                    fm_block(bw, kTb, None, j * 4)
                    tm_block(bw, j, kv_k, None)
                if kind == "p":
                    P.add("sp", DMA(kT_d.rearrange("h p t -> p h t")[:, :, t0:t0 + T], kTb[:, :, 0:T]),
                          reads=[("stg", id(kTb))], dma_key="kTb")
                    P.add("sp", DMA(nkp[l, t0:t0 + T, :].rearrange("(a p) n -> p a n", p=128), kv_k[:, :, :]),
                          reads=[("stg", id(kv_k))], dma_key="kvk")
                else:
                    P.add("sp", DMA(kTs_d[l].rearrange("h p t -> p h t")[:, :, PAST:PAST + 64], kTb[:, :, 0:64]),
                          reads=[("stg", id(kTb))], dma_key="kTb")
                    P.add("sp", DMA(nks[l], kv_k[0:64, 0, :]), reads=[("stg", id(kv_k))], dma_key="kvk")
                if KSTOP <= 3.4:
                    raise _Stop()
                for j in range(2):
                    bw = W.next(Wi[:, 2048 + j * 512:2048 + (j + 1) * 512], 16)
                    tm_block(bw, j, kv_v, v_bf)
                if kind == "p":
                    P.add("sp", DMA(nvp[l, t0:t0 + T, :].rearrange("(a p) n -> p a n", p=128), kv_v[:, :, :]),
                          reads=[("stg", id(kv_v))], dma_key="kvv")
                    P.add("sp", DMA(vS_d[t0:t0 + T, :].rearrange("(a p) n -> p a n", p=128), v_bf[:, :, :]),
                          reads=[("stg", id(v_bf))], dma_key="vbf")
                else:
                    P.add("sp", DMA(nvs[l], kv_v[0:64, 0, :]), reads=[("stg", id(kv_v))], dma_key="kvv")
                    P.add("sp", DMA(vSs_d[l, PAST:PAST + 64, :], v_bf[0:64, 0, :]), reads=[("stg", id(v_bf))], dma_key="vbf")
                if KSTOP <= 3.5:
                    raise _Stop()
                for j in range(2):
                    bv = W.next(Wi[:, 3072 + j * 512:3072 + (j + 1) * 512], 16)
                    bg = W.next(Wi[:, 4096 + j * 512:4096 + (j + 1) * 512], 16)
                    for ci in range(4):
                        ch = j * 4 + ci
                        bkv = nb()
                        bkg = nb()
                        P.add("pe", MMS([(ps[bkv][:, 0:T], WB[:, bv, c, ci * 128:(ci + 1) * 128], hT[:, c, 0:T], c == 0, c == DC - 1)
                                         for c in range(DC)]), reads=[("wb", bv)] + hreads, writes=[("ps", bkv)])
                        P.add("pe", MMS([(ps[bkg][:, 0:T], WB[:, bg, c, ci * 128:(ci + 1) * 128], hT[:, c, 0:T], c == 0, c == DC - 1)
                                         for c in range(DC)]), reads=[("wb", bg)] + hreads, writes=[("ps", bkg)])
                        tb = ch % 2
                        P.add("act", ACTV(tmpf[:, tb, 0:T], ps[bkg][:, 0:T], AF.Sigmoid), reads=[("ps", bkg)], writes=[("tmpf", tb)])
                        P.add("dve", TT(aTb[:, ch, 0:T], ps[bkv][:, 0:T], tmpf[:, tb, 0:T], ALU.mult),
                              reads=[("ps", bkv), ("tmpf", tb)], writes=[("stg", id(aTb))])
                if kind == "p":
                    P.add("sp", DMA(aT_d.rearrange("c p t -> p c t")[:, :, 30 + t0:30 + t0 + T], aTb[:, :, 0:T]),
                          reads=[("stg", id(aTb))], dma_key="aTb")
                else:
                    P.add("sp", DMA(aTs_d[l].rearrange("c p t -> p c t")[:, :, 30:94], aTb[:, :, 0:64]),
                          reads=[("stg", id(aTb))], dma_key="aTb")
                if kind == "s" or t0 + T == S:
                    for c in range(CC):
                        bk = nb()
                        P.add("pe", TRS([(ps[bk][0:32, 0:128], aTb[:, c, T - 32:T], identf[:])]),
                              reads=[("stg", id(aTb)), "identf"], writes=[("ps", bk)])
                        evac_copy(nct[0:32, c * 128:(c + 1) * 128], ps[bk][0:32, 0:128], [("ps", bk)], ["nct"])
                    dst = ncs[l] if kind == "s" else ncp[l]
                    P.add("sp", DMA(dst, nct[2:32, :]), reads=["nct"], dma_key="nct")

            def attn_head(KTv, Vv, qsrc_list, blocks_of, Tq_of, out_dst_of, kres, vres):
                for qi, qsrc in enumerate(qsrc_list):
                    Tq = Tq_of(qi)
                    blocks = blocks_of(qi)
                    qb = qi % 2
                    P.add("sp", DMA(qt[qb][:, 0:Tq], qsrc), writes=[("qt", qb)], dma_key=("qt", qb))
                    ob = 6 + qi % 2
                    nblk = len(blocks)
                    need_r_zero = any(b[1] < 128 for b in blocks)
                    if need_r_zero:
                        P.add("dve", MSET(Rf[:, 0:Tq], 0.0), writes=["Rf"])

                    def zlist(k, bank, last):
                        col0, rows, vblk, mr = blocks[k]
                        lst = [(ps[bank][:rows, 0:Tq], KTv[:, col0:col0 + rows], qt[qb][:, 0:Tq], True, last and mr is None)]
                        rd = [kres, ("qt", qb)]
                        if mr is not None:
                            lst.append((ps[bank][:rows, 0:Tq], identb[:rows, :rows], masks[:rows, mr, 0:Tq], False, last))
                            rd += ["identb", "masks"]
                        return lst, rd

                    def zmm(k):
                        lst, rd = zlist(k, k % 2, True)
                        P.add("pe", MMS(lst), reads=rd, writes=[("ps", k % 2)])

                    def emm(k):
                        rows_ = blocks[k][1]
                        P.add("act", ACTV(ebuf[k % 2][:rows_, 0:Tq], ps[k % 2][:rows_, 0:Tq], AF.Exp),
                              reads=[("ps", k % 2)], writes=[("e", k % 2)])

                    zmm(0)
                    if nblk > 1:
                        zmm(1)
                    emm(0)
                    for k in range(nblk + 1):
                        if k + 1 < nblk:
                            emm(k + 1)
                        if k + 2 < nblk:
                            zmm(k + 2)
                        if k < nblk:
                            col0, rows, vblk, mr = blocks[k]
                            z2 = 2 + k % 2
                            kb2 = k % 2
                            P.add("act", ACTV(spb[kb2][:rows, 0:Tq], ebuf[kb2][:rows, 0:Tq], AF.Ln, bias=1.0),
                                  reads=[("e", kb2)], writes=[("sp", kb2)])
                            lst, rd = zlist(k, z2, False)
                            lst.append((ps[z2][:rows, 0:Tq], ntri[:rows, :rows], spb[kb2][:rows, 0:Tq], False, k == 0))
                            rd += [("sp", kb2), "ntri"]
                            if k > 0:
                                prow = blocks[k - 1][1] if k == 1 else 128
                                lst.append((ps[z2][:rows, 0:Tq], nones[:prow, :rows], rbb[kb2][:prow, 0:Tq], False, True))
                                rd += [("rb", kb2), "nones"]
                            P.add("pe", MMS(lst), reads=rd, writes=[("ps", z2)])
                            if k + 1 < nblk:
                                if k == 0 and not need_r_zero:
                                    P.add("dve", CP(Rf[:rows, 0:Tq], spb[kb2][:rows, 0:Tq]), reads=[("sp", kb2)], writes=["Rf"])
                                else:
                                    P.add("dve", TT(Rf[:rows, 0:Tq], Rf[:rows, 0:Tq], spb[kb2][:rows, 0:Tq], ALU.add),
                                          reads=[("sp", kb2), "Rf"], writes=["Rf"])
                                nr = rows if k == 0 else 128
                                P.add("dve", CP(rbb[(k + 1) % 2][:nr, 0:Tq], Rf[:nr, 0:Tq]), reads=["Rf"],
                                      writes=[("rb", (k + 1) % 2)])
                        if k >= 1:
                            col0, rows, vblk, mr = blocks[k - 1]
                            z2 = 2 + (k - 1) % 2
                            kb2 = (k - 1) % 2
                            P.add("act", ACTV(pb[kb2][:rows, 0:Tq], ps[z2][:rows, 0:Tq], AF.Exp),
                                  reads=[("ps", z2)], writes=[("pb", kb2)])
                            P.add("pe", MMS([(ps[ob][:, 0:Tq], Vv[:rows, vblk, :], pb[kb2][:rows, 0:Tq], k == 1, k == nblk)]),
                                  reads=[("pb", kb2), vres], writes=[("ps", ob)])
                    osb = qi % 2
                    evac_copy(oTs[osb][:, 0:Tq], ps[ob][:, 0:Tq], [("ps", ob)], [("oT", osb)])
                    P.add("sp", DMA(out_dst_of(qi), oTs[osb][:, 0:Tq]), reads=[("oT", osb)], dma_key=("oT", osb))

            def attention(l):
                hc = 0
                for h in range(NH):
                    hb = hc % 2
                    hc += 1
                    P.add("sp", DMA(KT[hb][:, 0:S], kT_d[h]), writes=[("KT", hb)], dma_key=("KT", hb))
                    P.add("sp", DMA(Vb[hb][:, 0:S // 128, :], vS_d[:, h * 128:(h + 1) * 128].rearrange("(b p) d -> p b d", p=128)),
                          writes=[("V", hb)], dma_key=("V", hb))
                    attn_head(KT[hb], Vb[hb],
                              [qT_d[h, :, qi * 512:(qi + 1) * 512] for qi in range(S // 512)],
                              lambda qi: [(kb * 128, 128, kb, (kb - 4 * qi) if kb >= 4 * qi else None)
                                          for kb in range(4 * qi + 3, -1, -1)],
                              lambda qi: 512,
                              (lambda h_: lambda qi: atT_d[h_, :, qi * 512:(qi + 1) * 512])(h),
                              ("KT", hb), ("V", hb))
                for h in range(NH):
                    hb = hc % 2
                    hc += 1
                    P.add("sp", DMA(KT[hb][:, 0:SK], kTs_d[l, h]), writes=[("KT", hb)], dma_key=("KT", hb))
                    P.add("sp", DMA(Vb[hb][:, 0:NKBS, :],
                                    vSs_d[l, 0:PAST, h * 128:(h + 1) * 128].rearrange("(b p) d -> p b d", p=128)),
                          writes=[("V", hb)], dma_key=("V", hb))
                    P.add("sp", DMA(Vb[hb][0:64, NKBS, :], vSs_d[l, PAST:PAST + 64, h * 128:(h + 1) * 128]),
                          writes=[("V", hb)], dma_key=("Vt", hb))
                    attn_head(KT[hb], Vb[hb],
                              [qT_d[h, :, S:S + 64]],
                              lambda qi: [(PAST, 64, NKBS, 0)] + [(kb * 128, 128, kb, None) for kb in range(NKBS - 1, -1, -1)],
                              lambda qi: 64,
                              (lambda h_: lambda qi: atT_d[h_, :, S:S + 64])(h),
                              ("KT", hb), ("V", hb))

            def rep_rstd(bank, T, scale, dst_i, tmp_i):
                P.add("dve", TS(rep[:, tmp_i, 0:T], ps[bank][:, 0:T], scale, EPS, ALU.mult, ALU.add), reads=[("ps", bank)], writes=[("rep", tmp_i)])
                P.add("act", ACTV(rep[:, tmp_i, 0:T], rep[:, tmp_i, 0:T], AF.Sqrt), reads=[("rep", tmp_i)], writes=[("rep", tmp_i)])
                P.add("dve", RCP(rep[:, dst_i, 0:T], rep[:, tmp_i, 0:T]), reads=[("rep", tmp_i)], writes=[("rep", dst_i)])

            def sumsq_ps(bank, src, n, T):
                for i in range(n):
                    tb = i % 2
                    P.add("act", ACTV(tmpf[:, tb, 0:T], src[:, i, 0:T], AF.Square), reads=[("y", i)], writes=[("tmpf", tb)])
                    P.add("pe", MMS([(ps[bank][:, 0:T], onesf[:], tmpf[:, tb, 0:T], i == 0, i == n - 1)]),
                          reads=[("tmpf", tb), "onesf"], writes=[("ps", bank)])

            def mixer_tail(l, kind, t0, T, r0):
                sl = subs(T)
                pc = l * PL
                load_x(x1_d[r0:r0 + T, :], T)
                P.add("sp", DMA(bufA[:, :, 0:T], atT_d.rearrange("h p t -> p h t")[:, :, r0:r0 + T]), writes=[("y", c) for c in range(8)], dma_key="bufA")
                sumsq_ps(0, bufA, NH, T)
                rep_rstd(0, T, 1.0 / 1024, 0, 1)
                for h in range(NH):
                    P.add("dve", STT(hT[:, h, 0:T], bufA[:, h, 0:T], prm[:, pc + 320 + h:pc + 321 + h], rep[:, 0, 0:T], ALU.mult, ALU.mult),
                          reads=[("y", h), ("rep", 0), "prm"], writes=[("hT", h)])
                for c0 in range(0, CC, 2):
                    for c in (c0, c0 + 1):
                        ab = c % 2
                        if kind == "p":
                            src = aT_d[c, :, t0:t0 + 30 + T]
                        else:
                            src = aTs_d[l, c, :, 0:94]
                        P.add("sp", DMA(abuf[ab][:, 0:30 + T], src), writes=[("abuf", ab)], dma_key=("abuf", ab))
                    for w in range(31):
                        for c in (c0, c0 + 1):
                            ab = c % 2
                            yv = bufA[:, c, 0:T]
                            wc = pc + 48 + c * 31
                            if w == 0:
                                P.add("dve", TS(yv, abuf[ab][:, 0:T], prm[:, wc:wc + 1], prm[:, pc + 296 + c:pc + 297 + c], ALU.mult, ALU.add),
                                      reads=[("abuf", ab), "prm"], writes=[("y", c)])
                            else:
                                P.add("dve", STT(yv, abuf[ab][:, w:w + T], prm[:, wc + w:wc + w + 1], yv, ALU.mult, ALU.add),
                                      reads=[("abuf", ab), ("y", c)], writes=[("y", c)])
                for c in range(CC):
                    P.add("pe", MMS([(ps[1][:, 0:T], onesf[:], bufA[:, c, 0:T], c == 0, c == CC - 1)]),
                          reads=[("y", c), "onesf"], writes=[("ps", 1)])
                for c in range(CC):
                    tb = c % 2
                    P.add("act", ACTV(tmpf[:, tb, 0:T], bufA[:, c, 0:T], AF.Square), reads=[("y", c)], writes=[("tmpf", tb)])
                    P.add("pe", MMS([(ps[2][:, 0:T], onesf[:], tmpf[:, tb, 0:T], c == 0, c == CC - 1)]),
                          reads=[("tmpf", tb), "onesf"], writes=[("ps", 2)])
                P.add("dve", TS(rep[:, 2, 0:T], ps[1][:, 0:T], 1.0 / 1024, None, ALU.mult), reads=[("ps", 1)], writes=[("rep", 2)])
                P.add("dve", TT(rep[:, 3, 0:T], rep[:, 2, 0:T], rep[:, 2, 0:T], ALU.mult), reads=[("rep", 2)], writes=[("rep", 3)])
                P.add("dve", STT(rep[:, 3, 0:T], ps[2][:, 0:T], 1.0 / 1024, rep[:, 3, 0:T], ALU.mult, ALU.subtract),
                      reads=[("ps", 2), ("rep", 3)], writes=[("rep", 3)])
                P.add("dve", TS(rep[:, 3, 0:T], rep[:, 3, 0:T], EPS, None, ALU.add), reads=[("rep", 3)], writes=[("rep", 3)])
                P.add("act", ACTV(rep[:, 3, 0:T], rep[:, 3, 0:T], AF.Sqrt), reads=[("rep", 3)], writes=[("rep", 3)])
                P.add("dve", RCP(rep[:, 4, 0:T], rep[:, 3, 0:T]), reads=[("rep", 3)], writes=[("rep", 4)])
                for c in range(CC):
                    yv = bufA[:, c, 0:T]
                    P.add("dve", TT(yv, yv, rep[:, 2, 0:T], ALU.subtract), reads=[("y", c), ("rep", 2)], writes=[("y", c)])
                    P.add("dve", TT(yv, yv, rep[:, 4, 0:T], ALU.mult), reads=[("y", c), ("rep", 4)], writes=[("y", c)])
                    P.add("act", ACTV(yv, yv, AF.Silu, scale=prm[:, pc + 304 + c:pc + 305 + c], bias=prm[:, pc + 312 + c:pc + 313 + c]),
                          reads=[("y", c), "prm"], writes=[("y", c)])
                for c in range(CC):
                    tb = c % 2
                    P.add("act", ACTV(tmpf[:, tb, 0:T], bufA[:, c, 0:T], AF.Square), reads=[("y", c)], writes=[("tmpf", tb)])
                    P.add("pe", MMS([(ps[3][:, 0:T], onesf[:], tmpf[:, tb, 0:T], c == 0, c == CC - 1)]),
                          reads=[("tmpf", tb), "onesf"], writes=[("ps", 3)])
                P.add("dve", TS(rep[:, 5, 0:T], ps[3][:, 0:T], 1.0 / 1024, EPS, ALU.mult, ALU.add), reads=[("ps", 3)], writes=[("rep", 5)])
                P.add("act", ACTV(rep[:, 5, 0:T], rep[:, 5, 0:T], AF.Sqrt), reads=[("rep", 5)], writes=[("rep", 5)])
                P.add("dve", RCP(rep[:, 5, 0:T], rep[:, 5, 0:T]), reads=[("rep", 5)], writes=[("rep", 5)])
                for c in range(CC):
                    P.add("dve", STT(hT[:, 8 + c, 0:T], bufA[:, c, 0:T], prm[:, pc + 328 + c:pc + 329 + c], rep[:, 5, 0:T], ALU.mult, ALU.mult),
                          reads=[("y", c), ("rep", 5), "prm"], writes=[("hT", 8 + c)])
                for cg in range(4):
                    base = 4 * (cg % 2)
                    bw = W.next(wout[l][:, cg * 512:(cg + 1) * 512], 16)
                    for (ts, pr) in sl:
                        P.add("pe", MMS([(ps[base + ts][:pr, :], hT[:, c, ts * 128:ts * 128 + pr], WB[:, bw, c, :], c == 0, c == DC - 1)
                                         for c in range(DC)]), reads=[("wb", bw)] + hreads, writes=[("ps", base + ts)])
                        xs_ = x_tm[:pr, ts, cg * 512:(cg + 1) * 512]
                        P.add("dve", TT(xs_, ps[base + ts][:pr, :], xs_, ALU.add), reads=[("ps", base + ts), ("x", ts)], writes=[("x", ts)])

            def final_norm(kind, t0, T):
                sl = subs(T)
                P.add("dve", MSET(stat[:, 0:4], 0.0), writes=["stat"])
                for (ts, pr) in sl:
                    P.add("act", ACTV(xh[:pr, ts, :], x_tm[:pr, ts, :], AF.Square, accum_out=stat[:pr, ts:ts + 1]),
                          reads=[("x", ts), "stat"], writes=[("xh", ts), "stat"])
                P.add("dve", TS(stat[:, 4:8], stat[:, 0:4], 1.0 / D, EPS, ALU.mult, ALU.add), reads=["stat"], writes=["stat"])
                P.add("act", ACTV(stat[:, 8:12], stat[:, 4:8], AF.Sqrt), reads=["stat"], writes=["stat"])
                P.add("dve", RCP(stat[:, 12:16], stat[:, 8:12]), reads=["stat"], writes=["stat"])
                for (ts, pr) in sl:
                    P.add("dve", STT(x_tm[:pr, ts, :], x_tm[:pr, ts, :], stat[:pr, 12 + ts:13 + ts], gfin[:pr, :], ALU.mult, ALU.mult),
                          reads=[("x", ts), "stat", "gfin"], writes=[("x", ts)])
                dst = yp[t0:t0 + T, :] if kind == "p" else ys
                store_x(dst, T, "xst")

            for l in range(L):
                pc = l * PL
                for (kind, t0, T, r0) in groups:
                    if l == 0:
                        load_x(xp[t0:t0 + T, :] if kind == "p" else xs, T)
                    else:
                        load_x(xc_d[r0:r0 + T, :], T)
                    norm_hT(T, pc + 0)
                    P.barrier()
                    ffn(T, l, 0)
                    store_x(x1_d[r0:r0 + T, :], T, "xst")
                    if KSTOP <= 3:
                        raise _Stop()
                    norm_hT(T, pc + 16)
                    P.barrier()
                    if KSTOP <= 3.2:
                        raise _Stop()
                    win_stage(l, kind, t0, T, r0)
                    P.barrier()
                    if KSTOP <= 3.6 or (KSTOP <= 3.8 and kind == "p" and t0 + T == S):
                        raise _Stop()
                if KSTOP <= 4:
                    raise _Stop()
                attention(l)
                P.barrier()
                if KSTOP <= 5:
                    raise _Stop()
                for (kind, t0, T, r0) in groups:
                    mixer_tail(l, kind, t0, T, r0)
                    P.barrier()
                    norm_hT(T, pc + 32)
                    P.barrier()
                    ffn(T, l, 1)
                    if l < L - 1:
                        store_x(xc_d[r0:r0 + T, :], T, "xst")
                    else:
                        final_norm(kind, t0, T)
                    P.barrier()

        class WStream:
            def __init__(self, P, seq):
                self.P = P
                self.dry = seq is None
                self.seq = [] if seq is None else seq
                self.n = 0
                self.issued = 0

            def next(self, blk, kc):
                if self.dry:
                    self.seq.append((blk, kc))
                    return 0
                n = self.n
                self.n += 1
                lim = min(len(self.seq), n + NWB - 1)
                while self.issued < lim:
                    m = self.issued
                    b_, kc_ = self.seq[m]
                    bi = m % NWB
                    self.P.add("pool", DMA(WB[:, bi, 0:kc_, :], b_.rearrange("(k p) n -> p k n", p=128)),
                               writes=[("wb", bi)], dma_key=("wb", bi))
                    self.issued += 1
                return n % NWB

        dry = WStream(DryProg(), None)
        stages(dry.P, dry)
        P = Prog()
        W = WStream(P, dry.seq)
        stages(P, W)
        P.emit(nc, st)
    return nc


def _pack_params(inp, L):
    prm = np.zeros((128, NPAR), np.float32)

    def fm(v, nchunk):
        return np.ascontiguousarray(v.reshape(nchunk, 128).T)

    for l in range(L):
        pc = l * PL
        prm[:, pc + 0:pc + 16] = fm(inp["norm_ffn1"][l], 16)
        prm[:, pc + 16:pc + 32] = fm(inp["norm_mix"][l], 16)
        prm[:, pc + 32:pc + 48] = fm(inp["norm_ffn2"][l], 16)
        dw = inp["dw_weight"][l]
        prm[:, pc + 48:pc + 296] = dw.reshape(31, 8, 128).transpose(2, 1, 0).reshape(128, 248)
        prm[:, pc + 296:pc + 304] = fm(inp["dw_bias"][l], 8)
        prm[:, pc + 304:pc + 312] = fm(inp["conv_ln_gain"][l], 8)
        prm[:, pc + 312:pc + 320] = fm(inp["conv_ln_bias"][l], 8)
        prm[:, pc + 320:pc + 328] = fm(inp["norm_attn_out"][l], 8)
        prm[:, pc + 328:pc + 336] = fm(inp["norm_conv_out"][l], 8)
    return prm


_NC_CACHE = {}


def kernel(**inp):
    inp = {k: np.asarray(v) for k, v in inp.items()}
    B, S, _ = inp["x_prompt"].shape
    BS = inp["x_sample"].shape[0]
    L = inp["w_in"].shape[0]
    PAST = inp["cache_k"].shape[2]
    import os
    ncores = int(os.environ.get('KCORES', '8'))
    key = (S, PAST, L)
    if key not in _NC_CACHE:
        _NC_CACHE[key] = build(S, PAST, L)
    nc = _NC_CACHE[key]
    prm = _pack_params(inp, L)
    gfin = np.ascontiguousarray(np.broadcast_to(inp["norm_final"][None, :], (128, D))).astype(np.float32)
    pcores = [0, 1, 4, 5][:B] if ncores == 8 else list(range(min(B, ncores)))
    zero_xp = np.zeros((S, D), np.float32)
    in_maps = []
    for c in range(ncores):
        sbi = c % BS
        xp_c = np.ascontiguousarray(inp["x_prompt"][pcores.index(c)]) if c in pcores else zero_xp
        in_maps.append({
            "xp": xp_c,
            "xs": np.ascontiguousarray(inp["x_sample"][sbi]),
            "ck": np.ascontiguousarray(inp["cache_k"][:, sbi].reshape(L, PAST, 1024)),
            "cv": np.ascontiguousarray(inp["cache_v"][:, sbi].reshape(L, PAST, 1024)),
            "sc": np.ascontiguousarray(inp["state_conv"][:, sbi]),
            "wg1": inp["ffn1_gate"], "wu1": inp["ffn1_up"], "wd1": inp["ffn1_down"],
            "wg2": inp["ffn2_gate"], "wu2": inp["ffn2_up"], "wd2": inp["ffn2_down"],
            "win": inp["w_in"], "wout": inp["w_out"],
            "prm": prm, "gfin": gfin,
        })
    res = run_bass_kernel_spmd(nc, in_maps, core_ids=list(range(ncores)))
    R = list(res.results)
    while len(R) < 8:
        R.append(R[0])
    pc_ = [pcores[b] if b < len(pcores) else 0 for b in range(B)]
    y_p = np.stack([R[pc_[b]]["yp"] for b in range(B)])
    y_s = np.stack([R[b]["ys"] for b in range(BS)])
    nkp = np.stack([R[pc_[b]]["nkp"] for b in range(B)], axis=1).reshape(L, B, S, NH, 128)
    nvp = np.stack([R[pc_[b]]["nvp"] for b in range(B)], axis=1).reshape(L, B, S, NH, 128)
    ncp = np.stack([R[pc_[b]]["ncp"] for b in range(B)], axis=1)
    nks = np.stack([R[b]["nks"] for b in range(BS)], axis=1).reshape(L, BS, 64, NH, 128)
    nvs = np.stack([R[b]["nvs"] for b in range(BS)], axis=1).reshape(L, BS, 64, NH, 128)
    ncs = np.stack([R[b]["ncs"] for b in range(BS)], axis=1)
    return (y_p, y_s, nkp, nvp, ncp, nks, nvs, ncs)
```

```python
import math
from contextlib import ExitStack

import numpy as np
import concourse.bass as bass
import concourse.mybir as mybir
from concourse.bass_utils import run_bass_kernel_spmd

F32 = mybir.dt.float32
BF16 = mybir.dt.bfloat16
AF = mybir.ActivationFunctionType
ALU = mybir.AluOpType

D = 2048
DC = 16
FF = 5632
FC = 44
NH = 8
CC = 8
INC = 5120
EPS = 1e-6
QS = 1.0 / math.sqrt(128.0)
NEG = -30000.0
PL = 336
NPAR = 2 * PL + 16
NWB = 4

SAME_ENGINE_SYNC = True


class Op:
    __slots__ = ("eng", "fn", "deps", "sig", "dma_key", "dma_cnt", "needs_sig")


class Prog:
    ENG = ["pe", "act", "dve", "pool", "sp"]

    def __init__(self):
        self.ops = {e: [] for e in self.ENG}
        self.lw = {}
        self.rd = {}
        self.dma_cnt = {}
        self.dma_last = {}

    def add(self, eng, fn, reads=(), writes=(), dma_key=None, extra_deps=()):
        op = Op()
        op.eng = eng
        op.fn = fn
        op.dma_key = dma_key
        op.needs_sig = False
        op.sig = 0
        op.dma_cnt = 0
        deps = set(extra_deps)
        lw = self.lw
        rd = self.rd
        for r in reads:
            w = lw.get(r)
            if w is not None:
                deps.add(w)
        for w_ in writes:
            w = lw.get(w_)
            if w is not None:
                deps.add(w)
            for x in rd.get(w_, ()):
                deps.add(x)
        if dma_key is not None:
            c = self.dma_cnt.get(dma_key, 0) + 1
            self.dma_cnt[dma_key] = c
            op.dma_cnt = c
            self.dma_last[dma_key] = op
        for r in reads:
            rd.setdefault(r, []).append(op)
        for w_ in writes:
            lw[w_] = op
            rd[w_] = []
        op.deps = deps
        self.ops[eng].append(op)
        return op

    def barrier(self, engines=("pe", "act", "dve", "sp")):
        lasts = []
        for e in engines:
            for op in reversed(self.ops[e]):
                if op.fn is not None and op.dma_key is None:
                    lasts.append(op)
                    break
        deps = set(lasts)
        for k, v in self.dma_last.items():
            if not (isinstance(k, tuple) and k[0] == "wb"):
                deps.add(v)
        for e in engines:
            self.add(e, None, extra_deps=list(deps))

    def emit(self, nc, stack):
        ops = self.ops
        for e in self.ENG:
            for op in ops[e]:
                for d in op.deps:
                    if d.dma_key is not None:
                        continue
                    if d.eng == op.eng and (op.eng == "pe" or (not SAME_ENGINE_SYNC and op.eng != "pool")):
                        continue
                    d.needs_sig = True
        for e in self.ENG:
            c = 0
            for op in ops[e]:
                if op.dma_key is None and op.needs_sig:
                    c += 1
                    op.sig = c
        sems = {}
        for e in self.ENG:
            sems[("eng", e)] = stack.enter_context(nc.semaphore("s_" + e))
        for i, k in enumerate(self.dma_cnt):
            sems[("dma", k)] = stack.enter_context(nc.semaphore("d%d" % i))
        block = stack.enter_context(nc.Block())

        def run(ename, eng):
            known = {}
            for op in ops[ename]:
                waits = {}
                for d in op.deps:
                    if d.dma_key is not None:
                        key = ("dma", d.dma_key)
                        val = 16 * d.dma_cnt
                    else:
                        if d.eng == ename and (ename == "pe" or (not SAME_ENGINE_SYNC and ename != "pool")):
                            continue
                        key = ("eng", d.eng)
                        val = d.sig
                    if waits.get(key, 0) < val:
                        waits[key] = val
                for key, val in waits.items():
                    if known.get(key, 0) >= val:
                        continue
                    eng.wait_ge(sems[key], val)
                    known[key] = val
                if op.fn is None:
                    continue
                inst = op.fn(eng)
                if op.dma_key is not None:
                    inst.then_inc(sems[("dma", op.dma_key)], 16)
                elif op.needs_sig:
                    inst.then_inc(sems[("eng", ename)], 1)

        @block.tensor
        def _(eng):
            run("pe", eng)

        @block.scalar
        def _(eng):
            run("act", eng)

        @block.vector
        def _(eng):
            run("dve", eng)

        @block.gpsimd
        def _(eng):
            run("pool", eng)

        @block.sync
        def _(eng):
            run("sp", eng)


class DryProg:
    def add(self, *a, **k):
        return None

    def barrier(self, *a, **k):
        return None


def MMS(lst):
    def f(e):
        r = None
        for (o, l, rh, s, t) in lst:
            r = e.matmul(o, lhsT=l, rhs=rh, start=s, stop=t)
        return r
    return f


def TRS(lst):
    def f(e):
        r = None
        for (o, i, idn) in lst:
            r = e.transpose(out=o, in_=i, identity=idn)
        return r
    return f


def ACTV(out, in_, func, **kw):
    return lambda e: e.activation(out=out, in_=in_, func=func, **kw)


def TT(out, in0, in1, op):
    return lambda e: e.tensor_tensor(out=out, in0=in0, in1=in1, op=op)


def TS(out, in0, s1, s2, op0, op1=None):
    if op1 is None:
        return lambda e: e.tensor_scalar(out=out, in0=in0, scalar1=s1, scalar2=None, op0=op0)
    return lambda e: e.tensor_scalar(out=out, in0=in0, scalar1=s1, scalar2=s2, op0=op0, op1=op1)


def STT(out, in0, scalar, in1, op0, op1):
    return lambda e: e.scalar_tensor_tensor(out=out, in0=in0, scalar=scalar, in1=in1, op0=op0, op1=op1)


def CP(out, in_):
    return lambda e: e.tensor_copy(out=out, in_=in_)


def RCP(out, in_):
    return lambda e: e.reciprocal(out=out, in_=in_)


def MSET(ap, v):
    return lambda e: e.memset(ap, v)


def DMA(out, in_):
    return lambda e: e.dma_start(out=out, in_=in_)


def subs(T):
    return [(ts, min(128, T - ts * 128)) for ts in range((T + 127) // 128)]


def build(S, PAST, L):
    NG = S // 512
    NKBS = PAST // 128
    SK = PAST + 64
    nc = bass.Bass("TRN2", target_bir_lowering=False)

    def din(name, shape, dt=F32):
        return nc.dram_tensor(name, list(shape), dt, kind="ExternalInput").ap()

    def dout(name, shape, dt=F32):
        return nc.dram_tensor(name, list(shape), dt, kind="ExternalOutput").ap()

    def dscr(name, shape, dt=F32):
        return nc.dram_tensor(name, list(shape), dt, kind="Internal").ap()

    xp = din("xp", [S, D])
    xs = din("xs", [64, D])
    ck = din("ck", [L, PAST, 1024])
    cv = din("cv", [L, PAST, 1024])
    sc = din("sc", [L, 30, 1024])
    wg = [din("wg1", [L, D, FF]), din("wg2", [L, D, FF])]
    wu = [din("wu1", [L, D, FF]), din("wu2", [L, D, FF])]
    wd = [din("wd1", [L, FF, D]), din("wd2", [L, FF, D])]
    win = din("win", [L, D, INC])
    wout = din("wout", [L, D, D])
    prm_d = din("prm", [128, NPAR])
    gfin_d = din("gfin", [128, D])

    yp = dout("yp", [S, D])
    ys = dout("ys", [64, D])
    nkp = dout("nkp", [L, S, 1024])
    nvp = dout("nvp", [L, S, 1024])
    ncp = dout("ncp", [L, 30, 1024])
    nks = dout("nks", [L, 64, 1024])
    nvs = dout("nvs", [L, 64, 1024])
    ncs = dout("ncs", [L, 30, 1024])

    x1_d = dscr("x1_d", [S + 64, D])
    xc_d = dscr("xc_d", [S + 64, D])
    qT_d = dscr("qT_d", [NH, 128, S + 64], BF16)
    kT_d = dscr("kT_d", [NH, 128, S], BF16)
    kTs_d = dscr("kTs_d", [L, NH, 128, SK], BF16)
    vS_d = dscr("vS_d", [S, 1024], BF16)
    vSs_d = dscr("vSs_d", [L, SK, 1024], BF16)
    aT_d = dscr("aT_d", [CC, 128, 30 + S])
    aTs_d = dscr("aTs_d", [L, CC, 128, 94])
    atT_d = dscr("atT_d", [NH, 128, S + 64])

    groups = [("p", g * 512, 512, g * 512) for g in range(NG)] + [("s", 0, 64, S)]

    with ExitStack() as st:
        def sb(name, shape, dt):
            return st.enter_context(nc.sbuf_tensor("sb_" + name, list(shape), dt))

        x_tm = sb("x_tm", [128, 4, D], F32)
        arena = sb("arena", [128, 22528], BF16)
        xh = sb("xh", [128, 4, D], BF16)
        hT = sb("hT", [128, DC, 512], BF16)
        WB = sb("WB", [128, NWB, 16, 512], BF16)
        tmpf = sb("tmpf", [128, 2, 512], F32)
        stat = sb("stat", [128, 16], F32)
        rep = sb("rep", [128, 6, 512], F32)
        prm = sb("prm", [128, NPAR], F32)
        gfin = sb("gfin", [128, D], F32)
        identb = sb("identb", [128, 128], BF16)
        identf = sb("identf", [128, 128], F32)
        onesf = sb("onesf", [128, 128], F32)
        ntri = sb("ntri", [128, 128], BF16)
        nones = sb("nones", [128, 128], BF16)
        masks = sb("masks", [128, 4, 512], BF16)
        zf = sb("zf", [128, 512], F32)
        ps = [st.enter_context(nc.psum_tensor("ps%d" % i, [128, 512], F32)) for i in range(8)]

        def av(off_bytes, shape, dt):
            n = 1
            for s_ in shape[1:]:
                n *= s_
            nb = n * (4 if dt == F32 else 2)
            v = arena[:, off_bytes // 2:(off_bytes + nb) // 2]
            if dt == F32:
                v = v.bitcast(F32)
            if len(shape) == 3:
                v = v.rearrange("p (a b) -> p a b", a=shape[1])
            return v

        def xv(off_bytes, shape, dt):
            n = 1
            for s_ in shape[1:]:
                n *= s_
            flat = x_tm[:].rearrange("p a b -> p (a b)")
            nb = n * (4 if dt == F32 else 2)
            v = flat[:, off_bytes // 4:(off_bytes + nb) // 4]
            if dt == BF16:
                v = v.bitcast(BF16)
            if len(shape) == 3:
                v = v.rearrange("p (a b) -> p a b", a=shape[1])
            return v

        KB = 1024
        actT = av(0, [128, FC, 512], BF16)
        qTb = av(0, [128, NH, 512], BF16)
        kTb = av(8 * KB, [128, NH, 512], BF16)
        v_bf = av(16 * KB, [128, 4, 1024], BF16)
        aTb = av(24 * KB, [128, CC, 512], F32)
        kv_k = xv(0, [128, 4, 1024], F32)
        kv_v = xv(16 * KB, [128, 4, 1024], F32)
        bufA = av(0, [128, 8, 512], F32)
        abuf = [av(16 * KB, [128, 544], F32), av(16 * KB + 2176, [128, 544], F32)]
        nct = av(40 * KB, [128, 1024], F32)
        sct = av(28 * KB, [128, 1024], F32)
        scT = av(32 * KB, [128, CC, 32], F32)
        KT = [av(0, [128, 8192], BF16), xv(0, [128, 8192], BF16)]
        Vb = [av(16 * KB, [128, 64, 128], BF16), xv(16 * KB, [128, 64, 128], BF16)]
        o2 = 33 * KB
        qt = [av(o2, [128, 512], BF16), av(o2 + 1 * KB, [128, 512], BF16)]
        ebuf = [av(o2 + 2 * KB, [128, 512], F32), av(o2 + 4 * KB, [128, 512], F32)]
        spb = [av(o2 + 6 * KB, [128, 512], BF16), av(o2 + 7 * KB, [128, 512], BF16)]
        pb = [av(o2 + 8 * KB, [128, 512], BF16), av(o2 + 9 * KB, [128, 512], BF16)]
        rbb = [hT[:, 0, :], hT[:, 1, :]]
        Rf = xh[:, 0, 0:1024].bitcast(F32)
        oTs = [xh[:, 1, 0:1024].bitcast(F32), xh[:, 2, 0:1024].bitcast(F32)]
        ckb = [hT[:, 4:6, :].rearrange("p a b -> p (a b)"), hT[:, 6:8, :].rearrange("p a b -> p (a b)")]
        kTc = [hT[:, 8:10, :].rearrange("p a b -> p (a b)").rearrange("p (h s) -> p h s", h=NH),
               hT[:, 10:12, :].rearrange("p a b -> p (a b)").rearrange("p (h s) -> p h s", h=NH)]

        import os
        KSTOP = float(os.environ.get("KSTOP", "99"))

        class _Stop(Exception):
            pass

        def stages(P, W):
            try:
                _stages(P, W)
            except _Stop:
                pass
            P.barrier(engines=("pe", "act", "dve", "pool", "sp"))

        def _stages(P, W):
            P.add("sp", DMA(prm[:], prm_d), writes=["prm"], dma_key="prm")
            P.add("sp", DMA(gfin[:], gfin_d), writes=["gfin"], dma_key="gfin")
            P.add("pool", MSET(zf[:], 0.0), writes=["zf"])
            P.add("pool", MSET(identf[:], 0.0), writes=["identf"])
            P.add("pool", lambda e: e.affine_select(out=identf[:], in_=identf[:], pattern=[[-1, 128]],
                                                   compare_op=ALU.not_equal, fill=1.0, base=0,
                                                   channel_multiplier=1), reads=["identf"], writes=["identf"])
            P.add("dve", CP(identb[:], identf[:]), reads=["identf"], writes=["identb"])
            P.add("pool", MSET(onesf[:], 1.0), writes=["onesf"])
            P.add("pool", MSET(nones[:], -1.0), writes=["nones"])
            P.add("pool", lambda e: e.affine_select(out=ntri[:], in_=nones[:], pattern=[[-1, 128]],
                                                   compare_op=ALU.is_ge, fill=0.0, base=0,
                                                   channel_multiplier=1), reads=["nones"], writes=["ntri"])
            for r in range(4):
                P.add("pool", (lambda r_: lambda e: e.affine_select(
                    out=masks[:, r_, :], in_=zf[:], pattern=[[1, 512]], compare_op=ALU.is_ge, fill=NEG,
                    base=-128 * r_ - 1, channel_multiplier=-1))(r), reads=["zf"], writes=["masks"])
            P.add("sp", DMA(aT_d.rearrange("c p t -> p c t")[:, :, 0:30],
                            zf[:, 0:240].rearrange("p (c t) -> p c t", c=CC)), reads=["zf"], dma_key="zh")
            P.barrier(engines=("pe", "act", "dve", "pool", "sp"))
            if KSTOP <= 1:
                raise _Stop()

            evac_rr = [0]

            def evac_copy(out, in_, reads, writes, scale=None):
                evac_rr[0] ^= 1
                if evac_rr[0]:
                    if scale is None:
                        P.add("act", ACTV(out, in_, AF.Copy), reads=reads, writes=writes)
                    else:
                        P.add("act", ACTV(out, in_, AF.Copy, scale=scale), reads=reads, writes=writes)
                else:
                    if scale is None:
                        P.add("dve", CP(out, in_), reads=reads, writes=writes)
                    else:
                        P.add("dve", TS(out, in_, scale, None, ALU.mult), reads=reads, writes=writes)

            for l in range(L):
                for blk in range(NKBS):
                    b2 = blk % 2
                    P.add("pool", DMA(ckb[b2], ck[l, blk * 128:(blk + 1) * 128, :]), writes=[("ckb", b2)],
                          dma_key=("ckb", b2))
                    for half in range(2):
                        bank = ps[(2 * blk + half) % 4]
                        pv = bank[:].bitcast(BF16)
                        P.add("pe", TRS([(pv[:, i * 128:(i + 1) * 128],
                                          ckb[b2][:, (half * 4 + i) * 128:(half * 4 + i + 1) * 128], identb[:])
                                         for i in range(4)]),
                              reads=[("ckb", b2), "identb"], writes=[("ps", (2 * blk + half) % 4)])
                        evac_copy(kTc[b2][:, half * 4:half * 4 + 4, :],
                                  pv[:, 0:512].rearrange("p (h s) -> p h s", h=4),
                                  [("ps", (2 * blk + half) % 4)], [("kTc", b2)])
                    P.add("sp", DMA(kTs_d[l].rearrange("h p s -> p h s")[:, :, blk * 128:(blk + 1) * 128], kTc[b2]),
                          reads=[("kTc", b2)], dma_key=("kTc", b2))
                    P.add("pool", DMA(xh[:, b2, 0:1024], cv[l, blk * 128:(blk + 1) * 128, :]), writes=[("cvb", b2)],
                          dma_key=("cvb", b2))
                    P.add("sp", DMA(vSs_d[l, blk * 128:(blk + 1) * 128, :], xh[:, b2, 0:1024]), reads=[("cvb", b2)],
                          dma_key=("cvs", b2))
                P.add("sp", DMA(sct[0:30, :], sc[l]), writes=["sct"], dma_key="sct")
                for c in range(CC):
                    P.add("pe", TRS([(ps[4 + c % 2][:, 0:30], sct[0:30, c * 128:(c + 1) * 128], identf[0:30, 0:30])]),
                          reads=["sct", "identf"], writes=[("ps", 4 + c % 2)])
                    evac_copy(scT[:, c, 0:30], ps[4 + c % 2][:, 0:30], [("ps", 4 + c % 2)], ["scT"])
                P.add("sp", DMA(aTs_d[l].rearrange("c p t -> p c t")[:, :, 0:30], scT[:, :, 0:30]), reads=["scT"],
                      dma_key="scT")
            P.barrier()
            if KSTOP <= 2:
                raise _Stop()

            def norm_hT(T, gcol):
                sl = subs(T)
                P.add("dve", MSET(stat[:, 0:4], 0.0), writes=["stat"])
                for (ts, pr) in sl:
                    P.add("act", ACTV(xh[:pr, ts, :], x_tm[:pr, ts, :], AF.Square, accum_out=stat[:pr, ts:ts + 1]),
                          reads=[("x", ts), "stat"], writes=[("xh", ts), "stat"])
                P.add("dve", TS(stat[:, 4:8], stat[:, 0:4], 1.0 / D, EPS, ALU.mult, ALU.add), reads=["stat"], writes=["stat"])
                P.add("act", ACTV(stat[:, 8:12], stat[:, 4:8], AF.Sqrt), reads=["stat"], writes=["stat"])
                P.add("dve", RCP(stat[:, 12:16], stat[:, 8:12]), reads=["stat"], writes=["stat"])
                for (ts, pr) in sl:
                    if ts % 2 == 0:
                        P.add("act", ACTV(xh[:pr, ts, :], x_tm[:pr, ts, :], AF.Copy, scale=stat[:pr, 12 + ts:13 + ts]),
                              reads=[("x", ts), "stat"], writes=[("xh", ts)])
                    else:
                        P.add("dve", TS(xh[:pr, ts, :], x_tm[:pr, ts, :], stat[:pr, 12 + ts:13 + ts], None, ALU.mult),
                              reads=[("x", ts), "stat"], writes=[("xh", ts)])
                for c in range(DC):
                    bk = 4 + c % 2
                    pv = ps[bk][:].bitcast(BF16)
                    P.add("pe", TRS([(pv[:, ts * 128:ts * 128 + pr], xh[:pr, ts, c * 128:(c + 1) * 128], identb[:pr, :pr])
                                     for (ts, pr) in sl]),
                          reads=[("xh", ts) for (ts, pr) in sl] + ["identb"], writes=[("ps", bk)])
                    evac_copy(hT[:, c, 0:T], pv[:, 0:T], [("ps", bk), "prm"], [("hT", c)],
                              scale=prm[:, gcol + c:gcol + c + 1])

            def ffn(T, l, which):
                sl = subs(T)
                Wg, Wu, Wd = wg[which][l], wu[which][l], wd[which][l]
                hreads = [("hT", c) for c in range(DC)]
                for fg in range(11):
                    bg = W.next(Wg[:, fg * 512:(fg + 1) * 512], 16)
                    bu = W.next(Wu[:, fg * 512:(fg + 1) * 512], 16)
                    for fi in range(4):
                        f = fg * 4 + fi
                        gb = f % 2
                        ub = 2 + f % 2
                        P.add("pe", MMS([(ps[gb][:, 0:T], WB[:, bg, c, fi * 128:(fi + 1) * 128], hT[:, c, 0:T], c == 0, c == DC - 1)
                                         for c in range(DC)]),
                              reads=[("wb", bg)] + hreads, writes=[("ps", gb)])
                        P.add("pe", MMS([(ps[ub][:, 0:T], WB[:, bu, c, fi * 128:(fi + 1) * 128], hT[:, c, 0:T], c == 0, c == DC - 1)
                                         for c in range(DC)]),
                              reads=[("wb", bu)] + hreads, writes=[("ps", ub)])
                        P.add("act", ACTV(tmpf[:, gb, 0:T], ps[gb][:, 0:T], AF.Silu), reads=[("ps", gb)], writes=[("tmpf", gb)])
                        P.add("dve", TT(actT[:, f, 0:T], ps[ub][:, 0:T], tmpf[:, gb, 0:T], ALU.mult),
                              reads=[("ps", ub), ("tmpf", gb)], writes=[("actT", f)])
                for cg in range(4):
                    base = 4 * (cg % 2)
                    for blk in range(3):
                        kc = 16 if blk < 2 else 12
                        bw = W.next(Wd[blk * 2048:blk * 2048 + kc * 128, cg * 512:(cg + 1) * 512], kc)
                        lst = []
                        for fc in range(kc):
                            f = blk * 16 + fc
                            for (ts, pr) in sl:
                                lst.append((ps[base + ts][:pr, :], actT[:, f, ts * 128:ts * 128 + pr], WB[:, bw, fc, :],
                                            f == 0, f == FC - 1))
                        P.add("pe", MMS(lst), reads=[("wb", bw)] + [("actT", blk * 16 + fc) for fc in range(kc)],
                              writes=[("ps", base + ts) for (ts, pr) in sl])
                    for (ts, pr) in sl:
                        xs_ = x_tm[:pr, ts, cg * 512:(cg + 1) * 512]
                        P.add("dve", STT(xs_, ps[base + ts][:pr, :], 0.5, xs_, ALU.mult, ALU.add),
                              reads=[("ps", base + ts), ("x", ts)], writes=[("x", ts)])

            def load_x(src_rows, T):
                for (ts, pr) in subs(T):
                    P.add("sp", DMA(x_tm[:pr, ts, :], src_rows[ts * 128:ts * 128 + pr, :]), writes=[("x", ts)],
                          dma_key=("xl", ts))

            def store_x(dst_rows, T, key):
                for (ts, pr) in subs(T):
                    P.add("sp", DMA(dst_rows[ts * 128:ts * 128 + pr, :], x_tm[:pr, ts, :]), reads=[("x", ts)],
                          dma_key=(key, ts))

            hreads = [("hT", c) for c in range(DC)]

            def win_stage(l, kind, t0, T, r0):
                sl = subs(T)
                Wi = win[l]
                bankc = [0]

                def nb():
                    bankc[0] = (bankc[0] + 1) % 8
                    return bankc[0]

                def fm_block(bw, dst, scale, h0):
                    for hi in range(4):
                        bk = nb()
                        P.add("pe", MMS([(ps[bk][:, 0:T], WB[:, bw, c, hi * 128:(hi + 1) * 128], hT[:, c, 0:T], c == 0, c == DC - 1)
                                         for c in range(DC)]), reads=[("wb", bw)] + hreads, writes=[("ps", bk)])
                        evac_copy(dst[:, h0 + hi, 0:T], ps[bk][:, 0:T], [("ps", bk)], [("stg", id(dst))], scale=scale)

                def tm_block(bw, j, dst32, dstb):
                    for (ts, pr) in sl:
                        bk = nb()
                        P.add("pe", MMS([(ps[bk][:pr, :], hT[:, c, ts * 128:ts * 128 + pr], WB[:, bw, c, :], c == 0, c == DC - 1)
                                         for c in range(DC)]), reads=[("wb", bw)] + hreads, writes=[("ps", bk)])
                        P.add("act", ACTV(dst32[:pr, ts, j * 512:(j + 1) * 512], ps[bk][:pr, :], AF.Copy),
                              reads=[("ps", bk)], writes=[("stg", id(dst32))])
                        if dstb is not None:
                            P.add("dve", CP(dstb[:pr, ts, j * 512:(j + 1) * 512], dst32[:pr, ts, j * 512:(j + 1) * 512]),
                                  reads=[("stg", id(dst32))], writes=[("stg", id(dstb))])

                for j in range(2):
                    bw = W.next(Wi[:, j * 512:(j + 1) * 512], 16)
                    fm_block(bw, qTb, QS, j * 4)
                P.add("sp", DMA(qT_d.rearrange("h p t -> p h t")[:, :, r0:r0 + T], qTb[:, :, 0:T]),
                      reads=[("stg", id(qTb))], dma_key="qTb")
                for j in range(2):
                    bw = W.next(Wi[:, 1024 + j * 512:1024 + (j + 1) * 512], 16)
                    fm_block(bw, kTb, None, j * 4)
                    tm_block(bw, j, kv_k, None)
                if kind == "p":
                    P.add("sp", DMA(kT_d.rearrange("h p t -> p h t")[:, :, t0:t0 + T], kTb[:, :, 0:T]),
                          reads=[("stg", id(kTb))], dma_key="kTb")
                    P.add("sp", DMA(nkp[l, t0:t0 + T, :].rearrange("(a p) n -> p a n", p=128), kv_k[:, :, :]),
                          reads=[("stg", id(kv_k))], dma_key="kvk")
                else:
                    P.add("sp", DMA(kTs_d[l].rearrange("h p t -> p h t")[:, :, PAST:PAST + 64], kTb[:, :, 0:64]),
                          reads=[("stg", id(kTb))], dma_key="kTb")
                    P.add("sp", DMA(nks[l], kv_k[0:64, 0, :]), reads=[("stg", id(kv_k))], dma_key="kvk")
                if KSTOP <= 3.4:
                    raise _Stop()
                for j in range(2):
                    bw = W.next(Wi[:, 2048 + j * 512:2048 + (j + 1) * 512], 16)
                    tm_block(bw, j, kv_v, v_bf)
                if kind == "p":
                    P.add("sp", DMA(nvp[l, t0:t0 + T, :].rearrange("(a p) n -> p a n", p=128), kv_v[:, :, :]),
                          reads=[("stg", id(kv_v))], dma_key="kvv")
                    P.add("sp", DMA(vS_d[t0:t0 + T, :].rearrange("(a p) n -> p a n", p=128), v_bf[:, :, :]),
                          reads=[("stg", id(v_bf))], dma_key="vbf")
                else:
                    P.add("sp", DMA(nvs[l], kv_v[0:64, 0, :]), reads=[("stg", id(kv_v))], dma_key="kvv")
                    P.add("sp", DMA(vSs_d[l, PAST:PAST + 64, :], v_bf[0:64, 0, :]), reads=[("stg", id(v_bf))], dma_key="vbf")
                if KSTOP <= 3.5:
                    raise _Stop()
                for j in range(2):
                    bv = W.next(Wi[:, 3072 + j * 512:3072 + (j + 1) * 512], 16)
                    bg = W.next(Wi[:, 4096 + j * 512:4096 + (j + 1) * 512], 16)
                    for ci in range(4):
                        ch = j * 4 + ci
                        bkv = nb()
                        bkg = nb()
                        P.add("pe", MMS([(ps[bkv][:, 0:T], WB[:, bv, c, ci * 128:(ci + 1) * 128], hT[:, c, 0:T], c == 0, c == DC - 1)
                                         for c in range(DC)]), reads=[("wb", bv)] + hreads, writes=[("ps", bkv)])
                        P.add("pe", MMS([(ps[bkg][:, 0:T], WB[:, bg, c, ci * 128:(ci + 1) * 128], hT[:, c, 0:T], c == 0, c == DC - 1)
                                         for c in range(DC)]), reads=[("wb", bg)] + hreads, writes=[("ps", bkg)])
                        tb = ch % 2
                        P.add("act", ACTV(tmpf[:, tb, 0:T], ps[bkg][:, 0:T], AF.Sigmoid), reads=[("ps", bkg)], writes=[("tmpf", tb)])
                        P.add("dve", TT(aTb[:, ch, 0:T], ps[bkv][:, 0:T], tmpf[:, tb, 0:T], ALU.mult),
                              reads=[("ps", bkv), ("tmpf", tb)], writes=[("stg", id(aTb))])
                if kind == "p":
                    P.add("sp", DMA(aT_d.rearrange("c p t -> p c t")[:, :, 30 + t0:30 + t0 + T], aTb[:, :, 0:T]),
                          reads=[("stg", id(aTb))], dma_key="aTb")
                else:
                    P.add("sp", DMA(aTs_d[l].rearrange("c p t -> p c t")[:, :, 30:94], aTb[:, :, 0:64]),
                          reads=[("stg", id(aTb))], dma_key="aTb")
                if kind == "s" or t0 + T == S:
                    for c in range(CC):
                        bk = nb()
                        P.add("pe", TRS([(ps[bk][0:32, 0:128], aTb[:, c, T - 32:T], identf[:])]),
                              reads=[("stg", id(aTb)), "identf"], writes=[("ps", bk)])
                        evac_copy(nct[0:32, c * 128:(c + 1) * 128], ps[bk][0:32, 0:128], [("ps", bk)], ["nct"])
                    dst = ncs[l] if kind == "s" else ncp[l]
                    P.add("sp", DMA(dst, nct[2:32, :]), reads=["nct"], dma_key="nct")

            def attn_head(KTv, Vv, qsrc_list, blocks_of, Tq_of, out_dst_of, kres, vres):
                for qi, qsrc in enumerate(qsrc_list):
                    Tq = Tq_of(qi)
                    blocks = blocks_of(qi)
                    qb = qi % 2
                    P.add("sp", DMA(qt[qb][:, 0:Tq], qsrc), writes=[("qt", qb)], dma_key=("qt", qb))
                    ob = 6 + qi % 2
                    nblk = len(blocks)
                    need_r_zero = any(b[1] < 128 for b in blocks)
                    if need_r_zero:
                        P.add("dve", MSET(Rf[:, 0:Tq], 0.0), writes=["Rf"])

                    def zlist(k, bank, last):
                        col0, rows, vblk, mr = blocks[k]
                        lst = [(ps[bank][:rows, 0:Tq], KTv[:, col0:col0 + rows], qt[qb][:, 0:Tq], True, last and mr is None)]
                        rd = [kres, ("qt", qb)]
                        if mr is not None:
                            lst.append((ps[bank][:rows, 0:Tq], identb[:rows, :rows], masks[:rows, mr, 0:Tq], False, last))
                            rd += ["identb", "masks"]
                        return lst, rd

                    def zmm(k):
                        lst, rd = zlist(k, k % 2, True)
                        P.add("pe", MMS(lst), reads=rd, writes=[("ps", k % 2)])

                    def emm(k):
                        rows_ = blocks[k][1]
                        P.add("act", ACTV(ebuf[k % 2][:rows_, 0:Tq], ps[k % 2][:rows_, 0:Tq], AF.Exp),
                              reads=[("ps", k % 2)], writes=[("e", k % 2)])

                    zmm(0)
                    if nblk > 1:
                        zmm(1)
                    emm(0)
                    for k in range(nblk + 1):
                        if k + 1 < nblk:
                            emm(k + 1)
                        if k + 2 < nblk:
                            zmm(k + 2)
                        if k < nblk:
                            col0, rows, vblk, mr = blocks[k]
                            z2 = 2 + k % 2
                            kb2 = k % 2
                            P.add("act", ACTV(spb[kb2][:rows, 0:Tq], ebuf[kb2][:rows, 0:Tq], AF.Ln, bias=1.0),
                                  reads=[("e", kb2)], writes=[("sp", kb2)])
                            lst, rd = zlist(k, z2, False)
                            lst.append((ps[z2][:rows, 0:Tq], ntri[:rows, :rows], spb[kb2][:rows, 0:Tq], False, k == 0))
                            rd += [("sp", kb2), "ntri"]
                            if k > 0:
                                prow = blocks[k - 1][1] if k == 1 else 128
                                lst.append((ps[z2][:rows, 0:Tq], nones[:prow, :rows], rbb[kb2][:prow, 0:Tq], False, True))
                                rd += [("rb", kb2), "nones"]
                            P.add("pe", MMS(lst), reads=rd, writes=[("ps", z2)])
                            if k + 1 < nblk:
                                if k == 0 and not need_r_zero:
                                    P.add("dve", CP(Rf[:rows, 0:Tq], spb[kb2][:rows, 0:Tq]), reads=[("sp", kb2)], writes=["Rf"])
                                else:
                                    P.add("dve", TT(Rf[:rows, 0:Tq], Rf[:rows, 0:Tq], spb[kb2][:rows, 0:Tq], ALU.add),
                                          reads=[("sp", kb2), "Rf"], writes=["Rf"])
                                nr = rows if k == 0 else 128
                                P.add("dve", CP(rbb[(k + 1) % 2][:nr, 0:Tq], Rf[:nr, 0:Tq]), reads=["Rf"],
                                      writes=[("rb", (k + 1) % 2)])
                        if k >= 1:
                            col0, rows, vblk, mr = blocks[k - 1]
                            z2 = 2 + (k - 1) % 2
                            kb2 = (k - 1) % 2
                            P.add("act", ACTV(pb[kb2][:rows, 0:Tq], ps[z2][:rows, 0:Tq], AF.Exp),
                                  reads=[("ps", z2)], writes=[("pb", kb2)])
                            P.add("pe", MMS([(ps[ob][:, 0:Tq], Vv[:rows, vblk, :], pb[kb2][:rows, 0:Tq], k == 1, k == nblk)]),
                                  reads=[("pb", kb2), vres], writes=[("ps", ob)])
                    osb = qi % 2
                    evac_copy(oTs[osb][:, 0:Tq], ps[ob][:, 0:Tq], [("ps", ob)], [("oT", osb)])
                    P.add("sp", DMA(out_dst_of(qi), oTs[osb][:, 0:Tq]), reads=[("oT", osb)], dma_key=("oT", osb))

            def attention(l):
                hc = 0
                for h in range(NH):
                    hb = hc % 2
                    hc += 1
                    P.add("sp", DMA(KT[hb][:, 0:S], kT_d[h]), writes=[("KT", hb)], dma_key=("KT", hb))
                    P.add("sp", DMA(Vb[hb][:, 0:S // 128, :], vS_d[:, h * 128:(h + 1) * 128].rearrange("(b p) d -> p b d", p=128)),
                          writes=[("V", hb)], dma_key=("V", hb))
                    attn_head(KT[hb], Vb[hb],
                              [qT_d[h, :, qi * 512:(qi + 1) * 512] for qi in range(S // 512)],
                              lambda qi: [(kb * 128, 128, kb, (kb - 4 * qi) if kb >= 4 * qi else None)
                                          for kb in range(4 * qi + 3, -1, -1)],
                              lambda qi: 512,
                              (lambda h_: lambda qi: atT_d[h_, :, qi * 512:(qi + 1) * 512])(h),
                              ("KT", hb), ("V", hb))
                for h in range(NH):
                    hb = hc % 2
                    hc += 1
                    P.add("sp", DMA(KT[hb][:, 0:SK], kTs_d[l, h]), writes=[("KT", hb)], dma_key=("KT", hb))
                    P.add("sp", DMA(Vb[hb][:, 0:NKBS, :],
                                    vSs_d[l, 0:PAST, h * 128:(h + 1) * 128].rearrange("(b p) d -> p b d", p=128)),
                          writes=[("V", hb)], dma_key=("V", hb))
                    P.add("sp", DMA(Vb[hb][0:64, NKBS, :], vSs_d[l, PAST:PAST + 64, h * 128:(h + 1) * 128]),
                          writes=[("V", hb)], dma_key=("Vt", hb))
                    attn_head(KT[hb], Vb[hb],
                              [qT_d[h, :, S:S + 64]],
                              lambda qi: [(PAST, 64, NKBS, 0)] + [(kb * 128, 128, kb, None) for kb in range(NKBS - 1, -1, -1)],
                              lambda qi: 64,
                              (lambda h_: lambda qi: atT_d[h_, :, S:S + 64])(h),
                              ("KT", hb), ("V", hb))

            def rep_rstd(bank, T, scale, dst_i, tmp_i):
                P.add("dve", TS(rep[:, tmp_i, 0:T], ps[bank][:, 0:T], scale, EPS, ALU.mult, ALU.add), reads=[("ps", bank)], writes=[("rep", tmp_i)])
                P.add("act", ACTV(rep[:, tmp_i, 0:T], rep[:, tmp_i, 0:T], AF.Sqrt), reads=[("rep", tmp_i)], writes=[("rep", tmp_i)])
                P.add("dve", RCP(rep[:, dst_i, 0:T], rep[:, tmp_i, 0:T]), reads=[("rep", tmp_i)], writes=[("rep", dst_i)])

            def sumsq_ps(bank, src, n, T):
                for i in range(n):
                    tb = i % 2
                    P.add("act", ACTV(tmpf[:, tb, 0:T], src[:, i, 0:T], AF.Square), reads=[("y", i)], writes=[("tmpf", tb)])
                    P.add("pe", MMS([(ps[bank][:, 0:T], onesf[:], tmpf[:, tb, 0:T], i == 0, i == n - 1)]),
                          reads=[("tmpf", tb), "onesf"], writes=[("ps", bank)])

            def mixer_tail(l, kind, t0, T, r0):
                sl = subs(T)
                pc = l * PL
                load_x(x1_d[r0:r0 + T, :], T)
                P.add("sp", DMA(bufA[:, :, 0:T], atT_d.rearrange("h p t -> p h t")[:, :, r0:r0 + T]), writes=[("y", c) for c in range(8)], dma_key="bufA")
                sumsq_ps(0, bufA, NH, T)
                rep_rstd(0, T, 1.0 / 1024, 0, 1)
                for h in range(NH):
                    P.add("dve", STT(hT[:, h, 0:T], bufA[:, h, 0:T], prm[:, pc + 320 + h:pc + 321 + h], rep[:, 0, 0:T], ALU.mult, ALU.mult),
                          reads=[("y", h), ("rep", 0), "prm"], writes=[("hT", h)])
                for c0 in range(0, CC, 2):
                    for c in (c0, c0 + 1):
                        ab = c % 2
                        if kind == "p":
                            src = aT_d[c, :, t0:t0 + 30 + T]
                        else:
                            src = aTs_d[l, c, :, 0:94]
                        P.add("sp", DMA(abuf[ab][:, 0:30 + T], src), writes=[("abuf", ab)], dma_key=("abuf", ab))
                    for w in range(31):
                        for c in (c0, c0 + 1):
                            ab = c % 2
                            yv = bufA[:, c, 0:T]
                            wc = pc + 48 + c * 31
                            if w == 0:
                                P.add("dve", TS(yv, abuf[ab][:, 0:T], prm[:, wc:wc + 1], prm[:, pc + 296 + c:pc + 297 + c], ALU.mult, ALU.add),
                                      reads=[("abuf", ab), "prm"], writes=[("y", c)])
                            else:
                                P.add("dve", STT(yv, abuf[ab][:, w:w + T], prm[:, wc + w:wc + w + 1], yv, ALU.mult, ALU.add),
                                      reads=[("abuf", ab), ("y", c)], writes=[("y", c)])
                for c in range(CC):
                    P.add("pe", MMS([(ps[1][:, 0:T], onesf[:], bufA[:, c, 0:T], c == 0, c == CC - 1)]),
                          reads=[("y", c), "onesf"], writes=[("ps", 1)])
                for c in range(CC):
                    tb = c % 2
                    P.add("act", ACTV(tmpf[:, tb, 0:T], bufA[:, c, 0:T], AF.Square), reads=[("y", c)], writes=[("tmpf", tb)])
                    P.add("pe", MMS([(ps[2][:, 0:T], onesf[:], tmpf[:, tb, 0:T], c == 0, c == CC - 1)]),
                          reads=[("tmpf", tb), "onesf"], writes=[("ps", 2)])
                P.add("dve", TS(rep[:, 2, 0:T], ps[1][:, 0:T], 1.0 / 1024, None, ALU.mult), reads=[("ps", 1)], writes=[("rep", 2)])
                P.add("dve", TT(rep[:, 3, 0:T], rep[:, 2, 0:T], rep[:, 2, 0:T], ALU.mult), reads=[("rep", 2)], writes=[("rep", 3)])
                P.add("dve", STT(rep[:, 3, 0:T], ps[2][:, 0:T], 1.0 / 1024, rep[:, 3, 0:T], ALU.mult, ALU.subtract),
                      reads=[("ps", 2), ("rep", 3)], writes=[("rep", 3)])
                P.add("dve", TS(rep[:, 3, 0:T], rep[:, 3, 0:T], EPS, None, ALU.add), reads=[("rep", 3)], writes=[("rep", 3)])
                P.add("act", ACTV(rep[:, 3, 0:T], rep[:, 3, 0:T], AF.Sqrt), reads=[("rep", 3)], writes=[("rep", 3)])
                P.add("dve", RCP(rep[:, 4, 0:T], rep[:, 3, 0:T]), reads=[("rep", 3)], writes=[("rep", 4)])
                for c in range(CC):
                    yv = bufA[:, c, 0:T]
                    P.add("dve", TT(yv, yv, rep[:, 2, 0:T], ALU.subtract), reads=[("y", c), ("rep", 2)], writes=[("y", c)])
                    P.add("dve", TT(yv, yv, rep[:, 4, 0:T], ALU.mult), reads=[("y", c), ("rep", 4)], writes=[("y", c)])
                    P.add("act", ACTV(yv, yv, AF.Silu, scale=prm[:, pc + 304 + c:pc + 305 + c], bias=prm[:, pc + 312 + c:pc + 313 + c]),
                          reads=[("y", c), "prm"], writes=[("y", c)])
                for c in range(CC):
                    tb = c % 2
                    P.add("act", ACTV(tmpf[:, tb, 0:T], bufA[:, c, 0:T], AF.Square), reads=[("y", c)], writes=[("tmpf", tb)])
                    P.add("pe", MMS([(ps[3][:, 0:T], onesf[:], tmpf[:, tb, 0:T], c == 0, c == CC - 1)]),
                          reads=[("tmpf", tb), "onesf"], writes=[("ps", 3)])
                P.add("dve", TS(rep[:, 5, 0:T], ps[3][:, 0:T], 1.0 / 1024, EPS, ALU.mult, ALU.add), reads=[("ps", 3)], writes=[("rep", 5)])
                P.add("act", ACTV(rep[:, 5, 0:T], rep[:, 5, 0:T], AF.Sqrt), reads=[("rep", 5)], writes=[("rep", 5)])
                P.add("dve", RCP(rep[:, 5, 0:T], rep[:, 5, 0:T]), reads=[("rep", 5)], writes=[("rep", 5)])
                for c in range(CC):
                    P.add("dve", STT(hT[:, 8 + c, 0:T], bufA[:, c, 0:T], prm[:, pc + 328 + c:pc + 329 + c], rep[:, 5, 0:T], ALU.mult, ALU.mult),
                          reads=[("y", c), ("rep", 5), "prm"], writes=[("hT", 8 + c)])
                for cg in range(4):
                    base = 4 * (cg % 2)
                    bw = W.next(wout[l][:, cg * 512:(cg + 1) * 512], 16)
                    for (ts, pr) in sl:
                        P.add("pe", MMS([(ps[base + ts][:pr, :], hT[:, c, ts * 128:ts * 128 + pr], WB[:, bw, c, :], c == 0, c == DC - 1)
                                         for c in range(DC)]), reads=[("wb", bw)] + hreads, writes=[("ps", base + ts)])
                        xs_ = x_tm[:pr, ts, cg * 512:(cg + 1) * 512]
                        P.add("dve", TT(xs_, ps[base + ts][:pr, :], xs_, ALU.add), reads=[("ps", base + ts), ("x", ts)], writes=[("x", ts)])

            def final_norm(kind, t0, T):
                sl = subs(T)
                P.add("dve", MSET(stat[:, 0:4], 0.0), writes=["stat"])
                for (ts, pr) in sl:
                    P.add("act", ACTV(xh[:pr, ts, :], x_tm[:pr, ts, :], AF.Square, accum_out=stat[:pr, ts:ts + 1]),
                          reads=[("x", ts), "stat"], writes=[("xh", ts), "stat"])
                P.add("dve", TS(stat[:, 4:8], stat[:, 0:4], 1.0 / D, EPS, ALU.mult, ALU.add), reads=["stat"], writes=["stat"])
                P.add("act", ACTV(stat[:, 8:12], stat[:, 4:8], AF.Sqrt), reads=["stat"], writes=["stat"])
                P.add("dve", RCP(stat[:, 12:16], stat[:, 8:12]), reads=["stat"], writes=["stat"])
                for (ts, pr) in sl:
                    P.add("dve", STT(x_tm[:pr, ts, :], x_tm[:pr, ts, :], stat[:pr, 12 + ts:13 + ts], gfin[:pr, :], ALU.mult, ALU.mult),
                          reads=[("x", ts), "stat", "gfin"], writes=[("x", ts)])
                dst = yp[t0:t0 + T, :] if kind == "p" else ys
                store_x(dst, T, "xst")

            for l in range(L):
                pc = l * PL
                for (kind, t0, T, r0) in groups:
                    if l == 0:
                        load_x(xp[t0:t0 + T, :] if kind == "p" else xs, T)
                    else:
                        load_x(xc_d[r0:r0 + T, :], T)
                    norm_hT(T, pc + 0)
                    ffn(T, l, 0)
                    store_x(x1_d[r0:r0 + T, :], T, "xst")
                    if KSTOP <= 3:
                        raise _Stop()
                    norm_hT(T, pc + 16)
                    P.barrier()
                    if KSTOP <= 3.2:
                        raise _Stop()
                    win_stage(l, kind, t0, T, r0)
                    P.barrier()
                    if KSTOP <= 3.6 or (KSTOP <= 3.8 and kind == "p" and t0 + T == S):
                        raise _Stop()
                if KSTOP <= 4:
                    raise _Stop()
                attention(l)
                P.barrier()
                if KSTOP <= 5:
                    raise _Stop()
                for (kind, t0, T, r0) in groups:
                    mixer_tail(l, kind, t0, T, r0)
                    P.barrier()
                    norm_hT(T, pc + 32)
                    ffn(T, l, 1)
                    if l < L - 1:
                        store_x(xc_d[r0:r0 + T, :], T, "xst")
                    else:
                        final_norm(kind, t0, T)
                    P.barrier()

        class WStream:
            def __init__(self, P, seq):
                self.P = P
                self.dry = seq is None
                self.seq = [] if seq is None else seq
                self.n = 0
                self.issued = 0

            def next(self, blk, kc):
                if self.dry:
                    self.seq.append((blk, kc))
                    return 0
                n = self.n
                self.n += 1
                lim = min(len(self.seq), n + NWB - 1)
                while self.issued < lim:
                    m = self.issued
                    b_, kc_ = self.seq[m]
                    bi = m % NWB
                    self.P.add("pool", DMA(WB[:, bi, 0:kc_, :], b_.rearrange("(k p) n -> p k n", p=128)),
                               writes=[("wb", bi)], dma_key=("wb", bi))
                    self.issued += 1
                return n % NWB

        dry = WStream(DryProg(), None)
        stages(dry.P, dry)
        P = Prog()
        W = WStream(P, dry.seq)
        stages(P, W)
        P.emit(nc, st)
    return nc


def _pack_params(inp, L):
    prm = np.zeros((128, NPAR), np.float32)

    def fm(v, nchunk):
        return np.ascontiguousarray(v.reshape(nchunk, 128).T)

    for l in range(L):
        pc = l * PL
        prm[:, pc + 0:pc + 16] = fm(inp["norm_ffn1"][l], 16)
        prm[:, pc + 16:pc + 32] = fm(inp["norm_mix"][l], 16)
        prm[:, pc + 32:pc + 48] = fm(inp["norm_ffn2"][l], 16)
        dw = inp["dw_weight"][l]
        prm[:, pc + 48:pc + 296] = dw.reshape(31, 8, 128).transpose(2, 1, 0).reshape(128, 248)
        prm[:, pc + 296:pc + 304] = fm(inp["dw_bias"][l], 8)
        prm[:, pc + 304:pc + 312] = fm(inp["conv_ln_gain"][l], 8)
        prm[:, pc + 312:pc + 320] = fm(inp["conv_ln_bias"][l], 8)
        prm[:, pc + 320:pc + 328] = fm(inp["norm_attn_out"][l], 8)
        prm[:, pc + 328:pc + 336] = fm(inp["norm_conv_out"][l], 8)
    return prm


_NC_CACHE = {}


def kernel(**inp):
    inp = {k: np.asarray(v) for k, v in inp.items()}
    B, S, _ = inp["x_prompt"].shape
    BS = inp["x_sample"].shape[0]
    L = inp["w_in"].shape[0]
    PAST = inp["cache_k"].shape[2]
    import os
    ncores = int(os.environ.get('KCORES', '8'))
    key = (S, PAST, L)
    if key not in _NC_CACHE:
        _NC_CACHE[key] = build(S, PAST, L)
    nc = _NC_CACHE[key]
    prm = _pack_params(inp, L)
    gfin = np.ascontiguousarray(np.broadcast_to(inp["norm_final"][None, :], (128, D))).astype(np.float32)
    in_maps = []
    for c in range(ncores):
        pb = c % B
        sbi = c % BS
        in_maps.append({
            "xp": np.ascontiguousarray(inp["x_prompt"][pb]),
            "xs": np.ascontiguousarray(inp["x_sample"][sbi]),
            "ck": np.ascontiguousarray(inp["cache_k"][:, sbi].reshape(L, PAST, 1024)),
            "cv": np.ascontiguousarray(inp["cache_v"][:, sbi].reshape(L, PAST, 1024)),
            "sc": np.ascontiguousarray(inp["state_conv"][:, sbi]),
            "wg1": inp["ffn1_gate"], "wu1": inp["ffn1_up"], "wd1": inp["ffn1_down"],
            "wg2": inp["ffn2_gate"], "wu2": inp["ffn2_up"], "wd2": inp["ffn2_down"],
            "win": inp["w_in"], "wout": inp["w_out"],
            "prm": prm, "gfin": gfin,
        })
    res = run_bass_kernel_spmd(nc, in_maps, core_ids=list(range(ncores)))
    R = list(res.results)
    while len(R) < 8:
        R.append(R[0])
    y_p = np.stack([R[b]["yp"] for b in range(B)])
    y_s = np.stack([R[b]["ys"] for b in range(BS)])
    nkp = np.stack([R[b]["nkp"] for b in range(B)], axis=1).reshape(L, B, S, NH, 128)
    nvp = np.stack([R[b]["nvp"] for b in range(B)], axis=1).reshape(L, B, S, NH, 128)
    ncp = np.stack([R[b]["ncp"] for b in range(B)], axis=1)
    nks = np.stack([R[b]["nks"] for b in range(BS)], axis=1).reshape(L, BS, 64, NH, 128)
    nvs = np.stack([R[b]["nvs"] for b in range(BS)], axis=1).reshape(L, BS, 64, NH, 128)
    ncs = np.stack([R[b]["ncs"] for b in range(BS)], axis=1)
    return (y_p, y_s, nkp, nvp, ncp, nks, nvs, ncs)
```
